# Optimizing a Trainium2 kernel written in Bass

```python
import math
import jax
import jax.numpy as jnp
from jax import lax
import numpy as np

D_MODEL = 2048
BATCH = 8
SEQ = 2048
DEPTH = 2
DEC_BATCH = 128
DEC_SEQ = 1
PAST_LEN = 8192
PAGE_SIZE = 128

MIX_WIDTH = D_MODEL
N_MEM = 256
MEM_HEADS = 4
MEM_HD = 128
MEM_Q_DIM = MEM_HEADS * MEM_HD
WINDOW = 128
SWA_HD = 64
SWA_Q_HEADS = (MIX_WIDTH - MEM_Q_DIM) // SWA_HD
SWA_KV_HEADS = 4
SWA_GROUP = SWA_Q_HEADS // SWA_KV_HEADS
SWA_Q_DIM = SWA_Q_HEADS * SWA_HD
SWA_KV_DIM = SWA_KV_HEADS * SWA_HD
N_BUCKETS = 32
MAX_DISTANCE = WINDOW
HG_DK = 128
HG_DV = 128
HG_HEADS = (MIX_WIDTH - MEM_Q_DIM) // HG_DV
HG_K_DIM = HG_HEADS * HG_DK
HG_V_DIM = HG_HEADS * HG_DV
HG_CHUNK = 64
N_SWA_LAYERS = (DEPTH + 1) // 2
N_HG_LAYERS = DEPTH // 2
SWA_SPLITS = (SWA_Q_DIM, SWA_KV_DIM, SWA_KV_DIM, MEM_Q_DIM, MIX_WIDTH)
HG_SPLITS = (HG_K_DIM, HG_K_DIM, HG_V_DIM, MEM_Q_DIM, MIX_WIDTH)
DEEPNORM_ALPHA = (2.0 * DEPTH) ** 0.25
DEEPNORM_BETA = (8.0 * DEPTH) ** -0.25
LN_EPS = 1e-5
RMS_EPS = 1e-6

kernel_name = 'swa_sink_hgrn2_memxattn_hybrid_step'


def _split(u, sizes):
    offs = [int(o) for o in np.cumsum(sizes)[:-1]]
    return jnp.split(u, offs, axis=-1)


def _layer_norm(x, w, b):
    xf = x.astype(jnp.float32)
    mu = jnp.mean(xf, axis=-1, keepdims=True)
    var = jnp.mean(jnp.square(xf - mu), axis=-1, keepdims=True)
    y = (xf - mu) * lax.rsqrt(var + LN_EPS) * w.astype(jnp.float32) + b.astype(jnp.float32)
    return y.astype(x.dtype)


def _t5_bucket(d):
    n = jnp.maximum(d, 0)
    max_exact = N_BUCKETS // 2
    nf = jnp.maximum(n, 1).astype(jnp.float32)
    large = max_exact + (jnp.log(nf / max_exact) / math.log(MAX_DISTANCE / max_exact)
                         * (N_BUCKETS - max_exact)).astype(jnp.int32)
    large = jnp.minimum(large, N_BUCKETS - 1)
    return jnp.where(n < max_exact, n, large)


def _banded_attend(qb, kb, vb, d, valid, sinks, rel_bias):
    n, nq, nk = d.shape
    bias = rel_bias.astype(jnp.float32)[_t5_bucket(d)]
    bias = jnp.transpose(bias, (0, 3, 1, 2)).reshape(n, SWA_KV_HEADS, SWA_GROUP, nq, nk)
    mask = valid & (d >= 0) & (d < WINDOW)
    s = jnp.einsum('bnqhgd,bnkhd->bnhgqk', qb, kb).astype(jnp.float32) * (SWA_HD ** -0.5) + bias[None]
    s = jnp.where(mask[None, :, None, None], s, -jnp.inf)
    sink = sinks.astype(jnp.float32).reshape(1, 1, SWA_KV_HEADS, SWA_GROUP, 1, 1)
    m = jnp.maximum(jnp.max(s, axis=-1, keepdims=True), sink)
    p = jnp.exp(s - m)
    p = p / (jnp.sum(p, axis=-1, keepdims=True) + jnp.exp(sink - m))
    return jnp.einsum('bnhgqk,bnkhd->bnqhgd', p.astype(vb.dtype), vb)


def _swa_prompt(q, k, v, sinks, rel_bias):
    b, t = q.shape[:2]
    nb = t // WINDOW
    qb = q.reshape(b, nb, WINDOW, SWA_KV_HEADS, SWA_GROUP, SWA_HD)
    kc = k.reshape(b, nb, WINDOW, SWA_KV_HEADS, SWA_HD)
    vc = v.reshape(b, nb, WINDOW, SWA_KV_HEADS, SWA_HD)

    def prev(a):
        return jnp.pad(a, ((0, 0), (1, 0), (0, 0), (0, 0), (0, 0)))[:, :-1]

    kb = jnp.concatenate([prev(kc), kc], axis=2)
    vb = jnp.concatenate([prev(vc), vc], axis=2)
    blk = jnp.arange(nb)[:, None, None] * WINDOW
    qpos = blk + jnp.arange(WINDOW)[None, :, None]
    kpos = blk - WINDOW + jnp.arange(2 * WINDOW)[None, None, :]
    o = _banded_attend(qb, kb, vb, qpos - kpos, kpos >= 0, sinks, rel_bias)
    keep = min(WINDOW, t)
    return o.reshape(b, t, SWA_Q_DIM), k[:, t - keep:], v[:, t - keep:]


def _swa_sample(q, k, v, buf_k, buf_v, sinks, rel_bias):
    b, t = q.shape[:2]
    nbuf = buf_k.shape[1]
    kk = jnp.concatenate([buf_k.astype(k.dtype), k], axis=1)
    vv = jnp.concatenate([buf_v.astype(v.dtype), v], axis=1)
    qpos = PAST_LEN + jnp.arange(t)
    kpos = PAST_LEN - nbuf + jnp.arange(nbuf + t)
    d = (qpos[:, None] - kpos[None, :])[None]
    valid = (kpos >= 0)[None, None, :]
    qb = q.reshape(b, 1, t, SWA_KV_HEADS, SWA_GROUP, SWA_HD)
    o = _banded_attend(qb, kk[:, None], vv[:, None], d, valid, sinks, rel_bias)
    return o.reshape(b, t, SWA_Q_DIM), kk[:, -nbuf:], vv[:, -nbuf:]


def _hgrn2_chunked(q, k, v, logf, s0):
    b, t, h, dk = q.shape
    dv = v.shape[-1]
    c = HG_CHUNK if t % HG_CHUNK == 0 else t
    n = t // c

    def blocks(a):
        return jnp.moveaxis(a.reshape(b, n, c, h, a.shape[-1]), 1, 0)

    causal = jnp.tril(jnp.ones((c, c), dtype=bool))[None, :, :, None, None]

    def step(state, xs):
        qc, kc, vc, gc = xs
        g_cum = jnp.cumsum(gc, axis=1)
        diff = g_cum[:, :, None] - g_cum[:, None, :]
        decay = jnp.exp(jnp.where(causal, diff, -jnp.inf))
        scores = jnp.einsum('bthk,btshk,bshk->bhts', qc, decay, kc)
        o = (jnp.einsum('bhts,bshv->bthv', scores, vc)
             + jnp.einsum('bthk,bhkv->bthv', qc * jnp.exp(g_cum), state))
        g_last = g_cum[:, -1]
        k_dec = kc * jnp.exp(g_last[:, None] - g_cum)
        state = jnp.exp(g_last)[..., None] * state + jnp.einsum('bshk,bshv->bhkv', k_dec, vc)
        return state, o

    s_fin, o = lax.scan(step, s0, (blocks(q), blocks(k), blocks(v), blocks(logf)))
    return jnp.moveaxis(o, 0, 1).reshape(b, t, h, dv), s_fin


def _swa_proj(h, w_in):
    b, t = h.shape[:2]
    q, k, v, mq, g = _split(jnp.einsum('btd,de->bte', h, w_in), SWA_SPLITS)
    return (q.reshape(b, t, SWA_Q_HEADS, SWA_HD), k.reshape(b, t, SWA_KV_HEADS, SWA_HD),
            v.reshape(b, t, SWA_KV_HEADS, SWA_HD), mq, g)


def _hgrn_branch(h, w_in, lb, norm_w, s0):
    b, t = h.shape[:2]
    q, f, iv, mq, g = _split(jnp.einsum('btd,de->bte', h, w_in), HG_SPLITS)
    fg = lb[None, None, :] + (1.0 - lb[None, None, :]) * jax.nn.sigmoid(f.astype(jnp.float32))
    qh = jax.nn.silu(q.astype(jnp.float32)).reshape(b, t, HG_HEADS, HG_DK)
    kh = (1.0 - fg).reshape(b, t, HG_HEADS, HG_DK)
    gh = jnp.log(fg).reshape(b, t, HG_HEADS, HG_DK)
    vh = iv.astype(jnp.float32).reshape(b, t, HG_HEADS, HG_DV)
    o, s_new = _hgrn2_chunked(qh, kh, vh, gh, s0.astype(jnp.float32))
    o = o * lax.rsqrt(jnp.mean(jnp.square(o), axis=-1, keepdims=True) + RMS_EPS) \
        * norm_w.astype(jnp.float32).reshape(HG_HEADS, HG_DV)
    return o.reshape(b, t, HG_V_DIM).astype(h.dtype), s_new, mq, g


def _mem_attend(mq, mk, mv):
    s = jnp.einsum('bthd,bmhd->bhtm', mq, mk).astype(jnp.float32) * (MEM_HD ** -0.5)
    p = jax.nn.softmax(s, axis=-1)
    return jnp.einsum('bhtm,bmhd->bthd', p.astype(mv.dtype), mv)


def _finish(h, mix, mq, g, mk, mv, w_out_i, ln_w_i, ln_b_i):
    b, t = h.shape[:2]
    mem_o = _mem_attend(mq.reshape(b, t, MEM_HEADS, MEM_HD), mk.astype(mq.dtype), mv.astype(mq.dtype))
    branch = jnp.concatenate([mix.astype(h.dtype), mem_o.reshape(b, t, MEM_Q_DIM).astype(h.dtype)], axis=-1) \
        * jax.nn.silu(g)
    y = jnp.einsum('bte,ed->btd', branch, w_out_i)
    return _layer_norm(DEEPNORM_ALPHA * h + y, ln_w_i, ln_b_i)


def setup_inputs(seed: int = 0) -> dict:
    key = jax.random.key(seed)
    ks = jax.random.split(key, 20)

    def nrm(k, shape, scale=1.0):
        return jax.random.normal(k, shape, jnp.float32) * scale

    win_buf = min(WINDOW, PAST_LEN)
    d_in = D_MODEL ** -0.5
    return {
        'x_prompt': nrm(ks[0], (BATCH, SEQ, D_MODEL)),
        'x_sample': nrm(ks[1], (DEC_BATCH, DEC_SEQ, D_MODEL)),
        'cache_mem_k': nrm(ks[2], (DEPTH, DEC_BATCH, N_MEM, MEM_HEADS, MEM_HD)),
        'cache_mem_v': nrm(ks[3], (DEPTH, DEC_BATCH, N_MEM, MEM_HEADS, MEM_HD)),
        'cache_swa_k': nrm(ks[4], (N_SWA_LAYERS, DEC_BATCH, win_buf, SWA_KV_HEADS, SWA_HD)),
        'cache_swa_v': nrm(ks[5], (N_SWA_LAYERS, DEC_BATCH, win_buf, SWA_KV_HEADS, SWA_HD)),
        'state_hgrn': nrm(ks[6], (N_HG_LAYERS, DEC_BATCH, HG_HEADS, HG_DK, HG_DV), 0.5),
        'mem_prompt': nrm(ks[7], (BATCH, N_MEM, D_MODEL)),
        'rel_bias': nrm(ks[8], (N_BUCKETS, SWA_Q_HEADS), 0.5),
        'swa_w_in': nrm(ks[9], (N_SWA_LAYERS, D_MODEL, sum(SWA_SPLITS)), d_in),
        'swa_sinks': nrm(ks[10], (N_SWA_LAYERS, SWA_Q_HEADS), 0.5),
        'hg_w_in': nrm(ks[11], (N_HG_LAYERS, D_MODEL, sum(HG_SPLITS)), d_in),
        'hg_lb_logits': nrm(ks[12], (DEPTH, HG_K_DIM)),
        'hg_norm_w': 1.0 + nrm(ks[13], (N_HG_LAYERS, HG_V_DIM), 0.02),
        'w_mem_k': nrm(ks[14], (DEPTH, D_MODEL, MEM_Q_DIM), d_in),
        'w_mem_v': nrm(ks[15], (DEPTH, D_MODEL, MEM_Q_DIM), d_in),
        'w_out': nrm(ks[16], (DEPTH, MIX_WIDTH, D_MODEL), DEEPNORM_BETA * MIX_WIDTH ** -0.5),
        'ln_w': 1.0 + nrm(ks[17], (DEPTH, D_MODEL), 0.02),
        'ln_b': nrm(ks[18], (DEPTH, D_MODEL), 0.02),
    }


def reference(x_prompt, x_sample, cache_mem_k, cache_mem_v, cache_swa_k, cache_swa_v, state_hgrn, mem_prompt,
              rel_bias, swa_w_in, swa_sinks, hg_w_in, hg_lb_logits, hg_norm_w, w_mem_k, w_mem_v, w_out, ln_w, ln_b):
    lb_all = jnp.cumsum(jax.nn.softmax(hg_lb_logits.astype(jnp.float32), axis=0), axis=0)
    lb_all = lb_all - lb_all[0:1]
    bp = x_prompt.shape[0]
    hp, hs = x_prompt, x_sample
    mk_list, mv_list = [], []
    swa_kp, swa_vp, swa_ks, swa_vs = [], [], [], []
    hg_sp, hg_ss = [], []
    for i in range(DEPTH):
        mk_p = jnp.einsum('bmd,de->bme', mem_prompt, w_mem_k[i]).reshape(bp, N_MEM, MEM_HEADS, MEM_HD)
        mv_p = jnp.einsum('bmd,de->bme', mem_prompt, w_mem_v[i]).reshape(bp, N_MEM, MEM_HEADS, MEM_HD)
        mk_list.append(mk_p)
        mv_list.append(mv_p)
        j = i // 2
        if i % 2 == 0:
            q, k, v, mq, g = _swa_proj(hp, swa_w_in[j])
            mix, kw, vw = _swa_prompt(q, k, v, swa_sinks[j], rel_bias)
            hp = _finish(hp, mix, mq, g, mk_p, mv_p, w_out[i], ln_w[i], ln_b[i])
            swa_kp.append(kw)
            swa_vp.append(vw)
            q, k, v, mq, g = _swa_proj(hs, swa_w_in[j])
            mix, kw, vw = _swa_sample(q, k, v, cache_swa_k[j], cache_swa_v[j], swa_sinks[j], rel_bias)
            hs = _finish(hs, mix, mq, g, cache_mem_k[i], cache_mem_v[i], w_out[i], ln_w[i], ln_b[i])
            swa_ks.append(kw.astype(cache_swa_k.dtype))
            swa_vs.append(vw.astype(cache_swa_v.dtype))
        else:
            s0 = jnp.zeros((bp, HG_HEADS, HG_DK, HG_DV), jnp.float32)
            mix, s_new, mq, g = _hgrn_branch(hp, hg_w_in[j], lb_all[i], hg_norm_w[j], s0)
            hp = _finish(hp, mix, mq, g, mk_p, mv_p, w_out[i], ln_w[i], ln_b[i])
            hg_sp.append(s_new.astype(x_prompt.dtype))
            mix, s_new, mq, g = _hgrn_branch(hs, hg_w_in[j], lb_all[i], hg_norm_w[j], state_hgrn[j])
            hs = _finish(hs, mix, mq, g, cache_mem_k[i], cache_mem_v[i], w_out[i], ln_w[i], ln_b[i])
            hg_ss.append(s_new.astype(state_hgrn.dtype))
    mem_k_prompt = jnp.stack(mk_list)
    mem_v_prompt = jnp.stack(mv_list)
    swa_k_prompt = jnp.stack(swa_kp)
    swa_v_prompt = jnp.stack(swa_vp)
    hgrn_state_prompt = jnp.stack(hg_sp)
    swa_k_sample = jnp.stack(swa_ks)
    swa_v_sample = jnp.stack(swa_vs)
    hgrn_state_sample = jnp.stack(hg_ss)
    return (hp, hs, mem_k_prompt, mem_v_prompt, swa_k_prompt, swa_v_prompt, hgrn_state_prompt,
            swa_k_sample, swa_v_sample, hgrn_state_sample)
```

```python
import numpy as np
from contextlib import ExitStack
import concourse.bass as bass
import concourse.mybir as mybir
from concourse.bass_utils import run_bass_kernel_spmd

F32 = mybir.dt.float32
BF16 = mybir.dt.bfloat16
AF = mybir.ActivationFunctionType
ALU = mybir.AluOpType

NCORES = 8
D = 2048
SEQ = 2048
TT = 1024
NS = 16
XW = 1152
BW = 1040
ALPHA = float((2.0 * 2) ** 0.25)
LN_EPS = 1e-5
RMS_EPS = 1e-6
DEBUG_L0_ONLY = False
STOP = None
DEBUG_NCORES = 8
SETUP_STOP = 99
P1_STOP = 99
BIG = ('xT', 'xres', 'w0', 'w1', 'wo', 'cmkT', 'cmv', 'hst', 'cswaKT', 'cswaV', 'cswaKraw', 'cswaVraw')


class _Stop(Exception):
    pass


class Tok:
    __slots__ = ("name", "w", "r", "dsem", "ssem", "p")

    def __init__(self, name, p=False):
        self.name = name
        self.w = None
        self.r = {}
        self.dsem = None
        self.ssem = None
        self.p = p


class KB:
    ENG = ("pe", "act", "dve", "pool", "sp")

    def __init__(self, nc, es, n_dma_sems=95):
        self.nc = nc
        self.eng = {"pe": nc.tensor, "act": nc.scalar, "dve": nc.vector, "pool": nc.gpsimd, "sp": nc.sync}
        self.sems = []
        self.esem = {}
        for e in self.ENG:
            self.esem[e] = len(self.sems)
            self.sems.append(es.enter_context(nc.semaphore("e_" + e)))
        self.free_dma = []
        for i in range(n_dma_sems):
            self.free_dma.append(len(self.sems))
            self.sems.append(es.enter_context(nc.semaphore("d%d" % i)))
        self.cnt = {e: 0 for e in self.ENG}
        self.known = {e: {} for e in self.ENG}
        self.dcnt = {}
        self.nwait = 0
        self.scoped = []

    def _wait(self, e, deps):
        k = self.known[e]
        pe_own = self.esem["pe"]
        for (s, v) in deps:
            if e == "pe" and s == pe_own:
                continue
            if k.get(s, 0) >= v:
                continue
            self.eng[e].wait_ge(self.sems[s], v)
            self.nwait += 1
            k[s] = v

    @staticmethod
    def _deps(R, W):
        d = []
        for t in R:
            if t.w is not None:
                d.append(t.w)
        for t in W:
            if t.w is not None:
                d.append(t.w)
            d.extend(t.r.items())
        return d

    def op(self, e, fn, R=(), W=()):
        self._wait(e, self._deps(R, W))
        ins = fn(self.eng[e])
        self.cnt[e] += 1
        s = self.esem[e]
        v = self.cnt[e]
        ins.then_inc(self.sems[s], 1)
        for t in R:
            t.r[s] = v
        for t in W:
            t.w = (s, v)
            t.r = {}

    def _dsem(self, tok, store):
        if store:
            if tok.ssem is None:
                tok.ssem = self.free_dma.pop()
                self.dcnt.setdefault(tok.ssem, 0)
                if not tok.p:
                    self.scoped.append(tok)
            return tok.ssem
        if tok.dsem is None:
            tok.dsem = self.free_dma.pop()
            self.dcnt.setdefault(tok.dsem, 0)
            if not tok.p:
                self.scoped.append(tok)
        return tok.dsem

    def end_scope(self):
        self.barrier()
        for t in self.scoped:
            for a in ("dsem", "ssem"):
                v = getattr(t, a)
                if v is not None:
                    self.free_dma.append(v)
                    setattr(t, a, None)
        self.scoped = []

    def dma(self, e, out, in_, R=(), W=(), sem_tok=None, store=False):
        if sem_tok is None:
            sem_tok = R[0] if store else W[0]
        s = self._dsem(sem_tok, store)
        deps = [(a, b) for (a, b) in self._deps(R, W) if a != s]
        self._wait(e, deps)
        ins = self.eng[e].dma_start(out=out, in_=in_)
        self.dcnt[s] += 16
        v = self.dcnt[s]
        ins.then_inc(self.sems[s], 16)
        for t in R:
            t.r[s] = v
        for t in W:
            t.w = (s, v)
            t.r = {}

    def barrier(self, engines=None):
        engines = engines or self.ENG
        for e in engines:
            deps = [(self.esem[x], self.cnt[x]) for x in self.ENG if x != e and self.cnt[x] > 0]
            deps += [(s, v) for s, v in self.dcnt.items() if v > 0]
            self._wait(e, deps)


class Ring:
    def __init__(self, items):
        self.items = items
        self.i = 0

    def next(self):
        it = self.items[self.i % len(self.items)]
        self.i += 1
        return it


def t5_bucket_np(d):
    n = np.maximum(d, 0)
    max_exact = 16
    nf = np.maximum(n, 1).astype(np.float32)
    large = max_exact + (np.log(nf / np.float32(max_exact)) / np.float32(np.log(128 / max_exact))
                         * np.float32(32 - max_exact)).astype(np.int32)
    large = np.minimum(large, 31)
    return np.where(n < max_exact, n, large)


def build_program():
    nc = bass.Bass("TRN2", target_bir_lowering=False)

    def din(name, shape):
        if STOP == "setup" and name in BIG:
            shape = [1, 1]
        return nc.dram_tensor(name, list(shape), F32, kind="ExternalInput")

    def dout(name, shape):
        return nc.dram_tensor(name, list(shape), F32, kind="ExternalOutput")

    xT_d = din("xT", [128, 16, SEQ])
    xsT_d = din("xsT", [128, 16, NS])
    xres_d = din("xres", [SEQ, D])
    xsres_d = din("xsres", [NS, D])
    memT_d = din("memT", [128, 16, 256])
    wk_d = din("wk", [2, 128, 16, 512])
    wv_d = din("wv", [2, 128, 16, 512])
    w0_d = din("w0", [19, 128, 16, 256])
    w1_d = din("w1", [28, 128, 16, 256])
    wo_d = din("wo", [2, 128, 16, D])
    lnw_d = din("lnw", [2, D])
    lnb_d = din("lnb", [2, D])
    relb_d = din("relb", [32, 24])
    relb0_d = din("relb0", [128, 12])
    sinks_d = din("sinksc", [128, 12])
    lbl_d = din("lbl", [128, 2, 12])
    hgnw_d = din("hgnw", [128, 12])
    cswaKT_d = din("cswaKT", [128, NS, 4, 2, 128])
    cswaV_d = din("cswaV", [128, NS, 256])
    cswaKraw_d = din("cswaKraw", [NS, 128, 256])
    cswaVraw_d = din("cswaVraw", [NS, 128, 256])
    cmkT_d = din("cmkT", [2, 4, 128, NS, 256])
    cmv_d = din("cmv", [2, 4, 128, NS, 2, 128])
    hst_d = din("hst", [12, 128, NS, 128])
    c_ident_d = din("c_ident", [128, 128])
    c_ones_d = din("c_ones", [128, 128])
    c_blk_d = din("c_blk", [128, 128])
    c_oh_d = din("c_oh", [32, 384])
    c_vm_d = din("c_vm", [24, 384])
    c_cm_d = din("c_cm", [128, 2, 64])
    c_rm_d = din("c_rm", [128, 1024])

    y_d = dout("y", [SEQ, D])
    ys_d = dout("ys", [NS, D])
    mkT_o = dout("mkT_o", [2, 4, 128, 256])
    mv_o = dout("mv_o", [2, 256, 512])
    swakT_o = dout("swakT_o", [4, 64, 128])
    swav_o = dout("swav_o", [128, 256])
    hstp_o = dout("hstp_o", [12, 128, 128])
    swaks_sh_o = dout("swaks_sh_o", [NS, 127, 256])
    swaksT_o = dout("swaksT_o", [4, 64, NS])
    swavs_sh_o = dout("swavs_sh_o", [NS, 127, 256])
    swavs_new_o = dout("swavs_new_o", [NS, 256])
    hsts_o = dout("hsts_o", [12, 128, NS, 128])

    h1res_d = nc.dram_tensor("h1res", [SEQ + NS, D], F32, kind="Internal")
    gscr_d = nc.dram_tensor("gscr", [24, 384], F32, kind="Internal")
    vscr_d = nc.dram_tensor("vscr", [NS, 1536], F32, kind="Internal")

    es = ExitStack()
    kb = KB(nc, es)

    uid = [0]

    def sb(st, name, shape, dt):
        uid[0] += 1
        return st.enter_context(nc.sbuf_tensor("s%d_%s" % (uid[0], name), list(shape), dt))

    def ps(st, name, shape, dt):
        uid[0] += 1
        return st.enter_context(nc.psum_tensor("p%d_%s" % (uid[0], name), list(shape), dt))

    hT = sb(es, "hT", [128, 16, XW], BF16)
    brT = sb(es, "brT", [128, 16, BW], BF16)
    mkT = sb(es, "mkT", [128, 2, 4, 256], BF16)
    mvb = sb(es, "mvb", [128, 2, 2, 512], BF16)
    ident = sb(es, "ident", [128, 128], BF16)
    ones_b = sb(es, "ones_b", [128, 128], BF16)
    ones_f = sb(es, "ones_f", [128, 128], F32)
    blk_f = sb(es, "blk_f", [128, 128], F32)
    cmask = sb(es, "cmask", [128, 2, 64], F32)
    rmask = sb(es, "rmask", [128, 1024], F32)
    expsink = sb(es, "expsink", [128, 12], F32)
    expb0 = sb(es, "expb0", [128, 12], F32)
    expbcol = sb(es, "expbcol", [128, 24], F32)
    lbT = sb(es, "lbT", [128, 12], F32)
    omlbT = sb(es, "omlbT", [128, 12], F32)
    nwT = sb(es, "nwT", [128, 12], F32)
    epsln = sb(es, "epsln", [128, 1], F32)
    epsrms = sb(es, "epsrms", [128, 1], F32)
    qTs = sb(es, "qTs", [128, 16, NS], BF16)
    sgs = sb(es, "sgs", [128, 16, NS], F32)
    kTs_f = sb(es, "kTs_f", [128, 4, NS], F32)
    vTs_f = sb(es, "vTs_f", [128, 4, NS], F32)
    Sst = sb(es, "Sst", [128, 12, 128], F32)
    Sbf = sb(es, "Sbf", [128, 12, 128], BF16)

    pbank = [ps(es, "pb%d" % i, [128, 512], F32) for i in range(6)]
    pbank_t = [Tok("pb%d" % i, True) for i in range(6)]
    ptr_full = [ps(es, "ptrf%d" % i, [128, 1024], BF16) for i in range(2)]
    ptr = [ptr_full[0][:, 0:512], ptr_full[1][:, 0:512]]
    ptr_t = [Tok("ptr%d" % i, True) for i in range(2)]

    hTt = [Tok("hT%d" % i, True) for i in range(8)]
    hTx = Tok("hTx", True)
    brt = [[Tok("br%d_%d" % (j, g), True) for g in range(3)] for j in range(16)]
    t_const = Tok("const", True)
    t_mk = Tok("mk", True)
    t_small = Tok("small", True)
    t_qTs = Tok("qTs", True)
    t_sgs = Tok("sgs", True)
    t_kTs = Tok("kTs", True)
    t_vTs = Tok("vTs", True)
    t_S = [Tok("S%d" % h, True) for h in range(12)]
    t_gscr = Tok("gscr", True)
    t_h1res = [Tok("h1res%d" % i, True) for i in range(17)]
    t_out = Tok("out", True)

    def hT_toks(c0, c1):
        ts = []
        for t in range(8):
            if c0 < (t + 1) * 128 and c1 > t * 128:
                ts.append(hTt[t])
        if c1 > 1024:
            ts.append(hTx)
        return ts

    with ExitStack() as st:
        cst = sb(st, "cst", [128, 128], F32)
        relb = sb(st, "relb", [32, 24], F32)
        oh = sb(st, "oh", [32, 384], F32)
        vm = sb(st, "vm", [24, 384], F32)
        G = sb(st, "G", [24, 384], F32)
        lbl = sb(st, "lbl", [128, 2, 12], F32)
        memT = sb(st, "memT", [128, 16, 256], BF16)
        wkb = sb(st, "wkb", [128, 16, 512], BF16)
        wvb = sb(st, "wvb", [128, 16, 512], BF16)
        mko = sb(st, "mko", [128, 4, 256], F32)
        mvo = sb(st, "mvo", [128, 2, 512], F32)
        t_relb, t_oh, t_vm, t_G, t_lbl, t_memT = Tok("relb"), Tok("oh"), Tok("vm"), Tok("G"), Tok("lbl"), Tok("memT")
        t_wk, t_wv, t_mko, t_mvo = Tok("wk"), Tok("wv"), Tok("mko"), Tok("mvo")
        t_c1, t_c2, t_c3, t_c4, t_c5 = Tok("c1", True), Tok("c2", True), Tok("c3", True), Tok("c4", True), Tok("c5", True)

        kb.dma("pool", ident[:], c_ident_d.ap()[:, :], W=[t_c1])
        kb.dma("pool", ones_b[:], c_ones_d.ap()[:, :], W=[t_c2])
        kb.dma("sp", ones_f[:], c_ones_d.ap()[:, :], W=[t_c3])
        kb.dma("sp", blk_f[:], c_blk_d.ap()[:, :], W=[t_c4])
        kb.dma("sp", cmask[:], c_cm_d.ap()[:, :, :], W=[t_c5])
        kb.dma("sp", rmask[:], c_rm_d.ap()[:, :], W=[t_c5], sem_tok=t_c5)
        kb.dma("sp", relb[:], relb_d.ap()[:, :], W=[t_relb])
        kb.dma("sp", oh[:], c_oh_d.ap()[:, :], W=[t_oh])
        kb.dma("sp", vm[:], c_vm_d.ap()[:, :], W=[t_vm])
        kb.dma("sp", expsink[:], sinks_d.ap()[:, :], W=[t_small])
        kb.dma("sp", expb0[:], relb0_d.ap()[:, :], W=[t_small], sem_tok=t_small)
        kb.dma("sp", lbl[:], lbl_d.ap()[:, :, :], W=[t_lbl])
        kb.dma("sp", nwT[:], hgnw_d.ap()[:, :], W=[t_small], sem_tok=t_small)
        kb.dma("pool", memT[:], memT_d.ap()[:, :, :], W=[t_memT])

        def _ck(n):
            if SETUP_STOP == n:
                raise _Stop()
        try:
          _ck(1)
          kb.op("dve", lambda e: e.memset(epsln[:], LN_EPS), W=[t_const])
          kb.op("dve", lambda e: e.memset(epsrms[:], RMS_EPS), W=[t_const])
          kb.op("act", lambda e: e.activation(out=expsink[:], in_=expsink[:], func=AF.Exp), R=[t_small], W=[t_small])
          kb.op("act", lambda e: e.activation(out=expb0[:], in_=expb0[:], func=AF.Exp), R=[t_small], W=[t_small])
          kb.op("dve", lambda e: e.tensor_tensor(out=lbT[:], in0=lbl[:, 1, :], in1=lbl[:, 0, :], op=ALU.subtract),
                R=[t_lbl], W=[t_small])
          kb.op("act", lambda e: e.activation(out=lbT[:], in_=lbT[:], func=AF.Sigmoid), R=[t_small], W=[t_small])
          kb.op("dve", lambda e: e.tensor_scalar(out=omlbT[:], in0=lbT[:], scalar1=-1.0, scalar2=1.0,
                                                 op0=ALU.mult, op1=ALU.add), R=[t_small], W=[t_small])
          _ck(2)
          b0 = pbank[0]
          kb.op("pe", lambda e: e.matmul(b0[0:24, 0:384], lhsT=relb[0:32, 0:24], rhs=oh[0:32, 0:384],
                                         start=True, stop=True), R=[t_relb, t_oh], W=[pbank_t[0]])
          kb.op("act", lambda e: e.activation(out=G[:], in_=b0[0:24, 0:384], func=AF.Exp), R=[pbank_t[0]], W=[t_G])
          kb.op("dve", lambda e: e.tensor_tensor(out=G[:], in0=G[:], in1=vm[:], op=ALU.mult), R=[t_G, t_vm], W=[t_G])
          kb.dma("sp", gscr_d.ap()[:, :], G[:], R=[t_G], W=[t_gscr], store=True)

          _ck(3)
          for L in range(2):
              kb.dma("pool", wkb[:], wk_d.ap()[L], W=[t_wk])
              kb.dma("pool", wvb[:], wv_d.ap()[L], W=[t_wv])
              _ck(4)
              for i in range(4):
                  bk = 1 + (i % 2)
                  for kc in range(16):
                      kb.op("pe", lambda e: e.matmul(pbank[bk][:, 0:256], lhsT=wkb[:, kc, i * 128:(i + 1) * 128],
                                                     rhs=memT[:, kc, :], start=(kc == 0), stop=(kc == 15)),
                            R=[t_wk, t_memT, t_c1], W=[pbank_t[bk]])
                  kb.op("act", lambda e: e.activation(out=mkT[:, L, i, :], in_=pbank[bk][:, 0:256], func=AF.Copy),
                        R=[pbank_t[bk]], W=[t_mk])
                  kb.op("dve", lambda e: e.tensor_copy(out=mko[:, i, :], in_=pbank[bk][:, 0:256]),
                        W=[pbank_t[bk], t_mko])
              _ck(5)
              kb.dma("sp", mkT_o.ap()[L].rearrange("i p m -> p i m"), mko[:], R=[t_mko], store=True)
              _ck(6)
              for mc in range(2):
                  bk = 3 + mc
                  for kc in range(16):
                      kb.op("pe", lambda e: e.matmul(pbank[bk][:, :], lhsT=memT[:, kc, mc * 128:(mc + 1) * 128],
                                                     rhs=wvb[:, kc, :], start=(kc == 0), stop=(kc == 15)),
                            R=[t_wv, t_memT], W=[pbank_t[bk]])
                  kb.op("act", lambda e: e.activation(out=mvb[:, L, mc, :], in_=pbank[bk][:, :], func=AF.Copy),
                        R=[pbank_t[bk]], W=[t_mk])
                  kb.op("dve", lambda e: e.tensor_copy(out=mvo[:, mc, :], in_=pbank[bk][:, :]),
                        W=[pbank_t[bk], t_mvo])
              kb.dma("sp", mv_o.ap()[L].rearrange("(mc p) n -> p mc n", p=128), mvo[:], R=[t_mvo],
                     store=True)
        except _Stop:
            pass
        kb.end_scope()

    proj_ring = Ring([0, 1])
    sc_ring = Ring([2, 3])
    PO, PSM = 4, 5

    def load_x(hf):
        kb.dma("pool", hT[:, :, 0:1024], xT_d.ap()[:, :, hf * 1024:(hf + 1) * 1024], W=hTt, sem_tok=hTt[0])
        if hf == 0:
            kb.dma("pool", hT[:, :, 1024:1040], xsT_d.ap()[:, :, :], W=[hTx])
        else:
            kb.dma("pool", hT[:, :, 1024:1152], xT_d.ap()[:, :, 896:1024], W=[hTx])

    def proj_group(wb, t_wb, wcols, c0, c1):
        bk = proj_ring.next()
        toks = hT_toks(c0, c1)
        for kc in range(16):
            kb.op("pe", lambda e: e.matmul(pbank[bk][:, 0:c1 - c0], lhsT=wb[:, kc, wcols[0]:wcols[1]],
                                           rhs=hT[:, kc, c0:c1], start=(kc == 0), stop=(kc == 15)),
                  R=[t_wb] + toks, W=[pbank_t[bk]])
        return bk

    def phase2(L, hf):
        with ExitStack() as st:
            wo = sb(st, "wo", [128, 16, D], BF16)
            lnwb = sb(st, "lnwb", [128, 2, D], F32)
            xr = [sb(st, "xr%d" % i, [128, D], F32) for i in range(2)]
            yb = sb(st, "yb", [128, D], BF16)
            stats = sb(st, "stats", [128, 4, 6], F32)
            mv2 = sb(st, "mv2", [128, 2], F32)
            sd = sb(st, "sd", [128, 1], F32)
            t_wo = [Tok("wo%d" % i) for i in range(4)]
            t_ln = Tok("ln")
            t_xr = [Tok("xr0"), Tok("xr1")]
            t_yb, t_stats, t_mv2, t_sd = Tok("yb"), Tok("stats"), Tok("mv2"), Tok("sd")
            for dg in range(4):
                kb.dma("pool", wo[:, :, dg * 512:(dg + 1) * 512], wo_d.ap()[L][:, :, dg * 512:(dg + 1) * 512],
                       W=[t_wo[dg]])
            kb.dma("sp", lnwb[:, 0, :], bass.AP(lnw_d, L * D, [[0, 128], [1, D]]), W=[t_ln])
            kb.dma("sp", lnwb[:, 1, :], bass.AP(lnb_d, L * D, [[0, 128], [1, D]]), W=[t_ln])
            ring6 = Ring([0, 1, 2, 3, 4, 5])
            tr_ring = Ring([0, 1])
            tiles = list(range(8)) + ([8] if hf == 0 else [])

            def load_res(t):
                M = 128 if t < 8 else NS
                b = xr[t % 2]
                if L == 0:
                    src = xres_d.ap()[hf * 1024 + t * 128: hf * 1024 + t * 128 + 128, :] if t < 8 else xsres_d.ap()[:, :]
                    kb.dma("sp", b[0:M, :], src, W=[t_xr[t % 2]])
                else:
                    r0 = hf * 1024 + t * 128 if t < 8 else SEQ
                    ti = hf * 8 + t if t < 8 else 16
                    kb.dma("sp", b[0:M, :], h1res_d.ap()[r0:r0 + M, :], R=[t_h1res[ti]], W=[t_xr[t % 2]])

            load_res(tiles[0])
            for idx, t in enumerate(tiles):
                M = 128 if t < 8 else NS
                c0 = t * 128
                g = t // 4
                if idx + 1 < len(tiles):
                    load_res(tiles[idx + 1])
                y = xr[t % 2]
                ty = t_xr[t % 2]
                for dg in range(4):
                    bk = ring6.next()
                    for ec in range(16):
                        kb.op("pe", lambda e: e.matmul(pbank[bk][0:M, :], lhsT=brT[:, ec, c0:c0 + M],
                                                       rhs=wo[:, ec, dg * 512:(dg + 1) * 512],
                                                       start=(ec == 0), stop=(ec == 15)),
                              R=[brt[ec][g], t_wo[dg]], W=[pbank_t[bk]])
                    kb.op("dve", lambda e: e.scalar_tensor_tensor(out=y[0:M, dg * 512:(dg + 1) * 512],
                                                                  in0=y[0:M, dg * 512:(dg + 1) * 512], scalar=ALPHA,
                                                                  in1=pbank[bk][0:M, :], op0=ALU.mult, op1=ALU.add),
                          R=[pbank_t[bk]], W=[ty])
                    kb.op("dve", lambda e: e.bn_stats(out=stats[0:M, dg, :], in_=y[0:M, dg * 512:(dg + 1) * 512]),
                          R=[ty], W=[t_stats])
                kb.op("dve", lambda e: e.bn_aggr(out=mv2[0:M, :], in_=stats[0:M, :, :].rearrange("p a b -> p (a b)")),
                      R=[t_stats], W=[t_mv2])
                kb.op("act", lambda e: e.activation(out=sd[0:M, :], in_=mv2[0:M, 1:2], func=AF.Sqrt,
                                                    bias=epsln[0:M, :], scale=1.0), R=[t_mv2, t_const], W=[t_sd])
                kb.op("dve", lambda e: e.reciprocal(out=sd[0:M, :], in_=sd[0:M, :]), R=[t_sd], W=[t_sd])
                kb.op("dve", lambda e: e.tensor_scalar(out=y[0:M, :], in0=y[0:M, :], scalar1=mv2[0:M, 0:1],
                                                       scalar2=sd[0:M, 0:1], op0=ALU.subtract, op1=ALU.mult),
                      R=[t_mv2, t_sd], W=[ty])
                kb.op("pool", lambda e: e.tensor_tensor(out=y[0:M, :], in0=y[0:M, :], in1=lnwb[0:M, 0, :], op=ALU.mult),
                      R=[t_ln], W=[ty])
                kb.op("pool", lambda e: e.tensor_tensor(out=y[0:M, :], in0=y[0:M, :], in1=lnwb[0:M, 1, :], op=ALU.add),
                      R=[t_ln], W=[ty])
                if L == 1 or DEBUG_L0_ONLY:
                    dst = y_d.ap()[hf * 1024 + c0: hf * 1024 + c0 + 128, :] if t < 8 else ys_d.ap()[:, :]
                    kb.dma("sp", dst, y[0:M, :], R=[ty], store=True)
                if L == 0:
                    r0 = hf * 1024 + t * 128 if t < 8 else SEQ
                    ti = hf * 8 + t if t < 8 else 16
                    kb.dma("sp", h1res_d.ap()[r0:r0 + M, :], y[0:M, :], R=[ty], W=[t_h1res[ti]], store=True)
                    kb.op("act", lambda e: e.activation(out=yb[0:M, :], in_=y[0:M, :], func=AF.Copy), R=[ty], W=[t_yb])
                    htok = hTt[t] if t < 8 else hTx
                    for q4 in range(4):
                        pi = tr_ring.next()
                        for c in range(4):
                            dc = q4 * 4 + c
                            kb.op("pe", lambda e: e.transpose(out=ptr[pi][:, c * 128:c * 128 + M],
                                                              in_=yb[0:M, dc * 128:(dc + 1) * 128],
                                                              identity=ident[0:M, 0:M]),
                                  R=[t_yb, t_c1], W=[ptr_t[pi]])
                        kb.op("act", lambda e: e.activation(
                            out=hT[:, q4 * 4:(q4 + 1) * 4, c0:c0 + M],
                            in_=ptr[pi][:, :].rearrange("p (c m) -> p c m", c=4)[:, :, 0:M], func=AF.Copy),
                            R=[ptr_t[pi]], W=[htok])
            kb.end_scope()

    def l0_phase1(hf):
        with ExitStack() as st:
            expbT = sb(st, "expbT", [128, 24, 2, 128], F32)
            wbs = [sb(st, "wb%d" % i, [128, 16, 256], BF16) for i in range(3)]
            kT2 = sb(st, "kT2", [128, 4, 2, XW], BF16)
            vtm = sb(st, "vtm", [128, 9, 256], BF16)
            qTb = [sb(st, "qT%d" % i, [128, BW], BF16) for i in range(2)]
            sgb = [sb(st, "sg%d" % i, [128, BW], F32) for i in range(2)]
            Eb = [sb(st, "E%d" % i, [128, 512], F32) for i in range(2)]
            PTb = [sb(st, "PT%d" % i, [128, 512], BF16) for i in range(2)]
            rsb = sb(st, "rsb", [128, 512], F32)
            tmpb = sb(st, "tmpb", [128, 512], F32)
            kout = sb(st, "kout", [128, 4, 128], F32)
            vout = sb(st, "vout", [128, 256], F32)
            t_expbT = Tok("expbT")
            t_wb = [Tok("wb%d" % i) for i in range(3)]
            t_kT2 = [[Tok("kT2_%d_%d" % (h, g)) for g in range(3)] for h in range(4)]
            t_v = [Tok("v%d" % i) for i in range(9)]
            t_qT = [[Tok("qT%d_%d" % (i, g)) for g in range(3)] for i in range(2)]
            t_sg = [[Tok("sg%d_%d" % (i, g)) for g in range(3)] for i in range(2)]
            t_E = [Tok("E0"), Tok("E1")]
            t_PT = [Tok("PT0"), Tok("PT1")]
            t_rs, t_tmp, t_kout, t_vout = Tok("rs"), Tok("tmp"), Tok("kout"), Tok("vout")
            E_ring, PT_ring = Ring([0, 1]), Ring([0, 1])

            t_kz = Tok("kz")
            kb.op("dve", lambda e: e.memset(kT2[:, :, :, :].rearrange("p a b c -> p (a b c)"), 0.0), W=[t_kz])
            for k in range(128):
                kb.dma("sp", expbT[k:k + 1, :, :, :],
                       bass.AP(gscr_d, 127 - k, [[0, 1], [384, 24], [128, 2], [1, 128]]),
                       R=[t_gscr], W=[t_expbT])
            load_x(hf)
            nslab = 19
            wring = Ring([0, 1, 2])
            slab_buf = {}

            nxt_slab = [0]

            def ensure_slab(upto):
                while nxt_slab[0] <= min(upto, nslab - 1):
                    s = nxt_slab[0]
                    i = wring.next()
                    kb.dma("pool", wbs[i][:], w0_d.ap()[s], W=[t_wb[i]])
                    slab_buf[s] = i
                    nxt_slab[0] += 1

            ensure_slab(1)
            ext = NS if hf == 0 else 128
            kv_groups = [(0, 512), (512, 1024), (1024, 1024 + ext)]
            qg_groups = [(0, 512), (512, 1024)] + ([(1024, 1040)] if hf == 0 else [])

            if hf == 0:
                kb.op("dve", lambda e: e.tensor_copy(out=expbcol[:], in_=expbT[:, :, 1, 0]), R=[t_expbT], W=[t_small])

            for s in range(2):
                ensure_slab(s + 2)
                wi = slab_buf[s]
                for u in range(2):
                    hk = 2 * s + u
                    for g, (c0, c1) in enumerate(kv_groups):
                        bk = proj_group(wbs[wi], t_wb[wi], (u * 128, (u + 1) * 128), c0, c1)
                        kb.op("act", lambda e: e.activation(out=kT2[0:64, hk, 0, c0:c1], in_=pbank[bk][0:64, 0:c1 - c0],
                                                            func=AF.Copy), R=[pbank_t[bk], t_kz], W=[t_kT2[hk][g]])
                        kb.op("act", lambda e: e.activation(out=kT2[64:128, hk, 1, c0:c1], in_=pbank[bk][64:128, 0:c1 - c0],
                                                            func=AF.Copy), R=[pbank_t[bk], t_kz], W=[t_kT2[hk][g]])
                        if hf == 1 and g == 1:
                            kb.op("dve", lambda e: e.tensor_copy(out=kout[:, hk, :], in_=pbank[bk][:, 384:512]),
                                  W=[pbank_t[bk], t_kout])
                        if hf == 0 and g == 2:
                            kb.op("dve", lambda e: e.tensor_copy(out=kTs_f[:, hk, :], in_=pbank[bk][:, 0:NS]),
                                  W=[pbank_t[bk], t_kTs])
            if hf == 1:
                kb.dma("sp", swakT_o.ap().rearrange("h p q -> p h q"), kout[0:64, :, :], R=[t_kout],
                       store=True)
            else:
                kb.dma("sp", swaksT_o.ap().rearrange("h p q -> p h q"), kTs_f[0:64, :, :], R=[t_kTs],
                       store=True)
            if P1_STOP == 1:
                kb.end_scope()
                return
            wi = slab_buf[2]
            for t in range(9):
                M = 128 if (t < 8 or hf == 1) else NS
                bk = proj_ring.next()
                htok = [hTt[t]] if t < 8 else [hTx]
                for kc in range(16):
                    kb.op("pe", lambda e: e.matmul(pbank[bk][0:M, 0:256], lhsT=hT[:, kc, t * 128:t * 128 + M],
                                                   rhs=wbs[wi][:, kc, 0:256], start=(kc == 0), stop=(kc == 15)),
                          R=[t_wb[wi]] + htok, W=[pbank_t[bk]])
                kb.op("act", lambda e: e.activation(out=vtm[0:M, t, :], in_=pbank[bk][0:M, 0:256], func=AF.Copy),
                      R=[pbank_t[bk]], W=[t_v[t]])
                if hf == 1 and t == 7:
                    kb.op("dve", lambda e: e.tensor_copy(out=vout[:, :], in_=pbank[bk][:, 0:256]),
                          W=[pbank_t[bk], t_vout])
                    kb.dma("sp", swav_o.ap()[:, :], vout[:, :], R=[t_vout], store=True)
                if hf == 0 and t == 8:
                    kb.op("dve", lambda e: e.tensor_copy(out=vout[0:NS, :], in_=pbank[bk][0:NS, 0:256]),
                          W=[pbank_t[bk], t_vout])
                    kb.dma("sp", swavs_new_o.ap()[:, :], vout[0:NS, :], R=[t_vout], store=True)
            if hf == 0:
                bk = proj_ring.next()
                for hk in range(4):
                    for x in range(2):
                        for kc in range(16):
                            kb.op("pe", lambda e: e.matmul(pbank[bk][x * 64:(x + 1) * 64, hk * NS:(hk + 1) * NS],
                                                           lhsT=wbs[wi][:, kc, hk * 64:(hk + 1) * 64],
                                                           rhs=hT[:, kc, 1024:1040], start=(kc == 0), stop=(kc == 15)),
                                  R=[t_wb[wi], hTx], W=[pbank_t[bk]])
                kb.op("dve", lambda e: e.tensor_copy(out=vTs_f[:, :, :].rearrange("p a b -> p (a b)"),
                                                     in_=pbank[bk][:, 0:4 * NS]), R=[pbank_t[bk]], W=[t_vTs])

            if P1_STOP == 2:
                kb.end_scope()
                return
            def inproj_j(j):
                s = 3 + j
                ensure_slab(s + 2)
                wi = slab_buf[s]
                b = j % 2
                for g, (c0, c1) in enumerate(qg_groups):
                    bk = proj_group(wbs[wi], t_wb[wi], (0, 128), c0, c1)
                    kb.op("act", lambda e: e.activation(out=qTb[b][:, c0:c1], in_=pbank[bk][:, 0:c1 - c0],
                                                        func=AF.Copy), R=[pbank_t[bk]], W=[t_qT[b][g]])
                    bk2 = proj_group(wbs[wi], t_wb[wi], (128, 256), c0, c1)
                    kb.op("act", lambda e: e.activation(out=sgb[b][:, c0:c1], in_=pbank[bk2][:, 0:c1 - c0],
                                                        func=AF.Silu), R=[pbank_t[bk2]], W=[t_sg[b][g]])
                if hf == 0:
                    kb.op("dve", lambda e: e.tensor_copy(out=qTs[:, j, :], in_=qTb[b][:, 1024:1040]),
                          R=[t_qT[b][2]], W=[t_qTs])
                    kb.op("dve", lambda e: e.tensor_copy(out=sgs[:, j, :], in_=sgb[b][:, 1024:1040]),
                          R=[t_sg[b][2]], W=[t_sgs])

            def finalize(j, b, gi, add_sink):
                c0 = gi * 512
                if add_sink:
                    kb.op("dve", lambda e: e.tensor_scalar(out=rsb[:], in0=pbank[PSM][:, :], scalar1=expsink[:, j:j + 1],
                                                           scalar2=None, op0=ALU.add),
                          R=[pbank_t[PSM], t_small], W=[t_rs])
                    kb.op("dve", lambda e: e.reciprocal(out=rsb[:], in_=rsb[:]), R=[t_rs], W=[t_rs])
                else:
                    kb.op("dve", lambda e: e.reciprocal(out=rsb[:], in_=pbank[PSM][:, :]), R=[pbank_t[PSM]], W=[t_rs])
                kb.op("dve", lambda e: e.tensor_tensor(out=tmpb[:], in0=pbank[PO][:, :], in1=rsb[:], op=ALU.mult),
                      R=[pbank_t[PO], t_rs], W=[t_tmp])
                kb.op("dve", lambda e: e.tensor_tensor(out=brT[:, j, c0:c0 + 512], in0=tmpb[:],
                                                       in1=sgb[b][:, c0:c0 + 512], op=ALU.mult),
                      R=[t_tmp, t_sg[b][gi]], W=[brt[j][gi]])

            def swa_pair(j):
                hk = j // 3
                b = j % 2
                pend = []

                def scores(n):
                    gb = hf * 8 + n
                    cs = [0] if gb == 0 else [0, 1]
                    bk = sc_ring.next()
                    for x in range(2):
                        for c in cs:
                            if c == 0:
                                kc0, ktok = n * 128, t_kT2[hk][n // 4]
                            elif n > 0:
                                kc0, ktok = (n - 1) * 128, t_kT2[hk][(n - 1) // 4]
                            else:
                                kc0, ktok = 1024, t_kT2[hk][2]
                            kb.op("pe", lambda e: e.matmul(
                                pbank[bk][:, (x * 2 + c) * 128:(x * 2 + c + 1) * 128],
                                lhsT=kT2[:, hk, x, kc0:kc0 + 128],
                                rhs=qTb[b][:, n * 128:(n + 1) * 128], start=True, stop=True),
                                R=[ktok, t_qT[b][n // 4]], W=[pbank_t[bk]])
                    ei, pi = E_ring.next(), PT_ring.next()
                    if len(cs) == 2:
                        src = pbank[bk][:, :]
                        eo, po_, tb = Eb[ei][:, :], PTb[pi][:, :], expbT[:, 2 * j:2 * j + 2, :, :].rearrange("p a c q -> p (a c q)")
                    else:
                        v4 = lambda ap: ap.rearrange("p (a c q) -> p a c q", a=2, c=2)[:, :, 0, :]
                        src, eo, po_ = v4(pbank[bk][:, :]), v4(Eb[ei][:, :]), v4(PTb[pi][:, :])
                        tb = expbT[:, 2 * j:2 * j + 2, 0, :]
                    kb.op("act", lambda e: e.activation(out=eo, in_=src, func=AF.Exp, scale=0.125),
                          R=[pbank_t[bk]], W=[t_E[ei]])
                    kb.op("dve", lambda e: e.tensor_tensor(out=po_, in0=eo, in1=tb, op=ALU.mult),
                          R=[t_E[ei], t_expbT], W=[t_PT[pi]])
                    return (n, cs, pi)

                def pv(item):
                    n, cs, pi = item
                    for x in range(2):
                        for ci, c in enumerate(cs):
                            if c == 0:
                                vt = n
                            elif n > 0:
                                vt = n - 1
                            else:
                                vt = 8
                            rhs = PTb[pi][:, (x * 2 + c) * 128:(x * 2 + c + 1) * 128]
                            oc = (n % 4) * 128
                            kb.op("pe", lambda e: e.matmul(pbank[PO][x * 64:(x + 1) * 64, oc:oc + 128],
                                                           lhsT=vtm[:, vt, hk * 64:(hk + 1) * 64], rhs=rhs,
                                                           start=(ci == 0), stop=(ci == len(cs) - 1)),
                                  R=[t_v[vt], t_PT[pi]], W=[pbank_t[PO]])
                            kb.op("pe", lambda e: e.matmul(pbank[PSM][x * 64:(x + 1) * 64, oc:oc + 128],
                                                           lhsT=ones_b[:, 0:64], rhs=rhs,
                                                           start=(ci == 0), stop=(ci == len(cs) - 1)),
                                  R=[t_c2, t_PT[pi]], W=[pbank_t[PSM]])
                    if n % 4 == 3:
                        finalize(j, b, n // 4, True)

                prev = scores(0)
                for n in range(8):
                    nxt = scores(n + 1) if n + 1 < 8 else None
                    pv(prev)
                    prev = nxt

            def mem_head(i, L):
                j = 12 + i
                b = j % 2
                for gi in range(2):
                    pis = []
                    for mc in range(2):
                        bk = sc_ring.next()
                        kb.op("pe", lambda e: e.matmul(pbank[bk][:, :], lhsT=mkT[:, L, i, mc * 128:(mc + 1) * 128],
                                                       rhs=qTb[b][:, gi * 512:(gi + 1) * 512], start=True, stop=True),
                              R=[t_mk, t_qT[b][gi]], W=[pbank_t[bk]])
                        pi = PT_ring.next()
                        kb.op("act", lambda e: e.activation(out=PTb[pi][:, :], in_=pbank[bk][:, :], func=AF.Exp,
                                                            scale=float(128 ** -0.5)), R=[pbank_t[bk]], W=[t_PT[pi]])
                        pis.append(pi)
                    for mc in range(2):
                        pi = pis[mc]
                        kb.op("pe", lambda e: e.matmul(pbank[PO][:, :], lhsT=mvb[:, L, mc, i * 128:(i + 1) * 128],
                                                       rhs=PTb[pi][:, :], start=(mc == 0), stop=(mc == 1)),
                              R=[t_mk, t_PT[pi]], W=[pbank_t[PO]])
                        kb.op("pe", lambda e: e.matmul(pbank[PSM][:, :], lhsT=ones_b[:, :], rhs=PTb[pi][:, :],
                                                       start=(mc == 0), stop=(mc == 1)),
                              R=[t_c2, t_PT[pi]], W=[pbank_t[PSM]])
                    finalize(j, b, gi, False)

            inproj_j(0)
            if P1_STOP == 3:
                kb.end_scope()
                return
            for j in range(16):
                if j + 1 < 16:
                    inproj_j(j + 1)
                if j < 12:
                    swa_pair(j)
                else:
                    mem_head(j - 12, 0)
                if P1_STOP == 4 and j == 0:
                    kb.end_scope()
                    return
                if P1_STOP == 5 and j == 11:
                    kb.end_scope()
                    return
            kb.end_scope()

    def sample_mem(L, st_outer):
        with ExitStack() as st:
            ck = sb(st, "ck", [128, NS, 256], BF16)
            cv = sb(st, "cv", [128, NS, 2, 128], BF16)
            PTs = sb(st, "PTsm", [128, 32], BF16)
            r1 = sb(st, "r1m", [128, NS], F32)
            r2 = sb(st, "r2m", [128, NS], F32)
            t_ck, t_cv, t_PTs, t_r1, t_r2 = Tok("ck"), Tok("cv"), Tok("PTs"), Tok("r1"), Tok("r2")
            for i in range(4):
                kb.dma("pool", ck[:], cmkT_d.ap()[L, i], W=[t_ck])
                kb.dma("pool", cv[:], cmv_d.ap()[L, i], W=[t_cv])
                bk = sc_ring.next()
                for mc in range(2):
                    for bb in range(NS):
                        kb.op("pe", lambda e: e.matmul(pbank[bk][:, mc * NS + bb: mc * NS + bb + 1],
                                                       lhsT=ck[:, bb, mc * 128:(mc + 1) * 128],
                                                       rhs=qTs[:, 12 + i, bb:bb + 1], start=True, stop=True),
                              R=[t_ck, t_qTs], W=[pbank_t[bk]])
                kb.op("act", lambda e: e.activation(out=PTs[:, :], in_=pbank[bk][:, 0:32], func=AF.Exp,
                                                    scale=float(128 ** -0.5)), R=[pbank_t[bk]], W=[t_PTs])
                for bb in range(NS):
                    for mc in range(2):
                        kb.op("pe", lambda e: e.matmul(pbank[PO][:, bb:bb + 1], lhsT=cv[:, bb, mc, :],
                                                       rhs=PTs[:, mc * NS + bb: mc * NS + bb + 1],
                                                       start=(mc == 0), stop=(mc == 1)),
                              R=[t_cv, t_PTs], W=[pbank_t[PO]])
                for mc in range(2):
                    kb.op("pe", lambda e: e.matmul(pbank[PSM][:, 0:NS], lhsT=ones_b[:, :],
                                                   rhs=PTs[:, mc * NS:(mc + 1) * NS], start=(mc == 0), stop=(mc == 1)),
                          R=[t_c2, t_PTs], W=[pbank_t[PSM]])
                kb.op("dve", lambda e: e.reciprocal(out=r1[:], in_=pbank[PSM][:, 0:NS]), R=[pbank_t[PSM]], W=[t_r1])
                kb.op("dve", lambda e: e.tensor_tensor(out=r2[:], in0=pbank[PO][:, 0:NS], in1=r1[:], op=ALU.mult),
                      R=[pbank_t[PO], t_r1], W=[t_r2])
                kb.op("dve", lambda e: e.tensor_tensor(out=brT[:, 12 + i, 1024:1040], in0=r2[:], in1=sgs[:, 12 + i, :],
                                                       op=ALU.mult), R=[t_r2, t_sgs], W=[brt[12 + i][2]])
            kb.end_scope()

    def l0_sample():
        with ExitStack() as st:
            cK = sb(st, "cK", [128, NS, 4, 2, 128], BF16)
            cV = sb(st, "cV", [128, NS, 256], BF16)
            prod = sb(st, "prod", [128, 12, NS], F32)
            pn = sb(st, "pn", [128, 12, NS], F32)
            Es = sb(st, "Es", [128, 32], F32)
            PTs = sb(st, "PTs", [128, 32], BF16)
            a1 = sb(st, "a1", [128, NS], F32)
            a2 = sb(st, "a2", [128, NS], F32)
            t_cK, t_cV, t_prod, t_pn, t_Es, t_PTs, t_a1, t_a2 = (Tok("cK"), Tok("cV"), Tok("prod"), Tok("pn"),
                                                                   Tok("Es"), Tok("PTs"), Tok("a1"), Tok("a2"))
            kb.dma("pool", cK[:], cswaKT_d.ap()[:, :, :, :, :], W=[t_cK])
            kb.dma("pool", cV[:], cswaV_d.ap()[:, :, :], W=[t_cV])
            kb.dma("sp", swaks_sh_o.ap()[:, :, :], cswaKraw_d.ap()[:, 1:128, :], sem_tok=t_out)
            kb.dma("sp", swavs_sh_o.ap()[:, :, :], cswaVraw_d.ap()[:, 1:128, :], sem_tok=t_out)
            for j in range(12):
                kb.op("dve", lambda e: e.tensor_tensor(out=prod[:, j, :], in0=qTs[:, j, :], in1=kTs_f[:, j // 3, :],
                                                       op=ALU.mult), R=[t_qTs, t_kTs], W=[t_prod])
            bk = sc_ring.next()
            kb.op("pe", lambda e: e.matmul(pbank[bk][:, 0:12 * NS], lhsT=blk_f[:, :],
                                           rhs=prod[:, :, :].rearrange("p a b -> p (a b)"), start=True, stop=True),
                  R=[t_prod, t_c4], W=[pbank_t[bk]])
            kb.op("act", lambda e: e.activation(out=pn[:, :, :].rearrange("p a b -> p (a b)"), in_=pbank[bk][:, 0:12 * NS],
                                                func=AF.Exp, scale=0.125), R=[pbank_t[bk]], W=[t_pn])
            for j in range(12):
                kb.op("dve", lambda e: e.tensor_scalar(out=pn[:, j, :], in0=pn[:, j, :], scalar1=expb0[:, j:j + 1],
                                                       scalar2=None, op0=ALU.mult), R=[t_small], W=[t_pn])
            for j in range(12):
                hk = j // 3
                bk = sc_ring.next()
                for x in range(2):
                    for bb in range(NS):
                        kb.op("pe", lambda e: e.matmul(pbank[bk][:, x * NS + bb: x * NS + bb + 1],
                                                       lhsT=cK[:, bb, hk, x, :],
                                                       rhs=qTs[:, j, bb:bb + 1], start=True, stop=True),
                              R=[t_cK, t_qTs], W=[pbank_t[bk]])
                kb.op("act", lambda e: e.activation(out=Es[:, :], in_=pbank[bk][:, 0:32], func=AF.Exp, scale=0.125),
                      R=[pbank_t[bk]], W=[t_Es])
                for x in range(2):
                    kb.op("dve", lambda e: e.tensor_scalar(out=PTs[:, x * NS:(x + 1) * NS], in0=Es[:, x * NS:(x + 1) * NS],
                                                           scalar1=expbcol[:, 2 * j + x: 2 * j + x + 1], scalar2=None,
                                                           op0=ALU.mult), R=[t_Es, t_small], W=[t_PTs])
                for x in range(2):
                    for bb in range(NS):
                        kb.op("pe", lambda e: e.matmul(pbank[PO][x * 64:(x + 1) * 64, bb:bb + 1],
                                                       lhsT=cV[:, bb, hk * 64:(hk + 1) * 64],
                                                       rhs=PTs[:, x * NS + bb: x * NS + bb + 1], start=True, stop=True),
                              R=[t_cV, t_PTs], W=[pbank_t[PO]])
                    kb.op("pe", lambda e: e.matmul(pbank[PSM][x * 64:(x + 1) * 64, 0:NS], lhsT=ones_b[:, 0:64],
                                                   rhs=PTs[:, x * NS:(x + 1) * NS], start=True, stop=True),
                          R=[t_c2, t_PTs], W=[pbank_t[PSM]])
                kb.op("dve", lambda e: e.tensor_tensor(out=a1[:], in0=pn[:, j, :], in1=vTs_f[:, hk, :], op=ALU.mult),
                      R=[t_pn, t_vTs], W=[t_a1])
                kb.op("dve", lambda e: e.tensor_tensor(out=a1[:], in0=a1[:], in1=pbank[PO][:, 0:NS], op=ALU.add),
                      R=[pbank_t[PO]], W=[t_a1])
                kb.op("dve", lambda e: e.scalar_tensor_tensor(out=a2[:], in0=pbank[PSM][:, 0:NS],
                                                              scalar=expsink[:, j:j + 1], in1=pn[:, j, :],
                                                              op0=ALU.add, op1=ALU.add),
                      R=[pbank_t[PSM], t_pn, t_small], W=[t_a2])
                kb.op("dve", lambda e: e.reciprocal(out=a2[:], in_=a2[:]), R=[t_a2], W=[t_a2])
                kb.op("dve", lambda e: e.tensor_tensor(out=a1[:], in0=a1[:], in1=a2[:], op=ALU.mult),
                      R=[t_a2], W=[t_a1])
                kb.op("dve", lambda e: e.tensor_tensor(out=brT[:, j, 1024:1040], in0=a1[:], in1=sgs[:, j, :],
                                                       op=ALU.mult), R=[t_a1, t_sgs], W=[brt[j][2]])
            kb.end_scope()


    def l1_phase1(hf):
        with ExitStack() as st:
            wbs = [sb(st, "wb%d" % i, [128, 16, 256], BF16) for i in range(3)]
            qh = sb(st, "qh", [128, BW], F32)
            fg = sb(st, "fg", [128, BW], F32)
            kh = sb(st, "kh", [128, BW], F32)
            gl = sb(st, "gl", [128, BW], F32)
            gcs = sb(st, "gcs", [128, 1024], F32)
            eg = sb(st, "eg", [128, 1024], F32)
            ek = sb(st, "ek", [128, 1024], F32)
            qt = sb(st, "qt", [128, 1024], BF16)
            kt = sb(st, "kt", [128, 1024], BF16)
            vtm = sb(st, "vtm1", [128, 9, 128], BF16)
            vsf = sb(st, "vsf", [NS, 128], F32)
            sg = sb(st, "sg1", [128, BW], F32)
            qTm = sb(st, "qTm", [128, BW], BF16)
            ktT = sb(st, "ktT", [128, 2, 8, 128], BF16)
            ATp = [sb(st, "ATp%d" % i, [128, 2, 64], BF16) for i in range(2)]
            oT = sb(st, "oT", [128, 512], F32)
            osq = sb(st, "osq", [128, 512], F32)
            rstd = sb(st, "rstd", [128, 512], F32)
            PTb = [sb(st, "PTm%d" % i, [128, 512], BF16) for i in range(2)]
            rsb = sb(st, "rsb1", [128, 512], F32)
            tmpb = sb(st, "tmpb1", [128, 512], F32)
            t_wb = [Tok("wb%d" % i) for i in range(3)]
            t_qh, t_fg, t_kh, t_gl, t_gcs, t_eg, t_ek, t_qt, t_kt = (Tok("qh"), Tok("fg"), Tok("kh"), Tok("gl"),
                                                                       Tok("gcs"), Tok("eg"), Tok("ek"), Tok("qt"), Tok("kt"))
            t_v = [Tok("v1_%d" % i) for i in range(9)]
            t_vsf, t_sg, t_qTm, t_ktT, t_oT, t_osq, t_rstd = (Tok("vsf"), Tok("sg1"), Tok("qTm"), Tok("ktT"), Tok("oT"),
                                                             Tok("osq"), Tok("rstd"))
            t_AT = [Tok("AT0"), Tok("AT1")]
            t_PT = [Tok("PTm0"), Tok("PTm1")]
            t_rs, t_tmp = Tok("rs1"), Tok("tmp1")
            t_vscr = Tok("vscr")
            PT_ring, AT_ring, tr_ring = Ring([0, 1]), Ring([0, 1]), Ring([0, 1])
            if hf == 0:
                Sin = sb(st, "Sin", [128, NS, 128], F32)
                vbc = sb(st, "vbc", [128, NS, 128], F32)
                t_Sin, t_vbc = Tok("Sin"), Tok("vbc")
            kb.op("dve", lambda e: e.memset(ktT[:, :, :, :].rearrange("p a b c -> p (a b c)"), 0.0), W=[t_ktT])
            if hf == 0:
                for h in range(12):
                    kb.op("dve", lambda e: e.memset(Sst[:, h, :], 0.0), W=[t_S[h]])
                    kb.op("dve", lambda e: e.memset(Sbf[:, h, :], 0.0), W=[t_S[h]])
            nslab = 28
            wring = Ring([0, 1, 2])
            slab_buf = {}
            nxt_slab = [0]

            def ensure_slab(upto):
                while nxt_slab[0] <= min(upto, nslab - 1):
                    s_ = nxt_slab[0]
                    i = wring.next()
                    kb.dma("pool", wbs[i][:], w1_d.ap()[s_], W=[t_wb[i]])
                    slab_buf[s_] = i
                    nxt_slab[0] += 1

            ensure_slab(1)
            groups = [(0, 512), (512, 1024)] + ([(1024, 1040)] if hf == 0 else [])
            ncols = BW if hf == 0 else 1024
            SC = float(128 ** -0.5)

            def hgrn_head(h):
                s0, s1 = 2 * h, 2 * h + 1
                ensure_slab(s0 + 2)
                w0i = slab_buf[s0]
                for (c0, c1) in groups:
                    bk = proj_group(wbs[w0i], t_wb[w0i], (0, 128), c0, c1)
                    kb.op("act", lambda e: e.activation(out=qh[:, c0:c1], in_=pbank[bk][:, 0:c1 - c0], func=AF.Silu),
                          R=[pbank_t[bk]], W=[t_qh])
                    bk = proj_group(wbs[w0i], t_wb[w0i], (128, 256), c0, c1)
                    kb.op("act", lambda e: e.activation(out=fg[:, c0:c1], in_=pbank[bk][:, 0:c1 - c0], func=AF.Sigmoid),
                          R=[pbank_t[bk]], W=[t_fg])
                ensure_slab(s1 + 2)
                w1i = slab_buf[s1]
                for (c0, c1) in groups:
                    bk = proj_group(wbs[w1i], t_wb[w1i], (128, 256), c0, c1)
                    kb.op("act", lambda e: e.activation(out=sg[:, c0:c1], in_=pbank[bk][:, 0:c1 - c0], func=AF.Silu),
                          R=[pbank_t[bk]], W=[t_sg])
                ntile = 9 if hf == 0 else 8
                for t in range(ntile):
                    M = 128 if t < 8 else NS
                    bk = proj_ring.next()
                    htok = [hTt[t]] if t < 8 else [hTx]
                    for kc in range(16):
                        kb.op("pe", lambda e: e.matmul(pbank[bk][0:M, 0:128], lhsT=hT[:, kc, t * 128:t * 128 + M],
                                                       rhs=wbs[w1i][:, kc, 0:128], start=(kc == 0), stop=(kc == 15)),
                              R=[t_wb[w1i]] + htok, W=[pbank_t[bk]])
                    kb.op("act", lambda e: e.activation(out=vtm[0:M, t, :], in_=pbank[bk][0:M, 0:128], func=AF.Copy),
                          R=[pbank_t[bk]], W=[t_v[t]])
                    if t == 8:
                        kb.op("dve", lambda e: e.tensor_copy(out=vsf[:, :], in_=pbank[bk][0:NS, 0:128]),
                              W=[pbank_t[bk], t_vsf])
                        kb.dma("sp", vscr_d.ap()[:, h * 128:(h + 1) * 128], vsf[:, :], R=[t_vsf], W=[t_vscr], store=True)
                kb.op("dve", lambda e: e.tensor_scalar(out=fg[:, 0:ncols], in0=fg[:, 0:ncols], scalar1=omlbT[:, h:h + 1],
                                                       scalar2=lbT[:, h:h + 1], op0=ALU.mult, op1=ALU.add),
                      R=[t_small], W=[t_fg])
                kb.op("dve", lambda e: e.tensor_scalar(out=kh[:, 0:ncols], in0=fg[:, 0:ncols], scalar1=-1.0, scalar2=1.0,
                                                       op0=ALU.mult, op1=ALU.add), R=[t_fg], W=[t_kh])
                kb.op("act", lambda e: e.activation(out=gl[:, 0:1024], in_=fg[:, 0:1024], func=AF.Ln), R=[t_fg], W=[t_gl])
                kb.op("dve", lambda e: e.tensor_tensor_scan(out=gcs[:, :], data0=rmask[:, :], data1=gl[:, 0:1024],
                                                            initial=0.0, op0=ALU.mult, op1=ALU.add),
                      R=[t_gl, t_c5], W=[t_gcs])
                kb.op("act", lambda e: e.activation(out=eg[:, :], in_=gcs[:, :], func=AF.Exp), R=[t_gcs], W=[t_eg])
                kb.op("act", lambda e: e.activation(out=ek[:, :], in_=gcs[:, :], func=AF.Exp, scale=-1.0),
                      R=[t_gcs], W=[t_ek])
                kb.op("dve", lambda e: e.tensor_tensor(out=qt[:, :], in0=qh[:, 0:1024], in1=eg[:, :], op=ALU.mult),
                      R=[t_qh, t_eg], W=[t_qt])
                kb.op("dve", lambda e: e.tensor_tensor(out=kt[:, :], in0=kh[:, 0:1024], in1=ek[:, :], op=ALU.mult),
                      R=[t_kh, t_ek], W=[t_kt])
                for t in range(8):
                    pi = tr_ring.next()
                    kb.op("pe", lambda e: e.transpose(out=ptr[pi][:, 0:128], in_=kt[:, t * 128:(t + 1) * 128],
                                                      identity=ident[:, :]), R=[t_kt, t_c1], W=[ptr_t[pi]])
                    kb.op("act", lambda e: e.activation(out=ktT[0:64, 0, t, :], in_=ptr[pi][0:64, 0:128], func=AF.Copy),
                          R=[ptr_t[pi]], W=[t_ktT])
                    kb.op("act", lambda e: e.activation(out=ktT[64:128, 1, t, :], in_=ptr[pi][64:128, 0:128], func=AF.Copy),
                          R=[ptr_t[pi]], W=[t_ktT])
                eg3 = eg[:, :].rearrange("p (c s) -> p c s", s=64)
                for t in range(8):
                    bs = sc_ring.next()
                    for u in range(2):
                        c = 2 * t + u
                        kb.op("pe", lambda e: e.matmul(pbank[bs][u * 64:(u + 1) * 64, 0:64], lhsT=kt[:, c * 64:(c + 1) * 64],
                                                       rhs=qt[:, c * 64:(c + 1) * 64], start=True, stop=True),
                              R=[t_kt, t_qt], W=[pbank_t[bs]])
                    ai = AT_ring.next()
                    for u in range(2):
                        kb.op("dve", lambda e: e.tensor_tensor(out=ATp[ai][:, u, :], in0=pbank[bs][:, 0:64],
                                                               in1=cmask[:, u, :], op=ALU.mult),
                              R=[pbank_t[bs], t_c5], W=[t_AT[ai]])
                    for u in range(2):
                        c = 2 * t + u
                        oc = (c % 8) * 64
                        kb.op("pe", lambda e: e.matmul(pbank[PO][:, oc:oc + 64], lhsT=Sbf[:, h, :],
                                                       rhs=qt[:, c * 64:(c + 1) * 64], start=True, stop=False),
                              R=[t_S[h], t_qt], W=[pbank_t[PO]])
                        kb.op("pe", lambda e: e.matmul(pbank[PO][:, oc:oc + 64], lhsT=vtm[:, t, :],
                                                       rhs=ATp[ai][:, u, :], start=False, stop=True),
                              R=[t_v[t], t_AT[ai]], W=[pbank_t[PO]])
                        kb.op("pe", lambda e: e.matmul(pbank[PSM][:, 0:128], lhsT=ktT[:, u, t, :], rhs=vtm[:, t, :],
                                                       start=True, stop=True), R=[t_ktT, t_v[t]], W=[pbank_t[PSM]])
                        ecol = eg3[:, c, 63:64]
                        kb.op("dve", lambda e: e.tensor_scalar(out=Sst[:, h, :], in0=Sst[:, h, :], scalar1=ecol,
                                                               scalar2=None, op0=ALU.mult), R=[t_eg], W=[t_S[h]])
                        kb.op("dve", lambda e: e.scalar_tensor_tensor(out=Sst[:, h, :], in0=pbank[PSM][:, 0:128],
                                                                      scalar=ecol, in1=Sst[:, h, :], op0=ALU.mult,
                                                                      op1=ALU.add),
                              R=[pbank_t[PSM], t_eg], W=[t_S[h]])
                        kb.op("act", lambda e: e.activation(out=Sbf[:, h, :], in_=Sst[:, h, :], func=AF.Copy),
                              R=[], W=[t_S[h]])
                    if t % 4 == 3:
                        rms_finish(h, (t // 4) * 512, 512, t // 4)

            def rms_finish(h, c0, n, gi):
                kb.op("act", lambda e: e.activation(out=osq[:, 0:n], in_=pbank[PO][:, 0:n], func=AF.Square),
                      R=[pbank_t[PO]], W=[t_osq])
                kb.op("dve", lambda e: e.tensor_copy(out=oT[:, 0:n], in_=pbank[PO][:, 0:n]), W=[pbank_t[PO], t_oT])
                bn = sc_ring.next()
                kb.op("pe", lambda e: e.matmul(pbank[bn][:, 0:n], lhsT=ones_f[:, :], rhs=osq[:, 0:n], start=True, stop=True),
                      R=[t_osq, t_c3], W=[pbank_t[bn]])
                kb.op("act", lambda e: e.activation(out=rstd[:, 0:n], in_=pbank[bn][:, 0:n], func=AF.Sqrt,
                                                    bias=epsrms[:, :], scale=1.0 / 128.0),
                      R=[pbank_t[bn], t_const], W=[t_rstd])
                kb.op("dve", lambda e: e.reciprocal(out=rstd[:, 0:n], in_=rstd[:, 0:n]), R=[t_rstd], W=[t_rstd])
                kb.op("dve", lambda e: e.tensor_tensor(out=oT[:, 0:n], in0=oT[:, 0:n], in1=rstd[:, 0:n], op=ALU.mult),
                      R=[t_rstd], W=[t_oT])
                kb.op("dve", lambda e: e.scalar_tensor_tensor(out=brT[:, h, c0:c0 + n], in0=oT[:, 0:n],
                                                              scalar=nwT[:, h:h + 1], in1=sg[:, c0:c0 + n],
                                                              op0=ALU.mult, op1=ALU.mult),
                      R=[t_oT, t_sg, t_small], W=[brt[h][gi]])

            def hgrn_sample(h):
                kb.dma("sp", Sin[:], hst_d.ap()[h], W=[t_Sin])
                kb.dma("sp", vbc[:], bass.AP(vscr_d, h * 128, [[0, 128], [1536, NS], [1, 128]]), R=[t_vscr], W=[t_vbc])
                for bb in range(NS):
                    kb.op("dve", lambda e: e.tensor_scalar(out=Sin[:, bb, :], in0=Sin[:, bb, :],
                                                           scalar1=fg[:, 1024 + bb:1025 + bb], scalar2=None, op0=ALU.mult),
                          R=[t_fg], W=[t_Sin])
                    kb.op("dve", lambda e: e.scalar_tensor_tensor(out=Sin[:, bb, :], in0=vbc[:, bb, :],
                                                                  scalar=kh[:, 1024 + bb:1025 + bb], in1=Sin[:, bb, :],
                                                                  op0=ALU.mult, op1=ALU.add),
                          R=[t_vbc, t_kh], W=[t_Sin])
                for bb in range(NS):
                    kb.op("pe", lambda e: e.matmul(pbank[PO][:, bb:bb + 1], lhsT=Sin[:, bb, :],
                                                   rhs=qh[:, 1024 + bb:1025 + bb], start=True, stop=True),
                          R=[t_Sin, t_qh], W=[pbank_t[PO]])
                kb.dma("sp", hsts_o.ap()[h], Sin[:], R=[t_Sin], store=True)
                rms_finish(h, 1024, NS, 2)

            def mem_head(i, L):
                j = 12 + i
                s_ = 24 + i
                ensure_slab(s_ + 2)
                wi = slab_buf[s_]
                for (c0, c1) in groups:
                    bk = proj_group(wbs[wi], t_wb[wi], (0, 128), c0, c1)
                    kb.op("act", lambda e: e.activation(out=qTm[:, c0:c1], in_=pbank[bk][:, 0:c1 - c0], func=AF.Copy),
                          R=[pbank_t[bk]], W=[t_qTm])
                    bk = proj_group(wbs[wi], t_wb[wi], (128, 256), c0, c1)
                    kb.op("act", lambda e: e.activation(out=sg[:, c0:c1], in_=pbank[bk][:, 0:c1 - c0], func=AF.Silu),
                          R=[pbank_t[bk]], W=[t_sg])
                if hf == 0:
                    kb.op("dve", lambda e: e.tensor_copy(out=qTs[:, j, :], in_=qTm[:, 1024:1040]), R=[t_qTm], W=[t_qTs])
                    kb.op("dve", lambda e: e.tensor_copy(out=sgs[:, j, :], in_=sg[:, 1024:1040]), R=[t_sg], W=[t_sgs])
                for gi in range(2):
                    pis = []
                    for mc in range(2):
                        bk = sc_ring.next()
                        kb.op("pe", lambda e: e.matmul(pbank[bk][:, :], lhsT=mkT[:, L, i, mc * 128:(mc + 1) * 128],
                                                       rhs=qTm[:, gi * 512:(gi + 1) * 512], start=True, stop=True),
                              R=[t_mk, t_qTm], W=[pbank_t[bk]])
                        pi = PT_ring.next()
                        kb.op("act", lambda e: e.activation(out=PTb[pi][:, :], in_=pbank[bk][:, :], func=AF.Exp, scale=SC),
                              R=[pbank_t[bk]], W=[t_PT[pi]])
                        pis.append(pi)
                    for mc in range(2):
                        pi = pis[mc]
                        kb.op("pe", lambda e: e.matmul(pbank[PO][:, :], lhsT=mvb[:, L, mc, i * 128:(i + 1) * 128],
                                                       rhs=PTb[pi][:, :], start=(mc == 0), stop=(mc == 1)),
                              R=[t_mk, t_PT[pi]], W=[pbank_t[PO]])
                        kb.op("pe", lambda e: e.matmul(pbank[PSM][:, :], lhsT=ones_b[:, :], rhs=PTb[pi][:, :],
                                                       start=(mc == 0), stop=(mc == 1)),
                              R=[t_c2, t_PT[pi]], W=[pbank_t[PSM]])
                    c0 = gi * 512
                    kb.op("dve", lambda e: e.reciprocal(out=rsb[:], in_=pbank[PSM][:, :]), R=[pbank_t[PSM]], W=[t_rs])
                    kb.op("dve", lambda e: e.tensor_tensor(out=tmpb[:], in0=pbank[PO][:, :], in1=rsb[:], op=ALU.mult),
                          R=[pbank_t[PO], t_rs], W=[t_tmp])
                    kb.op("dve", lambda e: e.tensor_tensor(out=brT[:, j, c0:c0 + 512], in0=tmpb[:],
                                                           in1=sg[:, c0:c0 + 512], op=ALU.mult),
                          R=[t_tmp, t_sg], W=[brt[j][gi]])

            for h in range(12):
                hgrn_head(h)
                if hf == 0:
                    hgrn_sample(h)
            for i in range(4):
                mem_head(i, 1)
            if hf == 1:
                kb.dma("sp", hstp_o.ap().rearrange("h p v -> p h v"), Sst[:, :, :], R=t_S, store=True, sem_tok=t_S[0])
            kb.end_scope()

    for hf in range(2):
        if STOP == "setup":
            break
        l0_phase1(hf)
        if STOP == "p1a":
            break
        if hf == 0:
            l0_sample()
            sample_mem(0, None)
        if STOP == "sample":
            break
        phase2(0, hf)
        if STOP == "p2a":
            break
        if DEBUG_L0_ONLY:
            continue
        l1_phase1(hf)
        if hf == 0:
            sample_mem(1, None)
        phase2(1, hf)
    kb.barrier()
    es.close()
    return nc, kb


def arr_kp(w):
    w = np.asarray(w, dtype=np.float32)
    return np.ascontiguousarray(w.reshape(16, 128, -1).transpose(1, 0, 2))


def host_consts():
    c = {}
    c["c_ident"] = np.eye(128, dtype=np.float32)
    c["c_ones"] = np.ones((128, 128), np.float32)
    blk = np.zeros((128, 128), np.float32)
    blk[:64, :64] = 1
    blk[64:, 64:] = 1
    c["c_blk"] = blk
    oh = np.zeros((32, 384), np.float32)
    vm = np.zeros((24, 384), np.float32)
    bk = t5_bucket_np(np.arange(128))
    for d in range(128):
        oh[bk[d], 127 + d] = 1.0
        vm[:, 127 + d] = 1.0
    c["c_oh"] = oh
    c["c_vm"] = vm
    cm = np.zeros((128, 2, 64), np.float32)
    for r in range(128):
        cm[r, r // 64, (r % 64):] = 1.0
    c["c_cm"] = cm
    rm = np.ones((128, 1024), np.float32)
    rm[:, ::64] = 0.0
    c["c_rm"] = rm
    return c


_CACHE = {}


def kernel(x_prompt, x_sample, cache_mem_k, cache_mem_v, cache_swa_k, cache_swa_v, state_hgrn, mem_prompt,
           rel_bias, swa_w_in, swa_sinks, hg_w_in, hg_lb_logits, hg_norm_w, w_mem_k, w_mem_v, w_out, ln_w, ln_b):
    f = lambda a: np.asarray(a, dtype=np.float32)
    x_prompt, x_sample = f(x_prompt), f(x_sample)
    if "nc" not in _CACHE:
        _CACHE["nc"] = build_program()
    nc, kb = _CACHE["nc"]
    consts = host_consts()
    W = f(swa_w_in)[0]
    q, k, v, mq, g = W[:, :1536], W[:, 1536:1792], W[:, 1792:2048], W[:, 2048:2560], W[:, 2560:]
    slabs = []
    for a in (0, 2):
        slabs.append(np.concatenate([k[:, a * 64:(a + 1) * 64]] * 2 + [k[:, (a + 1) * 64:(a + 2) * 64]] * 2, axis=1))
    slabs.append(v)
    for j in range(16):
        A = q[:, j * 128:(j + 1) * 128] if j < 12 else mq[:, (j - 12) * 128:(j - 11) * 128]
        slabs.append(np.concatenate([A, g[:, j * 128:(j + 1) * 128]], axis=1))
    w0 = np.stack([arr_kp(s) for s in slabs])
    W = f(hg_w_in)[0]
    q, fq, iv, mq, g = W[:, :1536], W[:, 1536:3072], W[:, 3072:4608], W[:, 4608:5120], W[:, 5120:]
    slabs = []
    for h in range(12):
        sl = slice(h * 128, (h + 1) * 128)
        slabs.append(np.concatenate([q[:, sl], fq[:, sl]], axis=1))
        slabs.append(np.concatenate([iv[:, sl], g[:, sl]], axis=1))
    for i in range(4):
        slabs.append(np.concatenate([mq[:, i * 128:(i + 1) * 128], g[:, (12 + i) * 128:(13 + i) * 128]], axis=1))
    w1 = np.stack([arr_kp(s) for s in slabs])
    wk = np.stack([arr_kp(f(w_mem_k)[i]) for i in range(2)])
    wv = np.stack([arr_kp(f(w_mem_v)[i]) for i in range(2)])
    wo = np.stack([arr_kp(f(w_out)[i]) for i in range(2)])
    relb = f(rel_bias)
    hd_half = np.arange(128) // 64
    relb0 = np.stack([relb[0, 2 * j + hd_half] for j in range(12)], axis=1)
    sinksc = np.stack([f(swa_sinks)[0, 2 * j + hd_half] for j in range(12)], axis=1)
    lbl = np.ascontiguousarray(f(hg_lb_logits).reshape(2, 12, 128).transpose(2, 0, 1))
    hgnw = np.ascontiguousarray(f(hg_norm_w)[0].reshape(12, 128).T)
    shared = dict(wk=wk, wv=wv, w0=w0, w1=w1, wo=wo, lnw=f(ln_w), lnb=f(ln_b), relb=relb,
                  relb0=np.ascontiguousarray(relb0), sinksc=np.ascontiguousarray(sinksc), lbl=lbl, hgnw=hgnw)
    shared.update(consts)
    cmk, cmv_ = f(cache_mem_k), f(cache_mem_v)
    csk, csv = f(cache_swa_k)[0], f(cache_swa_v)[0]
    sth = f(state_hgrn)[0]
    mem_prompt = f(mem_prompt)
    in_maps = []
    for c in range(NCORES):
        sl = slice(c * NS, (c + 1) * NS)
        m = dict(shared)
        m["xT"] = arr_kp(x_prompt[c].T)
        m["xsT"] = arr_kp(x_sample[sl, 0, :].T)
        m["xres"] = np.ascontiguousarray(x_prompt[c])
        m["xsres"] = np.ascontiguousarray(x_sample[sl, 0, :])
        m["memT"] = arr_kp(mem_prompt[c].T)
        kk = csk[sl]
        kT = kk.transpose(3, 0, 2, 1)
        kTp = np.zeros((128, NS, 4, 2, 128), np.float32)
        kTp[0:64, :, :, 0, :] = kT
        kTp[64:128, :, :, 1, :] = kT
        m["cswaKT"] = kTp
        m["cswaV"] = np.ascontiguousarray(csv[sl].reshape(NS, 128, 256).transpose(1, 0, 2))
        m["cswaKraw"] = np.ascontiguousarray(kk.reshape(NS, 128, 256))
        m["cswaVraw"] = np.ascontiguousarray(csv[sl].reshape(NS, 128, 256))
        m["cmkT"] = np.ascontiguousarray(cmk[:, sl].transpose(0, 3, 4, 1, 2))
        m["cmv"] = np.ascontiguousarray(cmv_[:, sl].reshape(2, NS, 2, 128, 4, 128).transpose(0, 4, 3, 1, 2, 5))
        m["hst"] = np.ascontiguousarray(sth[sl].transpose(1, 2, 0, 3))
        in_maps.append(m)
    if STOP == 'setup':
        for m in in_maps:
            for kname in BIG:
                m[kname] = np.zeros((1, 1), np.float32)
    if DEBUG_NCORES < NCORES:
        res = run_bass_kernel_spmd(nc, in_maps[:DEBUG_NCORES], core_ids=list(range(DEBUG_NCORES)))
        R = list(res.results) + [res.results[0]] * (NCORES - DEBUG_NCORES)
    else:
        res = run_bass_kernel_spmd(nc, in_maps, core_ids=list(range(NCORES)))
        R = res.results
    y_prompt = np.stack([R[c]["y"] for c in range(NCORES)])
    y_sample = np.concatenate([R[c]["ys"] for c in range(NCORES)])[:, None, :]
    mem_k = np.stack([R[c]["mkT_o"].transpose(0, 3, 1, 2) for c in range(NCORES)], axis=1)
    mem_v = np.stack([R[c]["mv_o"].reshape(2, 256, 4, 128) for c in range(NCORES)], axis=1)
    swa_kp = np.stack([R[c]["swakT_o"].transpose(2, 0, 1) for c in range(NCORES)])[None]
    swa_vp = np.stack([R[c]["swav_o"].reshape(128, 4, 64) for c in range(NCORES)])[None]
    hstp = np.stack([R[c]["hstp_o"] for c in range(NCORES)])[None]
    ks = []
    vs = []
    hs = []
    for c in range(NCORES):
        knew = R[c]["swaksT_o"].transpose(2, 0, 1).reshape(NS, 1, 256)
        ks.append(np.concatenate([R[c]["swaks_sh_o"], knew], axis=1).reshape(NS, 128, 4, 64))
        vnew = R[c]["swavs_new_o"].reshape(NS, 1, 256)
        vs.append(np.concatenate([R[c]["swavs_sh_o"], vnew], axis=1).reshape(NS, 128, 4, 64))
        hs.append(R[c]["hsts_o"].transpose(2, 0, 1, 3))
    swa_ks = np.concatenate(ks)[None]
    swa_vs = np.concatenate(vs)[None]
    hsts = np.concatenate(hs)[None]
    outs = (y_prompt, y_sample, mem_k, mem_v, swa_kp, swa_vp, hstp, swa_ks, swa_vs, hsts)
    return tuple(np.ascontiguousarray(o, dtype=np.float32) for o in outs)
```

```python
import numpy as np
from contextlib import ExitStack
import concourse.bass as bass
import concourse.mybir as mybir
from concourse.bass_utils import run_bass_kernel_spmd

F32 = mybir.dt.float32
BF16 = mybir.dt.bfloat16
AF = mybir.ActivationFunctionType
ALU = mybir.AluOpType

NCORES = 8
D = 2048
SEQ = 2048
TT = 1024
NS = 16
XW = 1152
BW = 1040
ALPHA = float((2.0 * 2) ** 0.25)
LN_EPS = 1e-5
RMS_EPS = 1e-6
DEBUG_L0_ONLY = False
STOP = None
DEBUG_NCORES = 8
SETUP_STOP = 99
P1_STOP = 99
BIG = ('xT', 'xres', 'w0', 'w1', 'wo', 'cmkT', 'cmv', 'hst', 'cswaKT', 'cswaV', 'cswaKraw', 'cswaVraw')


class _Stop(Exception):
    pass


class Tok:
    __slots__ = ("name", "w", "r", "dsem", "ssem", "p")

    def __init__(self, name, p=False):
        self.name = name
        self.w = None
        self.r = {}
        self.dsem = None
        self.ssem = None
        self.p = p


class KB:
    ENG = ("pe", "act", "dve", "pool", "sp")

    def __init__(self, nc, es, n_dma_sems=95):
        self.nc = nc
        self.eng = {"pe": nc.tensor, "act": nc.scalar, "dve": nc.vector, "pool": nc.gpsimd, "sp": nc.sync}
        self.sems = []
        self.esem = {}
        for e in self.ENG:
            self.esem[e] = len(self.sems)
            self.sems.append(es.enter_context(nc.semaphore("e_" + e)))
        self.free_dma = []
        for i in range(n_dma_sems):
            self.free_dma.append(len(self.sems))
            self.sems.append(es.enter_context(nc.semaphore("d%d" % i)))
        self.cnt = {e: 0 for e in self.ENG}
        self.known = {e: {} for e in self.ENG}
        self.dcnt = {}
        self.nwait = 0
        self.scoped = []

    def _wait(self, e, deps):
        k = self.known[e]
        pe_own = self.esem["pe"]
        for (s, v) in deps:
            if e == "pe" and s == pe_own:
                continue
            if k.get(s, 0) >= v:
                continue
            self.eng[e].wait_ge(self.sems[s], v)
            self.nwait += 1
            k[s] = v

    @staticmethod
    def _deps(R, W):
        d = []
        for t in R:
            if t.w is not None:
                d.append(t.w)
        for t in W:
            if t.w is not None:
                d.append(t.w)
            d.extend(t.r.items())
        return d

    def op(self, e, fn, R=(), W=()):
        self._wait(e, self._deps(R, W))
        ins = fn(self.eng[e])
        self.cnt[e] += 1
        s = self.esem[e]
        v = self.cnt[e]
        ins.then_inc(self.sems[s], 1)
        for t in R:
            t.r[s] = v
        for t in W:
            t.w = (s, v)
            t.r = {}

    def _dsem(self, tok, store):
        if store:
            if tok.ssem is None:
                tok.ssem = self.free_dma.pop()
                self.dcnt.setdefault(tok.ssem, 0)
                if not tok.p:
                    self.scoped.append(tok)
            return tok.ssem
        if tok.dsem is None:
            tok.dsem = self.free_dma.pop()
            self.dcnt.setdefault(tok.dsem, 0)
            if not tok.p:
                self.scoped.append(tok)
        return tok.dsem

    def end_scope(self):
        self.barrier()
        for t in self.scoped:
            for a in ("dsem", "ssem"):
                v = getattr(t, a)
                if v is not None:
                    self.free_dma.append(v)
                    setattr(t, a, None)
        self.scoped = []

    def dma(self, e, out, in_, R=(), W=(), sem_tok=None, store=False):
        if sem_tok is None:
            sem_tok = R[0] if store else W[0]
        s = self._dsem(sem_tok, store)
        deps = [(a, b) for (a, b) in self._deps(R, W) if a != s]
        self._wait(e, deps)
        ins = self.eng[e].dma_start(out=out, in_=in_)
        self.dcnt[s] += 16
        v = self.dcnt[s]
        ins.then_inc(self.sems[s], 16)
        for t in R:
            t.r[s] = v
        for t in W:
            t.w = (s, v)
            t.r = {}

    def barrier(self, engines=None):
        engines = engines or self.ENG
        for e in engines:
            deps = [(self.esem[x], self.cnt[x]) for x in self.ENG if x != e and self.cnt[x] > 0]
            deps += [(s, v) for s, v in self.dcnt.items() if v > 0]
            self._wait(e, deps)


class Ring:
    def __init__(self, items):
        self.items = items
        self.i = 0

    def next(self):
        it = self.items[self.i % len(self.items)]
        self.i += 1
        return it


def t5_bucket_np(d):
    n = np.maximum(d, 0)
    max_exact = 16
    nf = np.maximum(n, 1).astype(np.float32)
    large = max_exact + (np.log(nf / np.float32(max_exact)) / np.float32(np.log(128 / max_exact))
                         * np.float32(32 - max_exact)).astype(np.int32)
    large = np.minimum(large, 31)
    return np.where(n < max_exact, n, large)


def build_program():
    nc = bass.Bass("TRN2", target_bir_lowering=False)

    def din(name, shape):
        if STOP == "setup" and name in BIG:
            shape = [1, 1]
        return nc.dram_tensor(name, list(shape), F32, kind="ExternalInput")

    def dout(name, shape):
        return nc.dram_tensor(name, list(shape), F32, kind="ExternalOutput")

    xT_d = din("xT", [128, 16, SEQ])
    xsT_d = din("xsT", [128, 16, NS])
    xres_d = din("xres", [SEQ, D])
    xsres_d = din("xsres", [NS, D])
    memT_d = din("memT", [128, 16, 256])
    wk_d = din("wk", [2, 128, 16, 512])
    wv_d = din("wv", [2, 128, 16, 512])
    w0_d = din("w0", [19, 128, 16, 256])
    w1_d = din("w1", [28, 128, 16, 256])
    wo_d = din("wo", [2, 128, 16, D])
    lnw_d = din("lnw", [2, D])
    lnb_d = din("lnb", [2, D])
    relb_d = din("relb", [32, 24])
    relb0_d = din("relb0", [128, 12])
    sinks_d = din("sinksc", [128, 12])
    lbl_d = din("lbl", [128, 2, 12])
    hgnw_d = din("hgnw", [128, 12])
    cswaKT_d = din("cswaKT", [128, NS, 4, 2, 128])
    cswaV_d = din("cswaV", [128, NS, 256])
    cswaKraw_d = din("cswaKraw", [NS, 128, 256])
    cswaVraw_d = din("cswaVraw", [NS, 128, 256])
    cmkT_d = din("cmkT", [2, 4, 128, NS, 256])
    cmv_d = din("cmv", [2, 4, 128, NS, 2, 128])
    hst_d = din("hst", [12, 128, NS, 128])
    c_ident_d = din("c_ident", [128, 128])
    c_ones_d = din("c_ones", [128, 128])
    c_blk_d = din("c_blk", [128, 128])
    c_oh_d = din("c_oh", [32, 384])
    c_vm_d = din("c_vm", [24, 384])
    c_cm_d = din("c_cm", [128, 2, 64])
    c_rm_d = din("c_rm", [128, 1024])

    y_d = dout("y", [SEQ, D])
    ys_d = dout("ys", [NS, D])
    mkT_o = dout("mkT_o", [2, 4, 128, 256])
    mv_o = dout("mv_o", [2, 256, 512])
    swakT_o = dout("swakT_o", [4, 64, 128])
    swav_o = dout("swav_o", [128, 256])
    hstp_o = dout("hstp_o", [12, 128, 128])
    swaks_sh_o = dout("swaks_sh_o", [NS, 127, 256])
    swaksT_o = dout("swaksT_o", [4, 64, NS])
    swavs_sh_o = dout("swavs_sh_o", [NS, 127, 256])
    swavs_new_o = dout("swavs_new_o", [NS, 256])
    hsts_o = dout("hsts_o", [12, 128, NS, 128])

    h1res_d = nc.dram_tensor("h1res", [SEQ + NS, D], F32, kind="Internal")
    gscr_d = nc.dram_tensor("gscr", [24, 384], F32, kind="Internal")
    vscr_d = nc.dram_tensor("vscr", [NS, 1536], F32, kind="Internal")

    es = ExitStack()
    kb = KB(nc, es)

    uid = [0]

    def sb(st, name, shape, dt):
        uid[0] += 1
        return st.enter_context(nc.sbuf_tensor("s%d_%s" % (uid[0], name), list(shape), dt))

    def ps(st, name, shape, dt):
        uid[0] += 1
        return st.enter_context(nc.psum_tensor("p%d_%s" % (uid[0], name), list(shape), dt))

    hT = sb(es, "hT", [128, 16, XW], BF16)
    brT = sb(es, "brT", [128, 16, BW], BF16)
    mkT = sb(es, "mkT", [128, 2, 4, 256], BF16)
    mvb = sb(es, "mvb", [128, 2, 2, 512], BF16)
    ident = sb(es, "ident", [128, 128], BF16)
    ones_b = sb(es, "ones_b", [128, 128], BF16)
    ones_f = sb(es, "ones_f", [128, 128], F32)
    blk_f = sb(es, "blk_f", [128, 128], F32)
    cmask = sb(es, "cmask", [128, 2, 64], F32)
    rmask = sb(es, "rmask", [128, 1024], F32)
    expsink = sb(es, "expsink", [128, 12], F32)
    expb0 = sb(es, "expb0", [128, 12], F32)
    expbcol = sb(es, "expbcol", [128, 24], F32)
    lbT = sb(es, "lbT", [128, 12], F32)
    omlbT = sb(es, "omlbT", [128, 12], F32)
    nwT = sb(es, "nwT", [128, 12], F32)
    epsln = sb(es, "epsln", [128, 1], F32)
    epsrms = sb(es, "epsrms", [128, 1], F32)
    qTs = sb(es, "qTs", [128, 16, NS], BF16)
    sgs = sb(es, "sgs", [128, 16, NS], F32)
    kTs_f = sb(es, "kTs_f", [128, 4, NS], F32)
    vTs_f = sb(es, "vTs_f", [128, 4, NS], F32)
    Sst = sb(es, "Sst", [128, 12, 128], F32)
    Sbf = sb(es, "Sbf", [128, 12, 128], BF16)

    pbank = [ps(es, "pb%d" % i, [128, 512], F32) for i in range(6)]
    pbank_t = [Tok("pb%d" % i, True) for i in range(6)]
    ptr_full = [ps(es, "ptrf%d" % i, [128, 1024], BF16) for i in range(2)]
    ptr = [ptr_full[0][:, 0:512], ptr_full[1][:, 0:512]]
    ptr_t = [Tok("ptr%d" % i, True) for i in range(2)]

    hTt = [Tok("hT%d" % i, True) for i in range(8)]
    hTx = Tok("hTx", True)
    brt = [[Tok("br%d_%d" % (j, g), True) for g in range(3)] for j in range(16)]
    t_const = Tok("const", True)
    t_mk = Tok("mk", True)
    t_small = Tok("small", True)
    t_qTs = Tok("qTs", True)
    t_sgs = Tok("sgs", True)
    t_kTs = Tok("kTs", True)
    t_vTs = Tok("vTs", True)
    t_S = [Tok("S%d" % h, True) for h in range(12)]
    t_gscr = Tok("gscr", True)
    t_h1res = [Tok("h1res%d" % i, True) for i in range(17)]
    t_out = Tok("out", True)

    def hT_toks(c0, c1):
        ts = []
        for t in range(8):
            if c0 < (t + 1) * 128 and c1 > t * 128:
                ts.append(hTt[t])
        if c1 > 1024:
            ts.append(hTx)
        return ts

    with ExitStack() as st:
        cst = sb(st, "cst", [128, 128], F32)
        relb = sb(st, "relb", [32, 24], F32)
        oh = sb(st, "oh", [32, 384], F32)
        vm = sb(st, "vm", [24, 384], F32)
        G = sb(st, "G", [24, 384], F32)
        lbl = sb(st, "lbl", [128, 2, 12], F32)
        memT = sb(st, "memT", [128, 16, 256], BF16)
        wkb = sb(st, "wkb", [128, 16, 512], BF16)
        wvb = sb(st, "wvb", [128, 16, 512], BF16)
        mko = sb(st, "mko", [128, 4, 256], F32)
        mvo = sb(st, "mvo", [128, 2, 512], F32)
        t_relb, t_oh, t_vm, t_G, t_lbl, t_memT = Tok("relb"), Tok("oh"), Tok("vm"), Tok("G"), Tok("lbl"), Tok("memT")
        t_wk, t_wv, t_mko, t_mvo = Tok("wk"), Tok("wv"), Tok("mko"), Tok("mvo")
        t_c1, t_c2, t_c3, t_c4, t_c5 = Tok("c1", True), Tok("c2", True), Tok("c3", True), Tok("c4", True), Tok("c5", True)

        kb.dma("pool", ident[:], c_ident_d.ap()[:, :], W=[t_c1])
        kb.dma("pool", ones_b[:], c_ones_d.ap()[:, :], W=[t_c2])
        kb.dma("sp", ones_f[:], c_ones_d.ap()[:, :], W=[t_c3])
        kb.dma("sp", blk_f[:], c_blk_d.ap()[:, :], W=[t_c4])
        kb.dma("sp", cmask[:], c_cm_d.ap()[:, :, :], W=[t_c5])
        kb.dma("sp", rmask[:], c_rm_d.ap()[:, :], W=[t_c5], sem_tok=t_c5)
        kb.dma("sp", relb[:], relb_d.ap()[:, :], W=[t_relb])
        kb.dma("sp", oh[:], c_oh_d.ap()[:, :], W=[t_oh])
        kb.dma("sp", vm[:], c_vm_d.ap()[:, :], W=[t_vm])
        kb.dma("sp", expsink[:], sinks_d.ap()[:, :], W=[t_small])
        kb.dma("sp", expb0[:], relb0_d.ap()[:, :], W=[t_small], sem_tok=t_small)
        kb.dma("sp", lbl[:], lbl_d.ap()[:, :, :], W=[t_lbl])
        kb.dma("sp", nwT[:], hgnw_d.ap()[:, :], W=[t_small], sem_tok=t_small)
        kb.dma("pool", memT[:], memT_d.ap()[:, :, :], W=[t_memT])

        def _ck(n):
            if SETUP_STOP == n:
                raise _Stop()
        try:
          _ck(1)
          kb.op("dve", lambda e: e.memset(epsln[:], LN_EPS), W=[t_const])
          kb.op("dve", lambda e: e.memset(epsrms[:], RMS_EPS), W=[t_const])
          kb.op("act", lambda e: e.activation(out=expsink[:], in_=expsink[:], func=AF.Exp), R=[t_small], W=[t_small])
          kb.op("act", lambda e: e.activation(out=expb0[:], in_=expb0[:], func=AF.Exp), R=[t_small], W=[t_small])
          kb.op("dve", lambda e: e.tensor_tensor(out=lbT[:], in0=lbl[:, 1, :], in1=lbl[:, 0, :], op=ALU.subtract),
                R=[t_lbl], W=[t_small])
          kb.op("act", lambda e: e.activation(out=lbT[:], in_=lbT[:], func=AF.Sigmoid), R=[t_small], W=[t_small])
          kb.op("dve", lambda e: e.tensor_scalar(out=omlbT[:], in0=lbT[:], scalar1=-1.0, scalar2=1.0,
                                                 op0=ALU.mult, op1=ALU.add), R=[t_small], W=[t_small])
          _ck(2)
          b0 = pbank[0]
          kb.op("pe", lambda e: e.matmul(b0[0:24, 0:384], lhsT=relb[0:32, 0:24], rhs=oh[0:32, 0:384],
                                         start=True, stop=True), R=[t_relb, t_oh], W=[pbank_t[0]])
          kb.op("act", lambda e: e.activation(out=G[:], in_=b0[0:24, 0:384], func=AF.Exp), R=[pbank_t[0]], W=[t_G])
          kb.op("dve", lambda e: e.tensor_tensor(out=G[:], in0=G[:], in1=vm[:], op=ALU.mult), R=[t_G, t_vm], W=[t_G])
          kb.dma("sp", gscr_d.ap()[:, :], G[:], R=[t_G], W=[t_gscr], store=True)

          _ck(3)
          for L in range(2):
              kb.dma("pool", wkb[:], wk_d.ap()[L], W=[t_wk])
              kb.dma("pool", wvb[:], wv_d.ap()[L], W=[t_wv])
              _ck(4)
              for i in range(4):
                  bk = 1 + (i % 2)
                  for kc in range(16):
                      kb.op("pe", lambda e: e.matmul(pbank[bk][:, 0:256], lhsT=wkb[:, kc, i * 128:(i + 1) * 128],
                                                     rhs=memT[:, kc, :], start=(kc == 0), stop=(kc == 15)),
                            R=[t_wk, t_memT, t_c1], W=[pbank_t[bk]])
                  kb.op("act", lambda e: e.activation(out=mkT[:, L, i, :], in_=pbank[bk][:, 0:256], func=AF.Copy),
                        R=[pbank_t[bk]], W=[t_mk])
                  kb.op("dve", lambda e: e.tensor_copy(out=mko[:, i, :], in_=pbank[bk][:, 0:256]),
                        W=[pbank_t[bk], t_mko])
              _ck(5)
              kb.dma("sp", mkT_o.ap()[L].rearrange("i p m -> p i m"), mko[:], R=[t_mko], store=True)
              _ck(6)
              for mc in range(2):
                  bk = 3 + mc
                  for kc in range(16):
                      kb.op("pe", lambda e: e.matmul(pbank[bk][:, :], lhsT=memT[:, kc, mc * 128:(mc + 1) * 128],
                                                     rhs=wvb[:, kc, :], start=(kc == 0), stop=(kc == 15)),
                            R=[t_wv, t_memT], W=[pbank_t[bk]])
                  kb.op("act", lambda e: e.activation(out=mvb[:, L, mc, :], in_=pbank[bk][:, :], func=AF.Copy),
                        R=[pbank_t[bk]], W=[t_mk])
                  kb.op("dve", lambda e: e.tensor_copy(out=mvo[:, mc, :], in_=pbank[bk][:, :]),
                        W=[pbank_t[bk], t_mvo])
              kb.dma("sp", mv_o.ap()[L].rearrange("(mc p) n -> p mc n", p=128), mvo[:], R=[t_mvo],
                     store=True)
        except _Stop:
            pass
        kb.end_scope()

    proj_ring = Ring([0, 1])
    sc_ring = Ring([2, 3])
    PO, PSM = 4, 5

    def load_x(hf):
        kb.dma("pool", hT[:, :, 0:1024], xT_d.ap()[:, :, hf * 1024:(hf + 1) * 1024], W=hTt, sem_tok=hTt[0])
        if hf == 0:
            kb.dma("pool", hT[:, :, 1024:1040], xsT_d.ap()[:, :, :], W=[hTx])
        else:
            kb.dma("pool", hT[:, :, 1024:1152], xT_d.ap()[:, :, 896:1024], W=[hTx])

    def proj_group(wb, t_wb, wcols, c0, c1):
        bk = proj_ring.next()
        toks = hT_toks(c0, c1)
        for kc in range(16):
            kb.op("pe", lambda e: e.matmul(pbank[bk][:, 0:c1 - c0], lhsT=wb[:, kc, wcols[0]:wcols[1]],
                                           rhs=hT[:, kc, c0:c1], start=(kc == 0), stop=(kc == 15)),
                  R=[t_wb] + toks, W=[pbank_t[bk]])
        return bk

    def phase2(L, hf):
        with ExitStack() as st:
            wo = sb(st, "wo", [128, 16, D], BF16)
            lnwb = sb(st, "lnwb", [128, 2, D], F32)
            xr = [sb(st, "xr%d" % i, [128, D], F32) for i in range(2)]
            yb = sb(st, "yb", [128, D], BF16)
            stats = sb(st, "stats", [128, 4, 6], F32)
            mv2 = sb(st, "mv2", [128, 2], F32)
            sd = sb(st, "sd", [128, 1], F32)
            t_wo = [Tok("wo%d" % i) for i in range(4)]
            t_ln = Tok("ln")
            t_xr = [Tok("xr0"), Tok("xr1")]
            t_yb, t_stats, t_mv2, t_sd = Tok("yb"), Tok("stats"), Tok("mv2"), Tok("sd")
            for dg in range(4):
                kb.dma("pool", wo[:, :, dg * 512:(dg + 1) * 512], wo_d.ap()[L][:, :, dg * 512:(dg + 1) * 512],
                       W=[t_wo[dg]])
            kb.dma("sp", lnwb[:, 0, :], bass.AP(lnw_d, L * D, [[0, 128], [1, D]]), W=[t_ln])
            kb.dma("sp", lnwb[:, 1, :], bass.AP(lnb_d, L * D, [[0, 128], [1, D]]), W=[t_ln])
            ring6 = Ring([0, 1, 2, 3, 4, 5])
            tr_ring = Ring([0, 1])
            tiles = list(range(8)) + ([8] if hf == 0 else [])

            def load_res(t):
                M = 128 if t < 8 else NS
                b = xr[t % 2]
                if L == 0:
                    src = xres_d.ap()[hf * 1024 + t * 128: hf * 1024 + t * 128 + 128, :] if t < 8 else xsres_d.ap()[:, :]
                    kb.dma("sp", b[0:M, :], src, W=[t_xr[t % 2]])
                else:
                    r0 = hf * 1024 + t * 128 if t < 8 else SEQ
                    ti = hf * 8 + t if t < 8 else 16
                    kb.dma("sp", b[0:M, :], h1res_d.ap()[r0:r0 + M, :], R=[t_h1res[ti]], W=[t_xr[t % 2]])

            load_res(tiles[0])
            for idx, t in enumerate(tiles):
                M = 128 if t < 8 else NS
                c0 = t * 128
                g = t // 4
                if idx + 1 < len(tiles):
                    load_res(tiles[idx + 1])
                y = xr[t % 2]
                ty = t_xr[t % 2]
                for dg in range(4):
                    bk = ring6.next()
                    for ec in range(16):
                        kb.op("pe", lambda e: e.matmul(pbank[bk][0:M, :], lhsT=brT[:, ec, c0:c0 + M],
                                                       rhs=wo[:, ec, dg * 512:(dg + 1) * 512],
                                                       start=(ec == 0), stop=(ec == 15)),
                              R=[brt[ec][g], t_wo[dg]], W=[pbank_t[bk]])
                    kb.op("dve", lambda e: e.scalar_tensor_tensor(out=y[0:M, dg * 512:(dg + 1) * 512],
                                                                  in0=y[0:M, dg * 512:(dg + 1) * 512], scalar=ALPHA,
                                                                  in1=pbank[bk][0:M, :], op0=ALU.mult, op1=ALU.add),
                          R=[pbank_t[bk]], W=[ty])
                    kb.op("dve", lambda e: e.bn_stats(out=stats[0:M, dg, :], in_=y[0:M, dg * 512:(dg + 1) * 512]),
                          R=[ty], W=[t_stats])
                kb.op("dve", lambda e: e.bn_aggr(out=mv2[0:M, :], in_=stats[0:M, :, :].rearrange("p a b -> p (a b)")),
                      R=[t_stats], W=[t_mv2])
                kb.op("act", lambda e: e.activation(out=sd[0:M, :], in_=mv2[0:M, 1:2], func=AF.Sqrt,
                                                    bias=epsln[0:M, :], scale=1.0), R=[t_mv2, t_const], W=[t_sd])
                kb.op("dve", lambda e: e.reciprocal(out=sd[0:M, :], in_=sd[0:M, :]), R=[t_sd], W=[t_sd])
                kb.op("dve", lambda e: e.scalar_tensor_tensor(out=y[0:M, :], in0=y[0:M, :], scalar=mv2[0:M, 0:1],
                                                              in1=lnwb[0:M, 0, :], op0=ALU.subtract, op1=ALU.mult),
                      R=[t_mv2, t_ln], W=[ty])
                kb.op("dve", lambda e: e.scalar_tensor_tensor(out=y[0:M, :], in0=y[0:M, :], scalar=sd[0:M, 0:1],
                                                              in1=lnwb[0:M, 1, :], op0=ALU.mult, op1=ALU.add),
                      R=[t_sd, t_ln], W=[ty])
                if L == 1 or DEBUG_L0_ONLY:
                    dst = y_d.ap()[hf * 1024 + c0: hf * 1024 + c0 + 128, :] if t < 8 else ys_d.ap()[:, :]
                    kb.dma("sp", dst, y[0:M, :], R=[ty], store=True)
                if L == 0:
                    r0 = hf * 1024 + t * 128 if t < 8 else SEQ
                    ti = hf * 8 + t if t < 8 else 16
                    kb.dma("sp", h1res_d.ap()[r0:r0 + M, :], y[0:M, :], R=[ty], W=[t_h1res[ti]], store=True)
                    kb.op("act", lambda e: e.activation(out=yb[0:M, :], in_=y[0:M, :], func=AF.Copy), R=[ty], W=[t_yb])
                    htok = hTt[t] if t < 8 else hTx
                    for q4 in range(4):
                        pi = tr_ring.next()
                        for c in range(4):
                            dc = q4 * 4 + c
                            kb.op("pe", lambda e: e.transpose(out=ptr[pi][:, c * 128:c * 128 + M],
                                                              in_=yb[0:M, dc * 128:(dc + 1) * 128],
                                                              identity=ident[0:M, 0:M]),
                                  R=[t_yb, t_c1], W=[ptr_t[pi]])
                        kb.op("act", lambda e: e.activation(
                            out=hT[:, q4 * 4:(q4 + 1) * 4, c0:c0 + M],
                            in_=ptr[pi][:, :].rearrange("p (c m) -> p c m", c=4)[:, :, 0:M], func=AF.Copy),
                            R=[ptr_t[pi]], W=[htok])
            kb.end_scope()

    def l0_phase1(hf):
        with ExitStack() as st:
            expbT = sb(st, "expbT", [128, 24, 2, 128], F32)
            wbs = [sb(st, "wb%d" % i, [128, 16, 256], BF16) for i in range(3)]
            kT2 = sb(st, "kT2", [128, 4, 2, XW], BF16)
            vtm = sb(st, "vtm", [128, 9, 256], BF16)
            qTb = [sb(st, "qT%d" % i, [128, BW], BF16) for i in range(2)]
            sgb = [sb(st, "sg%d" % i, [128, BW], F32) for i in range(2)]
            Eb = [sb(st, "E%d" % i, [128, 512], F32) for i in range(2)]
            PTb = [sb(st, "PT%d" % i, [128, 512], BF16) for i in range(2)]
            rsb = sb(st, "rsb", [128, 512], F32)
            tmpb = sb(st, "tmpb", [128, 512], F32)
            kout = sb(st, "kout", [128, 4, 128], F32)
            vout = sb(st, "vout", [128, 256], F32)
            t_expbT = Tok("expbT")
            t_wb = [Tok("wb%d" % i) for i in range(3)]
            t_kT2 = [[Tok("kT2_%d_%d" % (h, g)) for g in range(3)] for h in range(4)]
            t_v = [Tok("v%d" % i) for i in range(9)]
            t_qT = [[Tok("qT%d_%d" % (i, g)) for g in range(3)] for i in range(2)]
            t_sg = [[Tok("sg%d_%d" % (i, g)) for g in range(3)] for i in range(2)]
            t_E = [Tok("E0"), Tok("E1")]
            t_PT = [Tok("PT0"), Tok("PT1")]
            t_rs, t_tmp, t_kout, t_vout = Tok("rs"), Tok("tmp"), Tok("kout"), Tok("vout")
            E_ring, PT_ring = Ring([0, 1]), Ring([0, 1])

            t_kz = Tok("kz")
            kb.op("dve", lambda e: e.memset(kT2[:, :, :, :].rearrange("p a b c -> p (a b c)"), 0.0), W=[t_kz])
            for k in range(128):
                kb.dma("sp", expbT[k:k + 1, :, :, :],
                       bass.AP(gscr_d, 127 - k, [[0, 1], [384, 24], [128, 2], [1, 128]]),
                       R=[t_gscr], W=[t_expbT])
            load_x(hf)
            nslab = 19
            wring = Ring([0, 1, 2])
            slab_buf = {}

            nxt_slab = [0]

            def ensure_slab(upto):
                while nxt_slab[0] <= min(upto, nslab - 1):
                    s = nxt_slab[0]
                    i = wring.next()
                    kb.dma("pool", wbs[i][:], w0_d.ap()[s], W=[t_wb[i]])
                    slab_buf[s] = i
                    nxt_slab[0] += 1

            ensure_slab(1)
            ext = NS if hf == 0 else 128
            kv_groups = [(0, 512), (512, 1024), (1024, 1024 + ext)]
            qg_groups = [(0, 512), (512, 1024)] + ([(1024, 1040)] if hf == 0 else [])

            if hf == 0:
                kb.op("dve", lambda e: e.tensor_copy(out=expbcol[:], in_=expbT[:, :, 1, 0]), R=[t_expbT], W=[t_small])

            for s in range(2):
                ensure_slab(s + 2)
                wi = slab_buf[s]
                for u in range(2):
                    hk = 2 * s + u
                    for g, (c0, c1) in enumerate(kv_groups):
                        bk = proj_group(wbs[wi], t_wb[wi], (u * 128, (u + 1) * 128), c0, c1)
                        kb.op("act", lambda e: e.activation(out=kT2[0:64, hk, 0, c0:c1], in_=pbank[bk][0:64, 0:c1 - c0],
                                                            func=AF.Copy), R=[pbank_t[bk], t_kz], W=[t_kT2[hk][g]])
                        kb.op("act", lambda e: e.activation(out=kT2[64:128, hk, 1, c0:c1], in_=pbank[bk][64:128, 0:c1 - c0],
                                                            func=AF.Copy), R=[pbank_t[bk], t_kz], W=[t_kT2[hk][g]])
                        if hf == 1 and g == 1:
                            kb.op("dve", lambda e: e.tensor_copy(out=kout[:, hk, :], in_=pbank[bk][:, 384:512]),
                                  W=[pbank_t[bk], t_kout])
                        if hf == 0 and g == 2:
                            kb.op("dve", lambda e: e.tensor_copy(out=kTs_f[:, hk, :], in_=pbank[bk][:, 0:NS]),
                                  W=[pbank_t[bk], t_kTs])
            if hf == 1:
                kb.dma("sp", swakT_o.ap().rearrange("h p q -> p h q"), kout[0:64, :, :], R=[t_kout],
                       store=True)
            else:
                kb.dma("sp", swaksT_o.ap().rearrange("h p q -> p h q"), kTs_f[0:64, :, :], R=[t_kTs],
                       store=True)
            if P1_STOP == 1:
                kb.end_scope()
                return
            wi = slab_buf[2]
            for t in range(9):
                M = 128 if (t < 8 or hf == 1) else NS
                bk = proj_ring.next()
                htok = [hTt[t]] if t < 8 else [hTx]
                for kc in range(16):
                    kb.op("pe", lambda e: e.matmul(pbank[bk][0:M, 0:256], lhsT=hT[:, kc, t * 128:t * 128 + M],
                                                   rhs=wbs[wi][:, kc, 0:256], start=(kc == 0), stop=(kc == 15)),
                          R=[t_wb[wi]] + htok, W=[pbank_t[bk]])
                kb.op("act", lambda e: e.activation(out=vtm[0:M, t, :], in_=pbank[bk][0:M, 0:256], func=AF.Copy),
                      R=[pbank_t[bk]], W=[t_v[t]])
                if hf == 1 and t == 7:
                    kb.op("dve", lambda e: e.tensor_copy(out=vout[:, :], in_=pbank[bk][:, 0:256]),
                          W=[pbank_t[bk], t_vout])
                    kb.dma("sp", swav_o.ap()[:, :], vout[:, :], R=[t_vout], store=True)
                if hf == 0 and t == 8:
                    kb.op("dve", lambda e: e.tensor_copy(out=vout[0:NS, :], in_=pbank[bk][0:NS, 0:256]),
                          W=[pbank_t[bk], t_vout])
                    kb.dma("sp", swavs_new_o.ap()[:, :], vout[0:NS, :], R=[t_vout], store=True)
            if hf == 0:
                bk = proj_ring.next()
                for hk in range(4):
                    for x in range(2):
                        for kc in range(16):
                            kb.op("pe", lambda e: e.matmul(pbank[bk][x * 64:(x + 1) * 64, hk * NS:(hk + 1) * NS],
                                                           lhsT=wbs[wi][:, kc, hk * 64:(hk + 1) * 64],
                                                           rhs=hT[:, kc, 1024:1040], start=(kc == 0), stop=(kc == 15)),
                                  R=[t_wb[wi], hTx], W=[pbank_t[bk]])
                kb.op("dve", lambda e: e.tensor_copy(out=vTs_f[:, :, :].rearrange("p a b -> p (a b)"),
                                                     in_=pbank[bk][:, 0:4 * NS]), R=[pbank_t[bk]], W=[t_vTs])

            if P1_STOP == 2:
                kb.end_scope()
                return
            def inproj_j(j):
                s = 3 + j
                ensure_slab(s + 2)
                wi = slab_buf[s]
                b = j % 2
                for g, (c0, c1) in enumerate(qg_groups):
                    bk = proj_group(wbs[wi], t_wb[wi], (0, 128), c0, c1)
                    kb.op("act", lambda e: e.activation(out=qTb[b][:, c0:c1], in_=pbank[bk][:, 0:c1 - c0],
                                                        func=AF.Copy), R=[pbank_t[bk]], W=[t_qT[b][g]])
                    bk2 = proj_group(wbs[wi], t_wb[wi], (128, 256), c0, c1)
                    kb.op("act", lambda e: e.activation(out=sgb[b][:, c0:c1], in_=pbank[bk2][:, 0:c1 - c0],
                                                        func=AF.Silu), R=[pbank_t[bk2]], W=[t_sg[b][g]])
                if hf == 0:
                    kb.op("dve", lambda e: e.tensor_copy(out=qTs[:, j, :], in_=qTb[b][:, 1024:1040]),
                          R=[t_qT[b][2]], W=[t_qTs])
                    kb.op("dve", lambda e: e.tensor_copy(out=sgs[:, j, :], in_=sgb[b][:, 1024:1040]),
                          R=[t_sg[b][2]], W=[t_sgs])

            def finalize(j, b, gi, add_sink):
                c0 = gi * 512
                if add_sink:
                    kb.op("dve", lambda e: e.tensor_scalar(out=rsb[:], in0=pbank[PSM][:, :], scalar1=expsink[:, j:j + 1],
                                                           scalar2=None, op0=ALU.add),
                          R=[pbank_t[PSM], t_small], W=[t_rs])
                    kb.op("dve", lambda e: e.reciprocal(out=rsb[:], in_=rsb[:]), R=[t_rs], W=[t_rs])
                else:
                    kb.op("dve", lambda e: e.reciprocal(out=rsb[:], in_=pbank[PSM][:, :]), R=[pbank_t[PSM]], W=[t_rs])
                kb.op("dve", lambda e: e.tensor_tensor(out=tmpb[:], in0=pbank[PO][:, :], in1=rsb[:], op=ALU.mult),
                      R=[pbank_t[PO], t_rs], W=[t_tmp])
                kb.op("dve", lambda e: e.tensor_tensor(out=brT[:, j, c0:c0 + 512], in0=tmpb[:],
                                                       in1=sgb[b][:, c0:c0 + 512], op=ALU.mult),
                      R=[t_tmp, t_sg[b][gi]], W=[brt[j][gi]])

            def swa_pair(j):
                hk = j // 3
                b = j % 2
                pend = []

                def scores(n):
                    gb = hf * 8 + n
                    cs = [0] if gb == 0 else [0, 1]
                    bk = sc_ring.next()
                    for x in range(2):
                        for c in cs:
                            if c == 0:
                                kc0, ktok = n * 128, t_kT2[hk][n // 4]
                            elif n > 0:
                                kc0, ktok = (n - 1) * 128, t_kT2[hk][(n - 1) // 4]
                            else:
                                kc0, ktok = 1024, t_kT2[hk][2]
                            kb.op("pe", lambda e: e.matmul(
                                pbank[bk][:, (x * 2 + c) * 128:(x * 2 + c + 1) * 128],
                                lhsT=kT2[:, hk, x, kc0:kc0 + 128],
                                rhs=qTb[b][:, n * 128:(n + 1) * 128], start=True, stop=True),
                                R=[ktok, t_qT[b][n // 4]], W=[pbank_t[bk]])
                    ei, pi = E_ring.next(), PT_ring.next()
                    if len(cs) == 2:
                        src = pbank[bk][:, :]
                        eo, po_, tb = Eb[ei][:, :], PTb[pi][:, :], expbT[:, 2 * j:2 * j + 2, :, :].rearrange("p a c q -> p (a c q)")
                    else:
                        v4 = lambda ap: ap.rearrange("p (a c q) -> p a c q", a=2, c=2)[:, :, 0, :]
                        src, eo, po_ = v4(pbank[bk][:, :]), v4(Eb[ei][:, :]), v4(PTb[pi][:, :])
                        tb = expbT[:, 2 * j:2 * j + 2, 0, :]
                    kb.op("act", lambda e: e.activation(out=eo, in_=src, func=AF.Exp, scale=0.125),
                          R=[pbank_t[bk]], W=[t_E[ei]])
                    kb.op("dve", lambda e: e.tensor_tensor(out=po_, in0=eo, in1=tb, op=ALU.mult),
                          R=[t_E[ei], t_expbT], W=[t_PT[pi]])
                    return (n, cs, pi)

                def pv(item):
                    n, cs, pi = item
                    for x in range(2):
                        for ci, c in enumerate(cs):
                            if c == 0:
                                vt = n
                            elif n > 0:
                                vt = n - 1
                            else:
                                vt = 8
                            rhs = PTb[pi][:, (x * 2 + c) * 128:(x * 2 + c + 1) * 128]
                            oc = (n % 4) * 128
                            kb.op("pe", lambda e: e.matmul(pbank[PO][x * 64:(x + 1) * 64, oc:oc + 128],
                                                           lhsT=vtm[:, vt, hk * 64:(hk + 1) * 64], rhs=rhs,
                                                           start=(ci == 0), stop=(ci == len(cs) - 1)),
                                  R=[t_v[vt], t_PT[pi]], W=[pbank_t[PO]])
                            kb.op("pe", lambda e: e.matmul(pbank[PSM][x * 64:(x + 1) * 64, oc:oc + 128],
                                                           lhsT=ones_b[:, 0:64], rhs=rhs,
                                                           start=(ci == 0), stop=(ci == len(cs) - 1)),
                                  R=[t_c2, t_PT[pi]], W=[pbank_t[PSM]])
                    if n % 4 == 3:
                        finalize(j, b, n // 4, True)

                prev = scores(0)
                for n in range(8):
                    nxt = scores(n + 1) if n + 1 < 8 else None
                    pv(prev)
                    prev = nxt

            def mem_head(i, L):
                j = 12 + i
                b = j % 2
                for gi in range(2):
                    pis = []
                    for mc in range(2):
                        bk = sc_ring.next()
                        kb.op("pe", lambda e: e.matmul(pbank[bk][:, :], lhsT=mkT[:, L, i, mc * 128:(mc + 1) * 128],
                                                       rhs=qTb[b][:, gi * 512:(gi + 1) * 512], start=True, stop=True),
                              R=[t_mk, t_qT[b][gi]], W=[pbank_t[bk]])
                        pi = PT_ring.next()
                        kb.op("act", lambda e: e.activation(out=PTb[pi][:, :], in_=pbank[bk][:, :], func=AF.Exp,
                                                            scale=float(128 ** -0.5)), R=[pbank_t[bk]], W=[t_PT[pi]])
                        pis.append(pi)
                    for mc in range(2):
                        pi = pis[mc]
                        kb.op("pe", lambda e: e.matmul(pbank[PO][:, :], lhsT=mvb[:, L, mc, i * 128:(i + 1) * 128],
                                                       rhs=PTb[pi][:, :], start=(mc == 0), stop=(mc == 1)),
                              R=[t_mk, t_PT[pi]], W=[pbank_t[PO]])
                        kb.op("pe", lambda e: e.matmul(pbank[PSM][:, :], lhsT=ones_b[:, :], rhs=PTb[pi][:, :],
                                                       start=(mc == 0), stop=(mc == 1)),
                              R=[t_c2, t_PT[pi]], W=[pbank_t[PSM]])
                    finalize(j, b, gi, False)

            inproj_j(0)
            if P1_STOP == 3:
                kb.end_scope()
                return
            for j in range(16):
                if j + 1 < 16:
                    inproj_j(j + 1)
                if j < 12:
                    swa_pair(j)
                else:
                    mem_head(j - 12, 0)
                if P1_STOP == 4 and j == 0:
                    kb.end_scope()
                    return
                if P1_STOP == 5 and j == 11:
                    kb.end_scope()
                    return
            kb.end_scope()

    def sample_mem(L, st_outer):
        with ExitStack() as st:
            ck = sb(st, "ck", [128, NS, 256], BF16)
            cv = sb(st, "cv", [128, NS, 2, 128], BF16)
            PTs = sb(st, "PTsm", [128, 32], BF16)
            r1 = sb(st, "r1m", [128, NS], F32)
            r2 = sb(st, "r2m", [128, NS], F32)
            t_ck, t_cv, t_PTs, t_r1, t_r2 = Tok("ck"), Tok("cv"), Tok("PTs"), Tok("r1"), Tok("r2")
            for i in range(4):
                kb.dma("pool", ck[:], cmkT_d.ap()[L, i], W=[t_ck])
                kb.dma("pool", cv[:], cmv_d.ap()[L, i], W=[t_cv])
                bk = sc_ring.next()
                for mc in range(2):
                    for bb in range(NS):
                        kb.op("pe", lambda e: e.matmul(pbank[bk][:, mc * NS + bb: mc * NS + bb + 1],
                                                       lhsT=ck[:, bb, mc * 128:(mc + 1) * 128],
                                                       rhs=qTs[:, 12 + i, bb:bb + 1], start=True, stop=True),
                              R=[t_ck, t_qTs], W=[pbank_t[bk]])
                kb.op("act", lambda e: e.activation(out=PTs[:, :], in_=pbank[bk][:, 0:32], func=AF.Exp,
                                                    scale=float(128 ** -0.5)), R=[pbank_t[bk]], W=[t_PTs])
                for bb in range(NS):
                    for mc in range(2):
                        kb.op("pe", lambda e: e.matmul(pbank[PO][:, bb:bb + 1], lhsT=cv[:, bb, mc, :],
                                                       rhs=PTs[:, mc * NS + bb: mc * NS + bb + 1],
                                                       start=(mc == 0), stop=(mc == 1)),
                              R=[t_cv, t_PTs], W=[pbank_t[PO]])
                for mc in range(2):
                    kb.op("pe", lambda e: e.matmul(pbank[PSM][:, 0:NS], lhsT=ones_b[:, :],
                                                   rhs=PTs[:, mc * NS:(mc + 1) * NS], start=(mc == 0), stop=(mc == 1)),
                          R=[t_c2, t_PTs], W=[pbank_t[PSM]])
                kb.op("dve", lambda e: e.reciprocal(out=r1[:], in_=pbank[PSM][:, 0:NS]), R=[pbank_t[PSM]], W=[t_r1])
                kb.op("dve", lambda e: e.tensor_tensor(out=r2[:], in0=pbank[PO][:, 0:NS], in1=r1[:], op=ALU.mult),
                      R=[pbank_t[PO], t_r1], W=[t_r2])
                kb.op("dve", lambda e: e.tensor_tensor(out=brT[:, 12 + i, 1024:1040], in0=r2[:], in1=sgs[:, 12 + i, :],
                                                       op=ALU.mult), R=[t_r2, t_sgs], W=[brt[12 + i][2]])
            kb.end_scope()

    def l0_sample():
        with ExitStack() as st:
            cK = sb(st, "cK", [128, NS, 4, 2, 128], BF16)
            cV = sb(st, "cV", [128, NS, 256], BF16)
            prod = sb(st, "prod", [128, 12, NS], F32)
            pn = sb(st, "pn", [128, 12, NS], F32)
            Es = sb(st, "Es", [128, 32], F32)
            PTs = sb(st, "PTs", [128, 32], BF16)
            a1 = sb(st, "a1", [128, NS], F32)
            a2 = sb(st, "a2", [128, NS], F32)
            t_cK, t_cV, t_prod, t_pn, t_Es, t_PTs, t_a1, t_a2 = (Tok("cK"), Tok("cV"), Tok("prod"), Tok("pn"),
                                                                   Tok("Es"), Tok("PTs"), Tok("a1"), Tok("a2"))
            kb.dma("pool", cK[:], cswaKT_d.ap()[:, :, :, :, :], W=[t_cK])
            kb.dma("pool", cV[:], cswaV_d.ap()[:, :, :], W=[t_cV])
            kb.dma("sp", swaks_sh_o.ap()[:, :, :], cswaKraw_d.ap()[:, 1:128, :], sem_tok=t_out)
            kb.dma("sp", swavs_sh_o.ap()[:, :, :], cswaVraw_d.ap()[:, 1:128, :], sem_tok=t_out)
            for j in range(12):
                kb.op("dve", lambda e: e.tensor_tensor(out=prod[:, j, :], in0=qTs[:, j, :], in1=kTs_f[:, j // 3, :],
                                                       op=ALU.mult), R=[t_qTs, t_kTs], W=[t_prod])
            bk = sc_ring.next()
            kb.op("pe", lambda e: e.matmul(pbank[bk][:, 0:12 * NS], lhsT=blk_f[:, :],
                                           rhs=prod[:, :, :].rearrange("p a b -> p (a b)"), start=True, stop=True),
                  R=[t_prod, t_c4], W=[pbank_t[bk]])
            kb.op("act", lambda e: e.activation(out=pn[:, :, :].rearrange("p a b -> p (a b)"), in_=pbank[bk][:, 0:12 * NS],
                                                func=AF.Exp, scale=0.125), R=[pbank_t[bk]], W=[t_pn])
            for j in range(12):
                kb.op("dve", lambda e: e.tensor_scalar(out=pn[:, j, :], in0=pn[:, j, :], scalar1=expb0[:, j:j + 1],
                                                       scalar2=None, op0=ALU.mult), R=[t_small], W=[t_pn])
            for j in range(12):
                hk = j // 3
                bk = sc_ring.next()
                for x in range(2):
                    for bb in range(NS):
                        kb.op("pe", lambda e: e.matmul(pbank[bk][:, x * NS + bb: x * NS + bb + 1],
                                                       lhsT=cK[:, bb, hk, x, :],
                                                       rhs=qTs[:, j, bb:bb + 1], start=True, stop=True),
                              R=[t_cK, t_qTs], W=[pbank_t[bk]])
                kb.op("act", lambda e: e.activation(out=Es[:, :], in_=pbank[bk][:, 0:32], func=AF.Exp, scale=0.125),
                      R=[pbank_t[bk]], W=[t_Es])
                for x in range(2):
                    kb.op("dve", lambda e: e.tensor_scalar(out=PTs[:, x * NS:(x + 1) * NS], in0=Es[:, x * NS:(x + 1) * NS],
                                                           scalar1=expbcol[:, 2 * j + x: 2 * j + x + 1], scalar2=None,
                                                           op0=ALU.mult), R=[t_Es, t_small], W=[t_PTs])
                for x in range(2):
                    for bb in range(NS):
                        kb.op("pe", lambda e: e.matmul(pbank[PO][x * 64:(x + 1) * 64, bb:bb + 1],
                                                       lhsT=cV[:, bb, hk * 64:(hk + 1) * 64],
                                                       rhs=PTs[:, x * NS + bb: x * NS + bb + 1], start=True, stop=True),
                              R=[t_cV, t_PTs], W=[pbank_t[PO]])
                    kb.op("pe", lambda e: e.matmul(pbank[PSM][x * 64:(x + 1) * 64, 0:NS], lhsT=ones_b[:, 0:64],
                                                   rhs=PTs[:, x * NS:(x + 1) * NS], start=True, stop=True),
                          R=[t_c2, t_PTs], W=[pbank_t[PSM]])
                kb.op("dve", lambda e: e.tensor_tensor(out=a1[:], in0=pn[:, j, :], in1=vTs_f[:, hk, :], op=ALU.mult),
                      R=[t_pn, t_vTs], W=[t_a1])
                kb.op("dve", lambda e: e.tensor_tensor(out=a1[:], in0=a1[:], in1=pbank[PO][:, 0:NS], op=ALU.add),
                      R=[pbank_t[PO]], W=[t_a1])
                kb.op("dve", lambda e: e.scalar_tensor_tensor(out=a2[:], in0=pbank[PSM][:, 0:NS],
                                                              scalar=expsink[:, j:j + 1], in1=pn[:, j, :],
                                                              op0=ALU.add, op1=ALU.add),
                      R=[pbank_t[PSM], t_pn, t_small], W=[t_a2])
                kb.op("dve", lambda e: e.reciprocal(out=a2[:], in_=a2[:]), R=[t_a2], W=[t_a2])
                kb.op("dve", lambda e: e.tensor_tensor(out=a1[:], in0=a1[:], in1=a2[:], op=ALU.mult),
                      R=[t_a2], W=[t_a1])
                kb.op("dve", lambda e: e.tensor_tensor(out=brT[:, j, 1024:1040], in0=a1[:], in1=sgs[:, j, :],
                                                       op=ALU.mult), R=[t_a1, t_sgs], W=[brt[j][2]])
            kb.end_scope()


    def l1_phase1(hf):
        with ExitStack() as st:
            wbs = [sb(st, "wb%d" % i, [128, 16, 256], BF16) for i in range(3)]
            qh = sb(st, "qh", [128, BW], F32)
            fg = sb(st, "fg", [128, BW], F32)
            kh = sb(st, "kh", [128, BW], F32)
            gl = sb(st, "gl", [128, BW], F32)
            gcs = sb(st, "gcs", [128, 1024], F32)
            eg = sb(st, "eg", [128, 1024], F32)
            qt = sb(st, "qt", [128, 1024], BF16)
            kt = sb(st, "kt", [128, 1024], BF16)
            vtm = sb(st, "vtm1", [128, 9, 128], BF16)
            vsf = sb(st, "vsf", [NS, 128], F32)
            sg = sb(st, "sg1", [128, BW], F32)
            qTm = sb(st, "qTm", [128, BW], BF16)
            ktT = sb(st, "ktT", [128, 2, 8, 128], BF16)
            ATall = sb(st, "ATall", [128, 8, 2, 64], BF16)
            Sall_f = sb(st, "Sall_f", [128, 17, 128], F32)
            Sall_b = sb(st, "Sall_b", [128, 16, 128], BF16)
            t_Sf = [Tok("Sf%d" % i) for i in range(17)]
            t_Sb = [Tok("Sb0"), Tok("Sb1")]
            t_ATt = [Tok("ATt%d" % i) for i in range(8)]
            oT = sb(st, "oT", [128, 512], F32)
            osq = sb(st, "osq", [128, 512], F32)
            rstd = sb(st, "rstd", [128, 512], F32)
            PTb = [sb(st, "PTm%d" % i, [128, 512], BF16) for i in range(2)]
            rsb = osq
            tmpb = rstd
            t_wb = [Tok("wb%d" % i) for i in range(3)]
            t_qh, t_fg, t_kh, t_gl, t_gcs, t_eg, t_qt, t_kt = (Tok("qh"), Tok("fg"), Tok("kh"), Tok("gl"),
                                                                 Tok("gcs"), Tok("eg"), Tok("qt"), Tok("kt"))
            t_v = [Tok("v1_%d" % i) for i in range(9)]
            t_vsf, t_sg, t_qTm, t_ktT, t_oT, t_osq, t_rstd = (Tok("vsf"), Tok("sg1"), Tok("qTm"), Tok("ktT"), Tok("oT"),
                                                             Tok("osq"), Tok("rstd"))
            t_AT = [Tok("AT0"), Tok("AT1")]
            t_PT = [Tok("PTm0"), Tok("PTm1")]
            t_rs, t_tmp = t_osq, t_rstd
            t_vscr = Tok("vscr")
            PT_ring, AT_ring, tr_ring = Ring([0, 1]), Ring([0, 1]), Ring([0, 1])
            if hf == 0:
                Sin = sb(st, "Sin", [128, NS, 128], F32)
                vbc = sb(st, "vbc", [128, NS, 128], F32)
                t_Sin, t_vbc = Tok("Sin"), Tok("vbc")
            kb.op("dve", lambda e: e.memset(ktT[:, :, :, :].rearrange("p a b c -> p (a b c)"), 0.0), W=[t_ktT])
            if hf == 0:
                for h in range(12):
                    kb.op("dve", lambda e: e.memset(Sst[:, h, :], 0.0), W=[t_S[h]])
            nslab = 28
            wring = Ring([0, 1, 2])
            slab_buf = {}
            nxt_slab = [0]

            def ensure_slab(upto):
                while nxt_slab[0] <= min(upto, nslab - 1):
                    s_ = nxt_slab[0]
                    i = wring.next()
                    kb.dma("pool", wbs[i][:], w1_d.ap()[s_], W=[t_wb[i]])
                    slab_buf[s_] = i
                    nxt_slab[0] += 1

            ensure_slab(1)
            groups = [(0, 512), (512, 1024)] + ([(1024, 1040)] if hf == 0 else [])
            ncols = BW if hf == 0 else 1024
            SC = float(128 ** -0.5)

            def hgrn_head(h):
                s0, s1 = 2 * h, 2 * h + 1
                ensure_slab(s0 + 2)
                w0i = slab_buf[s0]
                for (c0, c1) in groups:
                    bk = proj_group(wbs[w0i], t_wb[w0i], (0, 128), c0, c1)
                    kb.op("act", lambda e: e.activation(out=qh[:, c0:c1], in_=pbank[bk][:, 0:c1 - c0], func=AF.Silu),
                          R=[pbank_t[bk]], W=[t_qh])
                    bk = proj_group(wbs[w0i], t_wb[w0i], (128, 256), c0, c1)
                    kb.op("act", lambda e: e.activation(out=fg[:, c0:c1], in_=pbank[bk][:, 0:c1 - c0], func=AF.Sigmoid),
                          R=[pbank_t[bk]], W=[t_fg])
                ensure_slab(s1 + 2)
                w1i = slab_buf[s1]
                for (c0, c1) in groups:
                    bk = proj_group(wbs[w1i], t_wb[w1i], (128, 256), c0, c1)
                    kb.op("act", lambda e: e.activation(out=sg[:, c0:c1], in_=pbank[bk][:, 0:c1 - c0], func=AF.Silu),
                          R=[pbank_t[bk]], W=[t_sg])
                ntile = 9 if hf == 0 else 8
                for t in range(ntile):
                    M = 128 if t < 8 else NS
                    bk = proj_ring.next()
                    htok = [hTt[t]] if t < 8 else [hTx]
                    for kc in range(16):
                        kb.op("pe", lambda e: e.matmul(pbank[bk][0:M, 0:128], lhsT=hT[:, kc, t * 128:t * 128 + M],
                                                       rhs=wbs[w1i][:, kc, 0:128], start=(kc == 0), stop=(kc == 15)),
                              R=[t_wb[w1i]] + htok, W=[pbank_t[bk]])
                    kb.op("act", lambda e: e.activation(out=vtm[0:M, t, :], in_=pbank[bk][0:M, 0:128], func=AF.Copy),
                          R=[pbank_t[bk]], W=[t_v[t]])
                    if t == 8:
                        kb.op("dve", lambda e: e.tensor_copy(out=vsf[:, :], in_=pbank[bk][0:NS, 0:128]),
                              W=[pbank_t[bk], t_vsf])
                        kb.dma("sp", vscr_d.ap()[:, h * 128:(h + 1) * 128], vsf[:, :], R=[t_vsf], W=[t_vscr], store=True)
                kb.op("dve", lambda e: e.tensor_scalar(out=fg[:, 0:ncols], in0=fg[:, 0:ncols], scalar1=omlbT[:, h:h + 1],
                                                       scalar2=lbT[:, h:h + 1], op0=ALU.mult, op1=ALU.add),
                      R=[t_small], W=[t_fg])
                kb.op("dve", lambda e: e.tensor_scalar(out=kh[:, 0:ncols], in0=fg[:, 0:ncols], scalar1=-1.0, scalar2=1.0,
                                                       op0=ALU.mult, op1=ALU.add), R=[t_fg], W=[t_kh])
                kb.op("act", lambda e: e.activation(out=gl[:, 0:1024], in_=fg[:, 0:1024], func=AF.Ln), R=[t_fg], W=[t_gl])
                kb.op("dve", lambda e: e.tensor_tensor_scan(out=gcs[:, :], data0=rmask[:, :], data1=gl[:, 0:1024],
                                                            initial=0.0, op0=ALU.mult, op1=ALU.add),
                      R=[t_gl, t_c5], W=[t_gcs])
                kb.op("act", lambda e: e.activation(out=eg[:, :], in_=gcs[:, :], func=AF.Exp), R=[t_gcs], W=[t_eg])
                ek = gl[:, 0:1024]
                kb.op("act", lambda e: e.activation(out=ek, in_=gcs[:, :], func=AF.Exp, scale=-1.0),
                      R=[t_gcs], W=[t_gl])
                kb.op("dve", lambda e: e.tensor_tensor(out=qt[:, :], in0=qh[:, 0:1024], in1=eg[:, :], op=ALU.mult),
                      R=[t_qh, t_eg], W=[t_qt])
                kb.op("dve", lambda e: e.tensor_tensor(out=kt[:, :], in0=kh[:, 0:1024], in1=ek, op=ALU.mult),
                      R=[t_kh, t_gl], W=[t_kt])
                for t in range(8):
                    pi = tr_ring.next()
                    kb.op("pe", lambda e: e.transpose(out=ptr[pi][:, 0:128], in_=kt[:, t * 128:(t + 1) * 128],
                                                      identity=ident[:, :]), R=[t_kt, t_c1], W=[ptr_t[pi]])
                    kb.op("act", lambda e: e.activation(out=ktT[0:64, 0, t, :], in_=ptr[pi][0:64, 0:128], func=AF.Copy),
                          R=[ptr_t[pi]], W=[t_ktT])
                    kb.op("act", lambda e: e.activation(out=ktT[64:128, 1, t, :], in_=ptr[pi][64:128, 0:128], func=AF.Copy),
                          R=[ptr_t[pi]], W=[t_ktT])
                eg3 = eg[:, :].rearrange("p (c s) -> p c s", s=64)
                SCB, UB = 2, (3, PSM)
                t_U = [pbank_t[3], pbank_t[PSM]]
                kb.op("dve", lambda e: e.tensor_copy(out=Sall_f[:, 0, :], in_=Sst[:, h, :]), R=[t_S[h]], W=[t_Sf[0]])

                def u_mm(c):
                    t, u = c // 2, c % 2
                    ub = UB[(c // 4) % 2]
                    kb.op("pe", lambda e: e.matmul(pbank[ub][:, (c % 4) * 128:(c % 4 + 1) * 128], lhsT=ktT[:, u, t, :],
                                                   rhs=vtm[:, t, :], start=True, stop=True),
                          R=[t_ktT, t_v[t]], W=[t_U[(c // 4) % 2]])

                def scan_step(c):
                    ub = UB[(c // 4) % 2]
                    ecol = eg3[:, c, 63:64]
                    kb.op("dve", lambda e: e.tensor_tensor(out=Sall_f[:, c + 1, :], in0=Sall_f[:, c, :],
                                                           in1=pbank[ub][:, (c % 4) * 128:(c % 4 + 1) * 128], op=ALU.add),
                          R=[t_Sf[c], t_U[(c // 4) % 2]], W=[t_Sf[c + 1]])
                    kb.op("dve", lambda e: e.tensor_scalar(out=Sall_f[:, c + 1, :], in0=Sall_f[:, c + 1, :], scalar1=ecol,
                                                           scalar2=None, op0=ALU.mult), R=[t_eg], W=[t_Sf[c + 1]])

                def cast_half(hh):
                    kb.op("act", lambda e: e.activation(
                        out=Sall_b[:, hh * 8:(hh + 1) * 8, :].rearrange("p a b -> p (a b)"),
                        in_=Sall_f[:, hh * 8:(hh + 1) * 8, :].rearrange("p a b -> p (a b)"), func=AF.Copy),
                        R=t_Sf[hh * 8:(hh + 1) * 8], W=[t_Sb[hh]])

                def o_mm(c):
                    t, u = c // 2, c % 2
                    oc = (c % 8) * 64
                    kb.op("pe", lambda e: e.matmul(pbank[PO][:, oc:oc + 64], lhsT=Sall_b[:, c, :],
                                                   rhs=qt[:, c * 64:(c + 1) * 64], start=True, stop=False),
                          R=[t_Sb[c // 8], t_qt], W=[pbank_t[PO]])
                    kb.op("pe", lambda e: e.matmul(pbank[PO][:, oc:oc + 64], lhsT=vtm[:, t, :],
                                                   rhs=ATall[:, t, u, :], start=False, stop=True),
                          R=[t_v[t], t_ATt[t]], W=[pbank_t[PO]])

                for c in range(8):
                    u_mm(c)
                for t in range(8):
                    for u in range(2):
                        c = 2 * t + u
                        kb.op("pe", lambda e: e.matmul(pbank[SCB][u * 64:(u + 1) * 64, t * 64:(t + 1) * 64],
                                                       lhsT=kt[:, c * 64:(c + 1) * 64], rhs=qt[:, c * 64:(c + 1) * 64],
                                                       start=True, stop=True), R=[t_kt, t_qt], W=[pbank_t[SCB]])
                for c in range(8):
                    scan_step(c)
                cast_half(0)
                for c in range(8, 16):
                    u_mm(c)
                for t in range(8):
                    for u in range(2):
                        kb.op("dve", lambda e: e.tensor_tensor(out=ATall[:, t, u, :], in0=pbank[SCB][:, t * 64:(t + 1) * 64],
                                                               in1=cmask[:, u, :], op=ALU.mult),
                              R=[pbank_t[SCB], t_c5], W=[t_ATt[t]])
                for c in range(8, 16):
                    scan_step(c)
                cast_half(1)
                kb.op("dve", lambda e: e.tensor_copy(out=Sst[:, h, :], in_=Sall_f[:, 16, :]), R=[t_Sf[16]], W=[t_S[h]])
                for c in range(8):
                    o_mm(c)
                rms_finish(h, 0, 512, 0)
                for c in range(8, 16):
                    o_mm(c)
                rms_finish(h, 512, 512, 1)

            def rms_finish(h, c0, n, gi):
                kb.op("act", lambda e: e.activation(out=osq[:, 0:n], in_=pbank[PO][:, 0:n], func=AF.Square),
                      R=[pbank_t[PO]], W=[t_osq])
                kb.op("dve", lambda e: e.tensor_copy(out=oT[:, 0:n], in_=pbank[PO][:, 0:n]), W=[pbank_t[PO], t_oT])
                bn = proj_ring.next()
                kb.op("pe", lambda e: e.matmul(pbank[bn][:, 0:n], lhsT=ones_f[:, :], rhs=osq[:, 0:n], start=True, stop=True),
                      R=[t_osq, t_c3], W=[pbank_t[bn]])
                kb.op("act", lambda e: e.activation(out=rstd[:, 0:n], in_=pbank[bn][:, 0:n], func=AF.Sqrt,
                                                    bias=epsrms[:, :], scale=1.0 / 128.0),
                      R=[pbank_t[bn], t_const], W=[t_rstd])
                kb.op("dve", lambda e: e.reciprocal(out=rstd[:, 0:n], in_=rstd[:, 0:n]), R=[t_rstd], W=[t_rstd])
                kb.op("dve", lambda e: e.tensor_tensor(out=oT[:, 0:n], in0=oT[:, 0:n], in1=rstd[:, 0:n], op=ALU.mult),
                      R=[t_rstd], W=[t_oT])
                kb.op("dve", lambda e: e.scalar_tensor_tensor(out=brT[:, h, c0:c0 + n], in0=oT[:, 0:n],
                                                              scalar=nwT[:, h:h + 1], in1=sg[:, c0:c0 + n],
                                                              op0=ALU.mult, op1=ALU.mult),
                      R=[t_oT, t_sg, t_small], W=[brt[h][gi]])

            def hgrn_sample(h):
                kb.dma("sp", Sin[:], hst_d.ap()[h], W=[t_Sin])
                kb.dma("sp", vbc[:], bass.AP(vscr_d, h * 128, [[0, 128], [1536, NS], [1, 128]]), R=[t_vscr], W=[t_vbc])
                for bb in range(NS):
                    kb.op("dve", lambda e: e.tensor_scalar(out=Sin[:, bb, :], in0=Sin[:, bb, :],
                                                           scalar1=fg[:, 1024 + bb:1025 + bb], scalar2=None, op0=ALU.mult),
                          R=[t_fg], W=[t_Sin])
                    kb.op("dve", lambda e: e.scalar_tensor_tensor(out=Sin[:, bb, :], in0=vbc[:, bb, :],
                                                                  scalar=kh[:, 1024 + bb:1025 + bb], in1=Sin[:, bb, :],
                                                                  op0=ALU.mult, op1=ALU.add),
                          R=[t_vbc, t_kh], W=[t_Sin])
                for bb in range(NS):
                    kb.op("pe", lambda e: e.matmul(pbank[PO][:, bb:bb + 1], lhsT=Sin[:, bb, :],
                                                   rhs=qh[:, 1024 + bb:1025 + bb], start=True, stop=True),
                          R=[t_Sin, t_qh], W=[pbank_t[PO]])
                kb.dma("sp", hsts_o.ap()[h], Sin[:], R=[t_Sin], store=True)
                rms_finish(h, 1024, NS, 2)

            def mem_head(i, L):
                j = 12 + i
                s_ = 24 + i
                ensure_slab(s_ + 2)
                wi = slab_buf[s_]
                for (c0, c1) in groups:
                    bk = proj_group(wbs[wi], t_wb[wi], (0, 128), c0, c1)
                    kb.op("act", lambda e: e.activation(out=qTm[:, c0:c1], in_=pbank[bk][:, 0:c1 - c0], func=AF.Copy),
                          R=[pbank_t[bk]], W=[t_qTm])
                    bk = proj_group(wbs[wi], t_wb[wi], (128, 256), c0, c1)
                    kb.op("act", lambda e: e.activation(out=sg[:, c0:c1], in_=pbank[bk][:, 0:c1 - c0], func=AF.Silu),
                          R=[pbank_t[bk]], W=[t_sg])
                if hf == 0:
                    kb.op("dve", lambda e: e.tensor_copy(out=qTs[:, j, :], in_=qTm[:, 1024:1040]), R=[t_qTm], W=[t_qTs])
                    kb.op("dve", lambda e: e.tensor_copy(out=sgs[:, j, :], in_=sg[:, 1024:1040]), R=[t_sg], W=[t_sgs])
                for gi in range(2):
                    pis = []
                    for mc in range(2):
                        bk = sc_ring.next()
                        kb.op("pe", lambda e: e.matmul(pbank[bk][:, :], lhsT=mkT[:, L, i, mc * 128:(mc + 1) * 128],
                                                       rhs=qTm[:, gi * 512:(gi + 1) * 512], start=True, stop=True),
                              R=[t_mk, t_qTm], W=[pbank_t[bk]])
                        pi = PT_ring.next()
                        kb.op("act", lambda e: e.activation(out=PTb[pi][:, :], in_=pbank[bk][:, :], func=AF.Exp, scale=SC),
                              R=[pbank_t[bk]], W=[t_PT[pi]])
                        pis.append(pi)
                    for mc in range(2):
                        pi = pis[mc]
                        kb.op("pe", lambda e: e.matmul(pbank[PO][:, :], lhsT=mvb[:, L, mc, i * 128:(i + 1) * 128],
                                                       rhs=PTb[pi][:, :], start=(mc == 0), stop=(mc == 1)),
                              R=[t_mk, t_PT[pi]], W=[pbank_t[PO]])
                        kb.op("pe", lambda e: e.matmul(pbank[PSM][:, :], lhsT=ones_b[:, :], rhs=PTb[pi][:, :],
                                                       start=(mc == 0), stop=(mc == 1)),
                              R=[t_c2, t_PT[pi]], W=[pbank_t[PSM]])
                    c0 = gi * 512
                    kb.op("dve", lambda e: e.reciprocal(out=rsb[:], in_=pbank[PSM][:, :]), R=[pbank_t[PSM]], W=[t_rs])
                    kb.op("dve", lambda e: e.tensor_tensor(out=tmpb[:], in0=pbank[PO][:, :], in1=rsb[:], op=ALU.mult),
                          R=[pbank_t[PO], t_rs], W=[t_tmp])
                    kb.op("dve", lambda e: e.tensor_tensor(out=brT[:, j, c0:c0 + 512], in0=tmpb[:],
                                                           in1=sg[:, c0:c0 + 512], op=ALU.mult),
                          R=[t_tmp, t_sg], W=[brt[j][gi]])

            for h in range(12):
                hgrn_head(h)
                if hf == 0:
                    hgrn_sample(h)
            for i in range(4):
                mem_head(i, 1)
            if hf == 1:
                kb.dma("sp", hstp_o.ap().rearrange("h p v -> p h v"), Sst[:, :, :], R=t_S, store=True, sem_tok=t_S[0])
            kb.end_scope()

    for hf in range(2):
        if STOP == "setup":
            break
        l0_phase1(hf)
        if STOP == "p1a":
            break
        if hf == 0:
            l0_sample()
            sample_mem(0, None)
        if STOP == "sample":
            break
        phase2(0, hf)
        if STOP == "p2a":
            break
        if DEBUG_L0_ONLY:
            continue
        l1_phase1(hf)
        if hf == 0:
            sample_mem(1, None)
        phase2(1, hf)
    kb.barrier()
    es.close()
    return nc, kb


def arr_kp(w):
    w = np.asarray(w, dtype=np.float32)
    return np.ascontiguousarray(w.reshape(16, 128, -1).transpose(1, 0, 2))


def host_consts():
    c = {}
    c["c_ident"] = np.eye(128, dtype=np.float32)
    c["c_ones"] = np.ones((128, 128), np.float32)
    blk = np.zeros((128, 128), np.float32)
    blk[:64, :64] = 1
    blk[64:, 64:] = 1
    c["c_blk"] = blk
    oh = np.zeros((32, 384), np.float32)
    vm = np.zeros((24, 384), np.float32)
    bk = t5_bucket_np(np.arange(128))
    for d in range(128):
        oh[bk[d], 127 + d] = 1.0
        vm[:, 127 + d] = 1.0
    c["c_oh"] = oh
    c["c_vm"] = vm
    cm = np.zeros((128, 2, 64), np.float32)
    for r in range(128):
        cm[r, r // 64, (r % 64):] = 1.0
    c["c_cm"] = cm
    rm = np.ones((128, 1024), np.float32)
    rm[:, ::64] = 0.0
    c["c_rm"] = rm
    return c


_CACHE = {}


def kernel(x_prompt, x_sample, cache_mem_k, cache_mem_v, cache_swa_k, cache_swa_v, state_hgrn, mem_prompt,
           rel_bias, swa_w_in, swa_sinks, hg_w_in, hg_lb_logits, hg_norm_w, w_mem_k, w_mem_v, w_out, ln_w, ln_b):
    f = lambda a: np.asarray(a, dtype=np.float32)
    x_prompt, x_sample = f(x_prompt), f(x_sample)
    if "nc" not in _CACHE:
        _CACHE["nc"] = build_program()
    nc, kb = _CACHE["nc"]
    consts = host_consts()
    W = f(swa_w_in)[0]
    q, k, v, mq, g = W[:, :1536], W[:, 1536:1792], W[:, 1792:2048], W[:, 2048:2560], W[:, 2560:]
    slabs = []
    for a in (0, 2):
        slabs.append(np.concatenate([k[:, a * 64:(a + 1) * 64]] * 2 + [k[:, (a + 1) * 64:(a + 2) * 64]] * 2, axis=1))
    slabs.append(v)
    for j in range(16):
        A = q[:, j * 128:(j + 1) * 128] if j < 12 else mq[:, (j - 12) * 128:(j - 11) * 128]
        slabs.append(np.concatenate([A, g[:, j * 128:(j + 1) * 128]], axis=1))
    w0 = np.stack([arr_kp(s) for s in slabs])
    W = f(hg_w_in)[0]
    q, fq, iv, mq, g = W[:, :1536], W[:, 1536:3072], W[:, 3072:4608], W[:, 4608:5120], W[:, 5120:]
    slabs = []
    for h in range(12):
        sl = slice(h * 128, (h + 1) * 128)
        slabs.append(np.concatenate([q[:, sl], fq[:, sl]], axis=1))
        slabs.append(np.concatenate([iv[:, sl], g[:, sl]], axis=1))
    for i in range(4):
        slabs.append(np.concatenate([mq[:, i * 128:(i + 1) * 128], g[:, (12 + i) * 128:(13 + i) * 128]], axis=1))
    w1 = np.stack([arr_kp(s) for s in slabs])
    wk = np.stack([arr_kp(f(w_mem_k)[i]) for i in range(2)])
    wv = np.stack([arr_kp(f(w_mem_v)[i]) for i in range(2)])
    wo = np.stack([arr_kp(f(w_out)[i]) for i in range(2)])
    relb = f(rel_bias)
    hd_half = np.arange(128) // 64
    relb0 = np.stack([relb[0, 2 * j + hd_half] for j in range(12)], axis=1)
    sinksc = np.stack([f(swa_sinks)[0, 2 * j + hd_half] for j in range(12)], axis=1)
    lbl = np.ascontiguousarray(f(hg_lb_logits).reshape(2, 12, 128).transpose(2, 0, 1))
    hgnw = np.ascontiguousarray(f(hg_norm_w)[0].reshape(12, 128).T)
    shared = dict(wk=wk, wv=wv, w0=w0, w1=w1, wo=wo, lnw=f(ln_w), lnb=f(ln_b), relb=relb,
                  relb0=np.ascontiguousarray(relb0), sinksc=np.ascontiguousarray(sinksc), lbl=lbl, hgnw=hgnw)
    shared.update(consts)
    cmk, cmv_ = f(cache_mem_k), f(cache_mem_v)
    csk, csv = f(cache_swa_k)[0], f(cache_swa_v)[0]
    sth = f(state_hgrn)[0]
    mem_prompt = f(mem_prompt)
    in_maps = []
    for c in range(NCORES):
        sl = slice(c * NS, (c + 1) * NS)
        m = dict(shared)
        m["xT"] = arr_kp(x_prompt[c].T)
        m["xsT"] = arr_kp(x_sample[sl, 0, :].T)
        m["xres"] = np.ascontiguousarray(x_prompt[c])
        m["xsres"] = np.ascontiguousarray(x_sample[sl, 0, :])
        m["memT"] = arr_kp(mem_prompt[c].T)
        kk = csk[sl]
        kT = kk.transpose(3, 0, 2, 1)
        kTp = np.zeros((128, NS, 4, 2, 128), np.float32)
        kTp[0:64, :, :, 0, :] = kT
        kTp[64:128, :, :, 1, :] = kT
        m["cswaKT"] = kTp
        m["cswaV"] = np.ascontiguousarray(csv[sl].reshape(NS, 128, 256).transpose(1, 0, 2))
        m["cswaKraw"] = np.ascontiguousarray(kk.reshape(NS, 128, 256))
        m["cswaVraw"] = np.ascontiguousarray(csv[sl].reshape(NS, 128, 256))
        m["cmkT"] = np.ascontiguousarray(cmk[:, sl].transpose(0, 3, 4, 1, 2))
        m["cmv"] = np.ascontiguousarray(cmv_[:, sl].reshape(2, NS, 2, 128, 4, 128).transpose(0, 4, 3, 1, 2, 5))
        m["hst"] = np.ascontiguousarray(sth[sl].transpose(1, 2, 0, 3))
        in_maps.append(m)
    if STOP == 'setup':
        for m in in_maps:
            for kname in BIG:
                m[kname] = np.zeros((1, 1), np.float32)
    if DEBUG_NCORES < NCORES:
        res = run_bass_kernel_spmd(nc, in_maps[:DEBUG_NCORES], core_ids=list(range(DEBUG_NCORES)))
        R = list(res.results) + [res.results[0]] * (NCORES - DEBUG_NCORES)
    else:
        res = run_bass_kernel_spmd(nc, in_maps, core_ids=list(range(NCORES)))
        R = res.results
    y_prompt = np.stack([R[c]["y"] for c in range(NCORES)])
    y_sample = np.concatenate([R[c]["ys"] for c in range(NCORES)])[:, None, :]
    mem_k = np.stack([R[c]["mkT_o"].transpose(0, 3, 1, 2) for c in range(NCORES)], axis=1)
    mem_v = np.stack([R[c]["mv_o"].reshape(2, 256, 4, 128) for c in range(NCORES)], axis=1)
    swa_kp = np.stack([R[c]["swakT_o"].transpose(2, 0, 1) for c in range(NCORES)])[None]
    swa_vp = np.stack([R[c]["swav_o"].reshape(128, 4, 64) for c in range(NCORES)])[None]
    hstp = np.stack([R[c]["hstp_o"] for c in range(NCORES)])[None]
    ks = []
    vs = []
    hs = []
    for c in range(NCORES):
        knew = R[c]["swaksT_o"].transpose(2, 0, 1).reshape(NS, 1, 256)
        ks.append(np.concatenate([R[c]["swaks_sh_o"], knew], axis=1).reshape(NS, 128, 4, 64))
        vnew = R[c]["swavs_new_o"].reshape(NS, 1, 256)
        vs.append(np.concatenate([R[c]["swavs_sh_o"], vnew], axis=1).reshape(NS, 128, 4, 64))
        hs.append(R[c]["hsts_o"].transpose(2, 0, 1, 3))
    swa_ks = np.concatenate(ks)[None]
    swa_vs = np.concatenate(vs)[None]
    hsts = np.concatenate(hs)[None]
    outs = (y_prompt, y_sample, mem_k, mem_v, swa_kp, swa_vp, hstp, swa_ks, swa_vs, hsts)
    return tuple(np.ascontiguousarray(o, dtype=np.float32) for o in outs)
```

```python
import numpy as np
from contextlib import ExitStack
import concourse.bass as bass
import concourse.mybir as mybir
from concourse.bass_utils import run_bass_kernel_spmd

F32 = mybir.dt.float32
BF16 = mybir.dt.bfloat16
AF = mybir.ActivationFunctionType
ALU = mybir.AluOpType

NCORES = 8
D = 2048
SEQ = 2048
TT = 1024
NS = 16
XW = 1152
BW = 1040
ALPHA = float((2.0 * 2) ** 0.25)
LN_EPS = 1e-5
RMS_EPS = 1e-6
DEBUG_L0_ONLY = False
STOP = None
DEBUG_NCORES = 8
SETUP_STOP = 99
P1_STOP = 99
BIG = ('xT', 'xres', 'w0', 'w1', 'wo', 'cmkT', 'cmv', 'hst', 'cswaKT', 'cswaV', 'cswaKraw', 'cswaVraw')


class _Stop(Exception):
    pass


class Tok:
    __slots__ = ("name", "w", "r", "dsem", "ssem", "p")

    def __init__(self, name, p=False):
        self.name = name
        self.w = None
        self.r = {}
        self.dsem = None
        self.ssem = None
        self.p = p


class KB:
    ENG = ("pe", "act", "dve", "pool", "sp")

    def __init__(self, nc, es, n_dma_sems=95):
        self.nc = nc
        self.eng = {"pe": nc.tensor, "act": nc.scalar, "dve": nc.vector, "pool": nc.gpsimd, "sp": nc.sync}
        self.sems = []
        self.esem = {}
        for e in self.ENG:
            self.esem[e] = len(self.sems)
            self.sems.append(es.enter_context(nc.semaphore("e_" + e)))
        self.free_dma = []
        for i in range(n_dma_sems):
            self.free_dma.append(len(self.sems))
            self.sems.append(es.enter_context(nc.semaphore("d%d" % i)))
        self.cnt = {e: 0 for e in self.ENG}
        self.known = {e: {} for e in self.ENG}
        self.dcnt = {}
        self.nwait = 0
        self.scoped = []

    def _wait(self, e, deps):
        k = self.known[e]
        pe_own = self.esem["pe"]
        for (s, v) in deps:
            if e == "pe" and s == pe_own:
                continue
            if k.get(s, 0) >= v:
                continue
            self.eng[e].wait_ge(self.sems[s], v)
            self.nwait += 1
            k[s] = v

    @staticmethod
    def _deps(R, W):
        d = []
        for t in R:
            if t.w is not None:
                d.append(t.w)
        for t in W:
            if t.w is not None:
                d.append(t.w)
            d.extend(t.r.items())
        return d

    def op(self, e, fn, R=(), W=()):
        self._wait(e, self._deps(R, W))
        ins = fn(self.eng[e])
        self.cnt[e] += 1
        s = self.esem[e]
        v = self.cnt[e]
        ins.then_inc(self.sems[s], 1)
        for t in R:
            t.r[s] = v
        for t in W:
            t.w = (s, v)
            t.r = {}

    def _dsem(self, tok, store):
        if store:
            if tok.ssem is None:
                tok.ssem = self.free_dma.pop()
                self.dcnt.setdefault(tok.ssem, 0)
                if not tok.p:
                    self.scoped.append(tok)
            return tok.ssem
        if tok.dsem is None:
            tok.dsem = self.free_dma.pop()
            self.dcnt.setdefault(tok.dsem, 0)
            if not tok.p:
                self.scoped.append(tok)
        return tok.dsem

    def end_scope(self):
        self.barrier()
        for t in self.scoped:
            for a in ("dsem", "ssem"):
                v = getattr(t, a)
                if v is not None:
                    self.free_dma.append(v)
                    setattr(t, a, None)
        self.scoped = []

    def dma(self, e, out, in_, R=(), W=(), sem_tok=None, store=False):
        if sem_tok is None:
            sem_tok = R[0] if store else W[0]
        s = self._dsem(sem_tok, store)
        deps = [(a, b) for (a, b) in self._deps(R, W) if a != s]
        self._wait(e, deps)
        ins = self.eng[e].dma_start(out=out, in_=in_)
        self.dcnt[s] += 16
        v = self.dcnt[s]
        ins.then_inc(self.sems[s], 16)
        for t in R:
            t.r[s] = v
        for t in W:
            t.w = (s, v)
            t.r = {}

    def barrier(self, engines=None):
        engines = engines or self.ENG
        for e in engines:
            deps = [(self.esem[x], self.cnt[x]) for x in self.ENG if x != e and self.cnt[x] > 0]
            deps += [(s, v) for s, v in self.dcnt.items() if v > 0]
            self._wait(e, deps)


class Ring:
    def __init__(self, items):
        self.items = items
        self.i = 0

    def next(self):
        it = self.items[self.i % len(self.items)]
        self.i += 1
        return it


def t5_bucket_np(d):
    n = np.maximum(d, 0)
    max_exact = 16
    nf = np.maximum(n, 1).astype(np.float32)
    large = max_exact + (np.log(nf / np.float32(max_exact)) / np.float32(np.log(128 / max_exact))
                         * np.float32(32 - max_exact)).astype(np.int32)
    large = np.minimum(large, 31)
    return np.where(n < max_exact, n, large)


def build_program():
    nc = bass.Bass("TRN2", target_bir_lowering=False)

    def din(name, shape):
        if STOP == "setup" and name in BIG:
            shape = [1, 1]
        return nc.dram_tensor(name, list(shape), F32, kind="ExternalInput")

    def dout(name, shape):
        return nc.dram_tensor(name, list(shape), F32, kind="ExternalOutput")

    xT_d = din("xT", [128, 16, SEQ])
    xsT_d = din("xsT", [128, 16, NS])
    xres_d = din("xres", [SEQ, D])
    xsres_d = din("xsres", [NS, D])
    memT_d = din("memT", [128, 16, 256])
    wk_d = din("wk", [2, 128, 16, 512])
    wv_d = din("wv", [2, 128, 16, 512])
    w0_d = din("w0", [19, 128, 16, 256])
    w1_d = din("w1", [28, 128, 16, 256])
    wo_d = din("wo", [2, 128, 16, D])
    lnw_d = din("lnw", [2, D])
    lnb_d = din("lnb", [2, D])
    relb_d = din("relb", [32, 24])
    relb0_d = din("relb0", [128, 12])
    sinks_d = din("sinksc", [128, 12])
    lbl_d = din("lbl", [128, 2, 12])
    hgnw_d = din("hgnw", [128, 12])
    cswaKT_d = din("cswaKT", [128, NS, 4, 2, 128])
    cswaV_d = din("cswaV", [128, NS, 256])
    cswaKraw_d = din("cswaKraw", [NS, 128, 256])
    cswaVraw_d = din("cswaVraw", [NS, 128, 256])
    cmkT_d = din("cmkT", [2, 4, 128, NS, 256])
    cmv_d = din("cmv", [2, 4, 128, NS, 2, 128])
    hst_d = din("hst", [12, 128, NS, 128])
    c_ident_d = din("c_ident", [128, 128])
    c_ones_d = din("c_ones", [128, 128])
    c_blk_d = din("c_blk", [128, 128])
    c_oh_d = din("c_oh", [32, 384])
    c_vm_d = din("c_vm", [24, 384])
    c_cm_d = din("c_cm", [128, 2, 64])
    c_rm_d = din("c_rm", [128, 1024])

    y_d = dout("y", [SEQ, D])
    ys_d = dout("ys", [NS, D])
    mkT_o = dout("mkT_o", [2, 4, 128, 256])
    mv_o = dout("mv_o", [2, 256, 512])
    swakT_o = dout("swakT_o", [4, 64, 128])
    swav_o = dout("swav_o", [128, 256])
    hstp_o = dout("hstp_o", [12, 128, 128])
    swaks_sh_o = dout("swaks_sh_o", [NS, 127, 256])
    swaksT_o = dout("swaksT_o", [4, 64, NS])
    swavs_sh_o = dout("swavs_sh_o", [NS, 127, 256])
    swavs_new_o = dout("swavs_new_o", [NS, 256])
    hsts_o = dout("hsts_o", [12, 128, NS, 128])

    h1res_d = nc.dram_tensor("h1res", [SEQ + NS, D], F32, kind="Internal")
    gscr_d = nc.dram_tensor("gscr", [24, 384], F32, kind="Internal")
    vscr_d = nc.dram_tensor("vscr", [NS, 1536], F32, kind="Internal")

    es = ExitStack()
    kb = KB(nc, es)

    uid = [0]

    def sb(st, name, shape, dt):
        uid[0] += 1
        return st.enter_context(nc.sbuf_tensor("s%d_%s" % (uid[0], name), list(shape), dt))

    def ps(st, name, shape, dt):
        uid[0] += 1
        return st.enter_context(nc.psum_tensor("p%d_%s" % (uid[0], name), list(shape), dt))

    hT = sb(es, "hT", [128, 16, XW], BF16)
    brT = sb(es, "brT", [128, 16, BW], BF16)
    mkT = sb(es, "mkT", [128, 2, 4, 256], BF16)
    mvb = sb(es, "mvb", [128, 2, 2, 512], BF16)
    ident = sb(es, "ident", [128, 128], BF16)
    ones_b = sb(es, "ones_b", [128, 128], BF16)
    ones_f = sb(es, "ones_f", [128, 128], F32)
    blk_f = sb(es, "blk_f", [128, 128], F32)
    cmask = sb(es, "cmask", [128, 2, 64], F32)
    rmask = sb(es, "rmask", [128, 1024], F32)
    expsink = sb(es, "expsink", [128, 12], F32)
    expb0 = sb(es, "expb0", [128, 12], F32)
    expbcol = sb(es, "expbcol", [128, 24], F32)
    lbT = sb(es, "lbT", [128, 12], F32)
    omlbT = sb(es, "omlbT", [128, 12], F32)
    nwT = sb(es, "nwT", [128, 12], F32)
    epsln = sb(es, "epsln", [128, 1], F32)
    epsrms = sb(es, "epsrms", [128, 1], F32)
    qTs = sb(es, "qTs", [128, 16, NS], BF16)
    sgs = sb(es, "sgs", [128, 16, NS], F32)
    kTs_f = sb(es, "kTs_f", [128, 4, NS], F32)
    vTs_f = sb(es, "vTs_f", [128, 4, NS], F32)
    Sst = sb(es, "Sst", [128, 12, 128], F32)

    pbank = [ps(es, "pb%d" % i, [128, 512], F32) for i in range(6)]
    pbank_t = [Tok("pb%d" % i, True) for i in range(6)]
    ptr_full = [ps(es, "ptrf%d" % i, [128, 1024], BF16) for i in range(2)]
    ptr = [ptr_full[0][:, 0:512], ptr_full[1][:, 0:512]]
    ptr_t = [Tok("ptr%d" % i, True) for i in range(2)]

    hTt = [Tok("hT%d" % i, True) for i in range(8)]
    hTx = Tok("hTx", True)
    brt = [[Tok("br%d_%d" % (j, g), True) for g in range(3)] for j in range(16)]
    t_const = Tok("const", True)
    t_mk = Tok("mk", True)
    t_small = Tok("small", True)
    t_qTs = Tok("qTs", True)
    t_sgs = Tok("sgs", True)
    t_kTs = Tok("kTs", True)
    t_vTs = Tok("vTs", True)
    t_S = [Tok("S%d" % h, True) for h in range(12)]
    t_gscr = Tok("gscr", True)
    t_h1res = [Tok("h1res%d" % i, True) for i in range(17)]
    t_out = Tok("out", True)

    def hT_toks(c0, c1):
        ts = []
        for t in range(8):
            if c0 < (t + 1) * 128 and c1 > t * 128:
                ts.append(hTt[t])
        if c1 > 1024:
            ts.append(hTx)
        return ts

    with ExitStack() as st:
        cst = sb(st, "cst", [128, 128], F32)
        relb = sb(st, "relb", [32, 24], F32)
        oh = sb(st, "oh", [32, 384], F32)
        vm = sb(st, "vm", [24, 384], F32)
        G = sb(st, "G", [24, 384], F32)
        lbl = sb(st, "lbl", [128, 2, 12], F32)
        memT = sb(st, "memT", [128, 16, 256], BF16)
        wkb = sb(st, "wkb", [128, 16, 512], BF16)
        wvb = sb(st, "wvb", [128, 16, 512], BF16)
        mko = sb(st, "mko", [128, 4, 256], F32)
        mvo = sb(st, "mvo", [128, 2, 512], F32)
        t_relb, t_oh, t_vm, t_G, t_lbl, t_memT = Tok("relb"), Tok("oh"), Tok("vm"), Tok("G"), Tok("lbl"), Tok("memT")
        t_wk, t_wv, t_mko, t_mvo = Tok("wk"), Tok("wv"), Tok("mko"), Tok("mvo")
        t_c1, t_c2, t_c3, t_c4, t_c5 = Tok("c1", True), Tok("c2", True), Tok("c3", True), Tok("c4", True), Tok("c5", True)

        kb.dma("pool", ident[:], c_ident_d.ap()[:, :], W=[t_c1])
        kb.dma("pool", ones_b[:], c_ones_d.ap()[:, :], W=[t_c2])
        kb.dma("sp", ones_f[:], c_ones_d.ap()[:, :], W=[t_c3])
        kb.dma("sp", blk_f[:], c_blk_d.ap()[:, :], W=[t_c4])
        kb.dma("sp", cmask[:], c_cm_d.ap()[:, :, :], W=[t_c5])
        kb.dma("sp", rmask[:], c_rm_d.ap()[:, :], W=[t_c5], sem_tok=t_c5)
        kb.dma("sp", relb[:], relb_d.ap()[:, :], W=[t_relb])
        kb.dma("sp", oh[:], c_oh_d.ap()[:, :], W=[t_oh])
        kb.dma("sp", vm[:], c_vm_d.ap()[:, :], W=[t_vm])
        kb.dma("sp", expsink[:], sinks_d.ap()[:, :], W=[t_small])
        kb.dma("sp", expb0[:], relb0_d.ap()[:, :], W=[t_small], sem_tok=t_small)
        kb.dma("sp", lbl[:], lbl_d.ap()[:, :, :], W=[t_lbl])
        kb.dma("sp", nwT[:], hgnw_d.ap()[:, :], W=[t_small], sem_tok=t_small)
        kb.dma("pool", memT[:], memT_d.ap()[:, :, :], W=[t_memT])

        def _ck(n):
            if SETUP_STOP == n:
                raise _Stop()
        try:
          _ck(1)
          kb.op("dve", lambda e: e.memset(epsln[:], LN_EPS), W=[t_const])
          kb.op("dve", lambda e: e.memset(epsrms[:], RMS_EPS), W=[t_const])
          kb.op("act", lambda e: e.activation(out=expsink[:], in_=expsink[:], func=AF.Exp), R=[t_small], W=[t_small])
          kb.op("act", lambda e: e.activation(out=expb0[:], in_=expb0[:], func=AF.Exp), R=[t_small], W=[t_small])
          kb.op("dve", lambda e: e.tensor_tensor(out=lbT[:], in0=lbl[:, 1, :], in1=lbl[:, 0, :], op=ALU.subtract),
                R=[t_lbl], W=[t_small])
          kb.op("act", lambda e: e.activation(out=lbT[:], in_=lbT[:], func=AF.Sigmoid), R=[t_small], W=[t_small])
          kb.op("dve", lambda e: e.tensor_scalar(out=omlbT[:], in0=lbT[:], scalar1=-1.0, scalar2=1.0,
                                                 op0=ALU.mult, op1=ALU.add), R=[t_small], W=[t_small])
          _ck(2)
          b0 = pbank[0]
          kb.op("pe", lambda e: e.matmul(b0[0:24, 0:384], lhsT=relb[0:32, 0:24], rhs=oh[0:32, 0:384],
                                         start=True, stop=True), R=[t_relb, t_oh], W=[pbank_t[0]])
          kb.op("act", lambda e: e.activation(out=G[:], in_=b0[0:24, 0:384], func=AF.Exp), R=[pbank_t[0]], W=[t_G])
          kb.op("dve", lambda e: e.tensor_tensor(out=G[:], in0=G[:], in1=vm[:], op=ALU.mult), R=[t_G, t_vm], W=[t_G])
          kb.dma("sp", gscr_d.ap()[:, :], G[:], R=[t_G], W=[t_gscr], store=True)

          _ck(3)
          for L in range(2):
              kb.dma("pool", wkb[:], wk_d.ap()[L], W=[t_wk])
              kb.dma("pool", wvb[:], wv_d.ap()[L], W=[t_wv])
              _ck(4)
              for i in range(4):
                  bk = 1 + (i % 2)
                  for kc in range(16):
                      kb.op("pe", lambda e: e.matmul(pbank[bk][:, 0:256], lhsT=wkb[:, kc, i * 128:(i + 1) * 128],
                                                     rhs=memT[:, kc, :], start=(kc == 0), stop=(kc == 15)),
                            R=[t_wk, t_memT, t_c1], W=[pbank_t[bk]])
                  kb.op("act", lambda e: e.activation(out=mkT[:, L, i, :], in_=pbank[bk][:, 0:256], func=AF.Copy),
                        R=[pbank_t[bk]], W=[t_mk])
                  kb.op("dve", lambda e: e.tensor_copy(out=mko[:, i, :], in_=pbank[bk][:, 0:256]),
                        W=[pbank_t[bk], t_mko])
              _ck(5)
              kb.dma("sp", mkT_o.ap()[L].rearrange("i p m -> p i m"), mko[:], R=[t_mko], store=True)
              _ck(6)
              for mc in range(2):
                  bk = 3 + mc
                  for kc in range(16):
                      kb.op("pe", lambda e: e.matmul(pbank[bk][:, :], lhsT=memT[:, kc, mc * 128:(mc + 1) * 128],
                                                     rhs=wvb[:, kc, :], start=(kc == 0), stop=(kc == 15)),
                            R=[t_wv, t_memT], W=[pbank_t[bk]])
                  kb.op("act", lambda e: e.activation(out=mvb[:, L, mc, :], in_=pbank[bk][:, :], func=AF.Copy),
                        R=[pbank_t[bk]], W=[t_mk])
                  kb.op("dve", lambda e: e.tensor_copy(out=mvo[:, mc, :], in_=pbank[bk][:, :]),
                        W=[pbank_t[bk], t_mvo])
              kb.dma("sp", mv_o.ap()[L].rearrange("(mc p) n -> p mc n", p=128), mvo[:], R=[t_mvo],
                     store=True)
        except _Stop:
            pass
        kb.end_scope()

    proj_ring = Ring([0, 1])
    sc_ring = Ring([2, 3])
    PO, PSM = 4, 5

    def load_x(hf):
        kb.dma("pool", hT[:, :, 0:1024], xT_d.ap()[:, :, hf * 1024:(hf + 1) * 1024], W=hTt, sem_tok=hTt[0])
        if hf == 0:
            kb.dma("pool", hT[:, :, 1024:1040], xsT_d.ap()[:, :, :], W=[hTx])
        else:
            kb.dma("pool", hT[:, :, 1024:1152], xT_d.ap()[:, :, 896:1024], W=[hTx])

    def proj_group(wb, t_wb, wcols, c0, c1):
        bk = proj_ring.next()
        toks = hT_toks(c0, c1)
        for kc in range(16):
            kb.op("pe", lambda e: e.matmul(pbank[bk][:, 0:c1 - c0], lhsT=wb[:, kc, wcols[0]:wcols[1]],
                                           rhs=hT[:, kc, c0:c1], start=(kc == 0), stop=(kc == 15)),
                  R=[t_wb] + toks, W=[pbank_t[bk]])
        return bk

    def phase2(L, hf):
        with ExitStack() as st:
            wo = sb(st, "wo", [128, 16, D], BF16)
            lnwb = sb(st, "lnwb", [128, 2, D], F32)
            xr = [sb(st, "xr%d" % i, [128, D], F32) for i in range(2)]
            yb = sb(st, "yb", [128, D], BF16)
            stats = sb(st, "stats", [128, 4, 6], F32)
            mv2 = sb(st, "mv2", [128, 2], F32)
            sd = sb(st, "sd", [128, 1], F32)
            t_wo = [Tok("wo%d" % i) for i in range(4)]
            t_ln = Tok("ln")
            t_xr = [Tok("xr0"), Tok("xr1")]
            t_yb, t_stats, t_mv2, t_sd = Tok("yb"), Tok("stats"), Tok("mv2"), Tok("sd")
            for dg in range(4):
                kb.dma("pool", wo[:, :, dg * 512:(dg + 1) * 512], wo_d.ap()[L][:, :, dg * 512:(dg + 1) * 512],
                       W=[t_wo[dg]])
            kb.dma("sp", lnwb[:, 0, :], bass.AP(lnw_d, L * D, [[0, 128], [1, D]]), W=[t_ln])
            kb.dma("sp", lnwb[:, 1, :], bass.AP(lnb_d, L * D, [[0, 128], [1, D]]), W=[t_ln])
            ring6 = Ring([0, 1, 2, 3, 4, 5])
            tr_ring = Ring([0, 1])
            tiles = list(range(8)) + ([8] if hf == 0 else [])

            def load_res(t):
                M = 128 if t < 8 else NS
                b = xr[t % 2]
                if L == 0:
                    src = xres_d.ap()[hf * 1024 + t * 128: hf * 1024 + t * 128 + 128, :] if t < 8 else xsres_d.ap()[:, :]
                    kb.dma("sp", b[0:M, :], src, W=[t_xr[t % 2]])
                else:
                    r0 = hf * 1024 + t * 128 if t < 8 else SEQ
                    ti = hf * 8 + t if t < 8 else 16
                    kb.dma("sp", b[0:M, :], h1res_d.ap()[r0:r0 + M, :], R=[t_h1res[ti]], W=[t_xr[t % 2]])

            load_res(tiles[0])
            for idx, t in enumerate(tiles):
                M = 128 if t < 8 else NS
                c0 = t * 128
                g = t // 4
                if idx + 1 < len(tiles):
                    load_res(tiles[idx + 1])
                y = xr[t % 2]
                ty = t_xr[t % 2]
                for dg in range(4):
                    bk = ring6.next()
                    for ec in range(16):
                        kb.op("pe", lambda e: e.matmul(pbank[bk][0:M, :], lhsT=brT[:, ec, c0:c0 + M],
                                                       rhs=wo[:, ec, dg * 512:(dg + 1) * 512],
                                                       start=(ec == 0), stop=(ec == 15)),
                              R=[brt[ec][g], t_wo[dg]], W=[pbank_t[bk]])
                    kb.op("dve", lambda e: e.scalar_tensor_tensor(out=y[0:M, dg * 512:(dg + 1) * 512],
                                                                  in0=y[0:M, dg * 512:(dg + 1) * 512], scalar=ALPHA,
                                                                  in1=pbank[bk][0:M, :], op0=ALU.mult, op1=ALU.add),
                          R=[pbank_t[bk]], W=[ty])
                    kb.op("dve", lambda e: e.bn_stats(out=stats[0:M, dg, :], in_=y[0:M, dg * 512:(dg + 1) * 512]),
                          R=[ty], W=[t_stats])
                kb.op("dve", lambda e: e.bn_aggr(out=mv2[0:M, :], in_=stats[0:M, :, :].rearrange("p a b -> p (a b)")),
                      R=[t_stats], W=[t_mv2])
                kb.op("act", lambda e: e.activation(out=sd[0:M, :], in_=mv2[0:M, 1:2], func=AF.Sqrt,
                                                    bias=epsln[0:M, :], scale=1.0), R=[t_mv2, t_const], W=[t_sd])
                kb.op("dve", lambda e: e.reciprocal(out=sd[0:M, :], in_=sd[0:M, :]), R=[t_sd], W=[t_sd])
                kb.op("dve", lambda e: e.scalar_tensor_tensor(out=y[0:M, :], in0=y[0:M, :], scalar=mv2[0:M, 0:1],
                                                              in1=lnwb[0:M, 0, :], op0=ALU.subtract, op1=ALU.mult),
                      R=[t_mv2, t_ln], W=[ty])
                kb.op("dve", lambda e: e.scalar_tensor_tensor(out=y[0:M, :], in0=y[0:M, :], scalar=sd[0:M, 0:1],
                                                              in1=lnwb[0:M, 1, :], op0=ALU.mult, op1=ALU.add),
                      R=[t_sd, t_ln], W=[ty])
                if L == 1 or DEBUG_L0_ONLY:
                    dst = y_d.ap()[hf * 1024 + c0: hf * 1024 + c0 + 128, :] if t < 8 else ys_d.ap()[:, :]
                    kb.dma("sp", dst, y[0:M, :], R=[ty], store=True)
                if L == 0:
                    r0 = hf * 1024 + t * 128 if t < 8 else SEQ
                    ti = hf * 8 + t if t < 8 else 16
                    kb.dma("sp", h1res_d.ap()[r0:r0 + M, :], y[0:M, :], R=[ty], W=[t_h1res[ti]], store=True)
                    kb.op("act", lambda e: e.activation(out=yb[0:M, :], in_=y[0:M, :], func=AF.Copy), R=[ty], W=[t_yb])
                    htok = hTt[t] if t < 8 else hTx
                    for q4 in range(4):
                        pi = tr_ring.next()
                        for c in range(4):
                            dc = q4 * 4 + c
                            kb.op("pe", lambda e: e.transpose(out=ptr[pi][:, c * 128:c * 128 + M],
                                                              in_=yb[0:M, dc * 128:(dc + 1) * 128],
                                                              identity=ident[0:M, 0:M]),
                                  R=[t_yb, t_c1], W=[ptr_t[pi]])
                        kb.op("act", lambda e: e.activation(
                            out=hT[:, q4 * 4:(q4 + 1) * 4, c0:c0 + M],
                            in_=ptr[pi][:, :].rearrange("p (c m) -> p c m", c=4)[:, :, 0:M], func=AF.Copy),
                            R=[ptr_t[pi]], W=[htok])
            kb.end_scope()

    def l0_phase1(hf):
        with ExitStack() as st:
            expbT = sb(st, "expbT", [128, 24, 2, 128], F32)
            wbs = [sb(st, "wb%d" % i, [128, 16, 256], BF16) for i in range(3)]
            kT2 = sb(st, "kT2", [128, 4, 2, XW], BF16)
            vtm = sb(st, "vtm", [128, 9, 256], BF16)
            qTb = [sb(st, "qT%d" % i, [128, BW], BF16) for i in range(2)]
            sgb = [sb(st, "sg%d" % i, [128, BW], F32) for i in range(2)]
            Eb = [sb(st, "E%d" % i, [128, 512], F32) for i in range(2)]
            PTb = [sb(st, "PT%d" % i, [128, 512], BF16) for i in range(2)]
            rsb = sb(st, "rsb", [128, 512], F32)
            tmpb = sb(st, "tmpb", [128, 512], F32)
            kout = sb(st, "kout", [128, 4, 128], F32)
            vout = sb(st, "vout", [128, 256], F32)
            t_expbT = Tok("expbT")
            t_wb = [Tok("wb%d" % i) for i in range(3)]
            t_kT2 = [[Tok("kT2_%d_%d" % (h, g)) for g in range(3)] for h in range(4)]
            t_v = [Tok("v%d" % i) for i in range(9)]
            t_qT = [[Tok("qT%d_%d" % (i, g)) for g in range(3)] for i in range(2)]
            t_sg = [[Tok("sg%d_%d" % (i, g)) for g in range(3)] for i in range(2)]
            t_E = [Tok("E0"), Tok("E1")]
            t_PT = [Tok("PT0"), Tok("PT1")]
            t_rs, t_tmp, t_kout, t_vout = Tok("rs"), Tok("tmp"), Tok("kout"), Tok("vout")
            E_ring, PT_ring = Ring([0, 1]), Ring([0, 1])

            t_kz = Tok("kz")
            kb.op("dve", lambda e: e.memset(kT2[:, :, :, :].rearrange("p a b c -> p (a b c)"), 0.0), W=[t_kz])
            for k in range(128):
                kb.dma("sp", expbT[k:k + 1, :, :, :],
                       bass.AP(gscr_d, 127 - k, [[0, 1], [384, 24], [128, 2], [1, 128]]),
                       R=[t_gscr], W=[t_expbT])
            load_x(hf)
            nslab = 19
            wring = Ring([0, 1, 2])
            slab_buf = {}

            nxt_slab = [0]

            def ensure_slab(upto):
                while nxt_slab[0] <= min(upto, nslab - 1):
                    s = nxt_slab[0]
                    i = wring.next()
                    kb.dma("pool", wbs[i][:], w0_d.ap()[s], W=[t_wb[i]])
                    slab_buf[s] = i
                    nxt_slab[0] += 1

            ensure_slab(1)
            ext = NS if hf == 0 else 128
            kv_groups = [(0, 512), (512, 1024), (1024, 1024 + ext)]
            qg_groups = [(0, 512), (512, 1024)] + ([(1024, 1040)] if hf == 0 else [])

            if hf == 0:
                kb.op("dve", lambda e: e.tensor_copy(out=expbcol[:], in_=expbT[:, :, 1, 0]), R=[t_expbT], W=[t_small])

            for s in range(2):
                ensure_slab(s + 2)
                wi = slab_buf[s]
                for u in range(2):
                    hk = 2 * s + u
                    for g, (c0, c1) in enumerate(kv_groups):
                        bk = proj_group(wbs[wi], t_wb[wi], (u * 128, (u + 1) * 128), c0, c1)
                        kb.op("act", lambda e: e.activation(out=kT2[0:64, hk, 0, c0:c1], in_=pbank[bk][0:64, 0:c1 - c0],
                                                            func=AF.Copy), R=[pbank_t[bk], t_kz], W=[t_kT2[hk][g]])
                        kb.op("act", lambda e: e.activation(out=kT2[64:128, hk, 1, c0:c1], in_=pbank[bk][64:128, 0:c1 - c0],
                                                            func=AF.Copy), R=[pbank_t[bk], t_kz], W=[t_kT2[hk][g]])
                        if hf == 1 and g == 1:
                            kb.op("dve", lambda e: e.tensor_copy(out=kout[:, hk, :], in_=pbank[bk][:, 384:512]),
                                  W=[pbank_t[bk], t_kout])
                        if hf == 0 and g == 2:
                            kb.op("dve", lambda e: e.tensor_copy(out=kTs_f[:, hk, :], in_=pbank[bk][:, 0:NS]),
                                  W=[pbank_t[bk], t_kTs])
            if hf == 1:
                kb.dma("sp", swakT_o.ap().rearrange("h p q -> p h q"), kout[0:64, :, :], R=[t_kout],
                       store=True)
            else:
                kb.dma("sp", swaksT_o.ap().rearrange("h p q -> p h q"), kTs_f[0:64, :, :], R=[t_kTs],
                       store=True)
            if P1_STOP == 1:
                kb.end_scope()
                return
            wi = slab_buf[2]
            for t in range(9):
                M = 128 if (t < 8 or hf == 1) else NS
                bk = proj_ring.next()
                htok = [hTt[t]] if t < 8 else [hTx]
                for kc in range(16):
                    kb.op("pe", lambda e: e.matmul(pbank[bk][0:M, 0:256], lhsT=hT[:, kc, t * 128:t * 128 + M],
                                                   rhs=wbs[wi][:, kc, 0:256], start=(kc == 0), stop=(kc == 15)),
                          R=[t_wb[wi]] + htok, W=[pbank_t[bk]])
                kb.op("act", lambda e: e.activation(out=vtm[0:M, t, :], in_=pbank[bk][0:M, 0:256], func=AF.Copy),
                      R=[pbank_t[bk]], W=[t_v[t]])
                if hf == 1 and t == 7:
                    kb.op("dve", lambda e: e.tensor_copy(out=vout[:, :], in_=pbank[bk][:, 0:256]),
                          W=[pbank_t[bk], t_vout])
                    kb.dma("sp", swav_o.ap()[:, :], vout[:, :], R=[t_vout], store=True)
                if hf == 0 and t == 8:
                    kb.op("dve", lambda e: e.tensor_copy(out=vout[0:NS, :], in_=pbank[bk][0:NS, 0:256]),
                          W=[pbank_t[bk], t_vout])
                    kb.dma("sp", swavs_new_o.ap()[:, :], vout[0:NS, :], R=[t_vout], store=True)
            if hf == 0:
                bk = proj_ring.next()
                for hk in range(4):
                    for x in range(2):
                        for kc in range(16):
                            kb.op("pe", lambda e: e.matmul(pbank[bk][x * 64:(x + 1) * 64, hk * NS:(hk + 1) * NS],
                                                           lhsT=wbs[wi][:, kc, hk * 64:(hk + 1) * 64],
                                                           rhs=hT[:, kc, 1024:1040], start=(kc == 0), stop=(kc == 15)),
                                  R=[t_wb[wi], hTx], W=[pbank_t[bk]])
                kb.op("dve", lambda e: e.tensor_copy(out=vTs_f[:, :, :].rearrange("p a b -> p (a b)"),
                                                     in_=pbank[bk][:, 0:4 * NS]), R=[pbank_t[bk]], W=[t_vTs])

            if P1_STOP == 2:
                kb.end_scope()
                return
            def inproj_j(j):
                s = 3 + j
                ensure_slab(s + 2)
                wi = slab_buf[s]
                b = j % 2
                for g, (c0, c1) in enumerate(qg_groups):
                    bk = proj_group(wbs[wi], t_wb[wi], (0, 128), c0, c1)
                    kb.op("act", lambda e: e.activation(out=qTb[b][:, c0:c1], in_=pbank[bk][:, 0:c1 - c0],
                                                        func=AF.Copy), R=[pbank_t[bk]], W=[t_qT[b][g]])
                    bk2 = proj_group(wbs[wi], t_wb[wi], (128, 256), c0, c1)
                    kb.op("act", lambda e: e.activation(out=sgb[b][:, c0:c1], in_=pbank[bk2][:, 0:c1 - c0],
                                                        func=AF.Silu), R=[pbank_t[bk2]], W=[t_sg[b][g]])
                if hf == 0:
                    kb.op("dve", lambda e: e.tensor_copy(out=qTs[:, j, :], in_=qTb[b][:, 1024:1040]),
                          R=[t_qT[b][2]], W=[t_qTs])
                    kb.op("dve", lambda e: e.tensor_copy(out=sgs[:, j, :], in_=sgb[b][:, 1024:1040]),
                          R=[t_sg[b][2]], W=[t_sgs])

            def finalize(j, b, gi, add_sink):
                c0 = gi * 512
                if add_sink:
                    kb.op("dve", lambda e: e.tensor_scalar(out=rsb[:], in0=pbank[PSM][:, :], scalar1=expsink[:, j:j + 1],
                                                           scalar2=None, op0=ALU.add),
                          R=[pbank_t[PSM], t_small], W=[t_rs])
                    kb.op("dve", lambda e: e.reciprocal(out=rsb[:], in_=rsb[:]), R=[t_rs], W=[t_rs])
                else:
                    kb.op("dve", lambda e: e.reciprocal(out=rsb[:], in_=pbank[PSM][:, :]), R=[pbank_t[PSM]], W=[t_rs])
                kb.op("dve", lambda e: e.tensor_tensor(out=tmpb[:], in0=pbank[PO][:, :], in1=rsb[:], op=ALU.mult),
                      R=[pbank_t[PO], t_rs], W=[t_tmp])
                kb.op("dve", lambda e: e.tensor_tensor(out=brT[:, j, c0:c0 + 512], in0=tmpb[:],
                                                       in1=sgb[b][:, c0:c0 + 512], op=ALU.mult),
                      R=[t_tmp, t_sg[b][gi]], W=[brt[j][gi]])

            def swa_pair(j):
                hk = j // 3
                b = j % 2
                pend = []

                def scores(n):
                    gb = hf * 8 + n
                    cs = [0] if gb == 0 else [0, 1]
                    bk = sc_ring.next()
                    for x in range(2):
                        for c in cs:
                            if c == 0:
                                kc0, ktok = n * 128, t_kT2[hk][n // 4]
                            elif n > 0:
                                kc0, ktok = (n - 1) * 128, t_kT2[hk][(n - 1) // 4]
                            else:
                                kc0, ktok = 1024, t_kT2[hk][2]
                            kb.op("pe", lambda e: e.matmul(
                                pbank[bk][:, (x * 2 + c) * 128:(x * 2 + c + 1) * 128],
                                lhsT=kT2[:, hk, x, kc0:kc0 + 128],
                                rhs=qTb[b][:, n * 128:(n + 1) * 128], start=True, stop=True),
                                R=[ktok, t_qT[b][n // 4]], W=[pbank_t[bk]])
                    ei, pi = E_ring.next(), PT_ring.next()
                    if len(cs) == 2:
                        src = pbank[bk][:, :]
                        eo, po_, tb = Eb[ei][:, :], PTb[pi][:, :], expbT[:, 2 * j:2 * j + 2, :, :].rearrange("p a c q -> p (a c q)")
                    else:
                        v4 = lambda ap: ap.rearrange("p (a c q) -> p a c q", a=2, c=2)[:, :, 0, :]
                        src, eo, po_ = v4(pbank[bk][:, :]), v4(Eb[ei][:, :]), v4(PTb[pi][:, :])
                        tb = expbT[:, 2 * j:2 * j + 2, 0, :]
                    kb.op("act", lambda e: e.activation(out=eo, in_=src, func=AF.Exp, scale=0.125),
                          R=[pbank_t[bk]], W=[t_E[ei]])
                    kb.op("dve", lambda e: e.tensor_tensor(out=po_, in0=eo, in1=tb, op=ALU.mult),
                          R=[t_E[ei], t_expbT], W=[t_PT[pi]])
                    return (n, cs, pi)

                def pv(item):
                    n, cs, pi = item
                    for x in range(2):
                        for ci, c in enumerate(cs):
                            if c == 0:
                                vt = n
                            elif n > 0:
                                vt = n - 1
                            else:
                                vt = 8
                            rhs = PTb[pi][:, (x * 2 + c) * 128:(x * 2 + c + 1) * 128]
                            oc = (n % 4) * 128
                            kb.op("pe", lambda e: e.matmul(pbank[PO][x * 64:(x + 1) * 64, oc:oc + 128],
                                                           lhsT=vtm[:, vt, hk * 64:(hk + 1) * 64], rhs=rhs,
                                                           start=(ci == 0), stop=(ci == len(cs) - 1)),
                                  R=[t_v[vt], t_PT[pi]], W=[pbank_t[PO]])
                            kb.op("pe", lambda e: e.matmul(pbank[PSM][x * 64:(x + 1) * 64, oc:oc + 128],
                                                           lhsT=ones_b[:, 0:64], rhs=rhs,
                                                           start=(ci == 0), stop=(ci == len(cs) - 1)),
                                  R=[t_c2, t_PT[pi]], W=[pbank_t[PSM]])
                    if n % 4 == 3:
                        finalize(j, b, n // 4, True)

                prev = scores(0)
                for n in range(8):
                    nxt = scores(n + 1) if n + 1 < 8 else None
                    pv(prev)
                    prev = nxt

            def mem_head(i, L):
                j = 12 + i
                b = j % 2
                for gi in range(2):
                    pis = []
                    for mc in range(2):
                        bk = sc_ring.next()
                        kb.op("pe", lambda e: e.matmul(pbank[bk][:, :], lhsT=mkT[:, L, i, mc * 128:(mc + 1) * 128],
                                                       rhs=qTb[b][:, gi * 512:(gi + 1) * 512], start=True, stop=True),
                              R=[t_mk, t_qT[b][gi]], W=[pbank_t[bk]])
                        pi = PT_ring.next()
                        kb.op("act", lambda e: e.activation(out=PTb[pi][:, :], in_=pbank[bk][:, :], func=AF.Exp,
                                                            scale=float(128 ** -0.5)), R=[pbank_t[bk]], W=[t_PT[pi]])
                        pis.append(pi)
                    for mc in range(2):
                        pi = pis[mc]
                        kb.op("pe", lambda e: e.matmul(pbank[PO][:, :], lhsT=mvb[:, L, mc, i * 128:(i + 1) * 128],
                                                       rhs=PTb[pi][:, :], start=(mc == 0), stop=(mc == 1)),
                              R=[t_mk, t_PT[pi]], W=[pbank_t[PO]])
                        kb.op("pe", lambda e: e.matmul(pbank[PSM][:, :], lhsT=ones_b[:, :], rhs=PTb[pi][:, :],
                                                       start=(mc == 0), stop=(mc == 1)),
                              R=[t_c2, t_PT[pi]], W=[pbank_t[PSM]])
                    finalize(j, b, gi, False)

            inproj_j(0)
            if P1_STOP == 3:
                kb.end_scope()
                return
            for j in range(16):
                if j + 1 < 16:
                    inproj_j(j + 1)
                if j < 12:
                    swa_pair(j)
                else:
                    mem_head(j - 12, 0)
                if P1_STOP == 4 and j == 0:
                    kb.end_scope()
                    return
                if P1_STOP == 5 and j == 11:
                    kb.end_scope()
                    return
            kb.end_scope()

    def sample_mem(L, st_outer):
        with ExitStack() as st:
            ck = sb(st, "ck", [128, NS, 256], BF16)
            cv = sb(st, "cv", [128, NS, 2, 128], BF16)
            PTs = sb(st, "PTsm", [128, 32], BF16)
            r1 = sb(st, "r1m", [128, NS], F32)
            r2 = sb(st, "r2m", [128, NS], F32)
            t_ck, t_cv, t_PTs, t_r1, t_r2 = Tok("ck"), Tok("cv"), Tok("PTs"), Tok("r1"), Tok("r2")
            for i in range(4):
                kb.dma("pool", ck[:], cmkT_d.ap()[L, i], W=[t_ck])
                kb.dma("pool", cv[:], cmv_d.ap()[L, i], W=[t_cv])
                bk = sc_ring.next()
                for mc in range(2):
                    for bb in range(NS):
                        kb.op("pe", lambda e: e.matmul(pbank[bk][:, mc * NS + bb: mc * NS + bb + 1],
                                                       lhsT=ck[:, bb, mc * 128:(mc + 1) * 128],
                                                       rhs=qTs[:, 12 + i, bb:bb + 1], start=True, stop=True),
                              R=[t_ck, t_qTs], W=[pbank_t[bk]])
                kb.op("act", lambda e: e.activation(out=PTs[:, :], in_=pbank[bk][:, 0:32], func=AF.Exp,
                                                    scale=float(128 ** -0.5)), R=[pbank_t[bk]], W=[t_PTs])
                for bb in range(NS):
                    for mc in range(2):
                        kb.op("pe", lambda e: e.matmul(pbank[PO][:, bb:bb + 1], lhsT=cv[:, bb, mc, :],
                                                       rhs=PTs[:, mc * NS + bb: mc * NS + bb + 1],
                                                       start=(mc == 0), stop=(mc == 1)),
                              R=[t_cv, t_PTs], W=[pbank_t[PO]])
                for mc in range(2):
                    kb.op("pe", lambda e: e.matmul(pbank[PSM][:, 0:NS], lhsT=ones_b[:, :],
                                                   rhs=PTs[:, mc * NS:(mc + 1) * NS], start=(mc == 0), stop=(mc == 1)),
                          R=[t_c2, t_PTs], W=[pbank_t[PSM]])
                kb.op("dve", lambda e: e.reciprocal(out=r1[:], in_=pbank[PSM][:, 0:NS]), R=[pbank_t[PSM]], W=[t_r1])
                kb.op("dve", lambda e: e.tensor_tensor(out=r2[:], in0=pbank[PO][:, 0:NS], in1=r1[:], op=ALU.mult),
                      R=[pbank_t[PO], t_r1], W=[t_r2])
                kb.op("dve", lambda e: e.tensor_tensor(out=brT[:, 12 + i, 1024:1040], in0=r2[:], in1=sgs[:, 12 + i, :],
                                                       op=ALU.mult), R=[t_r2, t_sgs], W=[brt[12 + i][2]])
            kb.end_scope()

    def l0_sample():
        with ExitStack() as st:
            cK = sb(st, "cK", [128, NS, 4, 2, 128], BF16)
            cV = sb(st, "cV", [128, NS, 256], BF16)
            prod = sb(st, "prod", [128, 12, NS], F32)
            pn = sb(st, "pn", [128, 12, NS], F32)
            Es = sb(st, "Es", [128, 32], F32)
            PTs = sb(st, "PTs", [128, 32], BF16)
            a1 = sb(st, "a1", [128, NS], F32)
            a2 = sb(st, "a2", [128, NS], F32)
            t_cK, t_cV, t_prod, t_pn, t_Es, t_PTs, t_a1, t_a2 = (Tok("cK"), Tok("cV"), Tok("prod"), Tok("pn"),
                                                                   Tok("Es"), Tok("PTs"), Tok("a1"), Tok("a2"))
            kb.dma("pool", cK[:], cswaKT_d.ap()[:, :, :, :, :], W=[t_cK])
            kb.dma("pool", cV[:], cswaV_d.ap()[:, :, :], W=[t_cV])
            kb.dma("sp", swaks_sh_o.ap()[:, :, :], cswaKraw_d.ap()[:, 1:128, :], sem_tok=t_out)
            kb.dma("sp", swavs_sh_o.ap()[:, :, :], cswaVraw_d.ap()[:, 1:128, :], sem_tok=t_out)
            for j in range(12):
                kb.op("dve", lambda e: e.tensor_tensor(out=prod[:, j, :], in0=qTs[:, j, :], in1=kTs_f[:, j // 3, :],
                                                       op=ALU.mult), R=[t_qTs, t_kTs], W=[t_prod])
            bk = sc_ring.next()
            kb.op("pe", lambda e: e.matmul(pbank[bk][:, 0:12 * NS], lhsT=blk_f[:, :],
                                           rhs=prod[:, :, :].rearrange("p a b -> p (a b)"), start=True, stop=True),
                  R=[t_prod, t_c4], W=[pbank_t[bk]])
            kb.op("act", lambda e: e.activation(out=pn[:, :, :].rearrange("p a b -> p (a b)"), in_=pbank[bk][:, 0:12 * NS],
                                                func=AF.Exp, scale=0.125), R=[pbank_t[bk]], W=[t_pn])
            for j in range(12):
                kb.op("dve", lambda e: e.tensor_scalar(out=pn[:, j, :], in0=pn[:, j, :], scalar1=expb0[:, j:j + 1],
                                                       scalar2=None, op0=ALU.mult), R=[t_small], W=[t_pn])
            for j in range(12):
                hk = j // 3
                bk = sc_ring.next()
                for x in range(2):
                    for bb in range(NS):
                        kb.op("pe", lambda e: e.matmul(pbank[bk][:, x * NS + bb: x * NS + bb + 1],
                                                       lhsT=cK[:, bb, hk, x, :],
                                                       rhs=qTs[:, j, bb:bb + 1], start=True, stop=True),
                              R=[t_cK, t_qTs], W=[pbank_t[bk]])
                kb.op("act", lambda e: e.activation(out=Es[:, :], in_=pbank[bk][:, 0:32], func=AF.Exp, scale=0.125),
                      R=[pbank_t[bk]], W=[t_Es])
                for x in range(2):
                    kb.op("dve", lambda e: e.tensor_scalar(out=PTs[:, x * NS:(x + 1) * NS], in0=Es[:, x * NS:(x + 1) * NS],
                                                           scalar1=expbcol[:, 2 * j + x: 2 * j + x + 1], scalar2=None,
                                                           op0=ALU.mult), R=[t_Es, t_small], W=[t_PTs])
                for x in range(2):
                    for bb in range(NS):
                        kb.op("pe", lambda e: e.matmul(pbank[PO][x * 64:(x + 1) * 64, bb:bb + 1],
                                                       lhsT=cV[:, bb, hk * 64:(hk + 1) * 64],
                                                       rhs=PTs[:, x * NS + bb: x * NS + bb + 1], start=True, stop=True),
                              R=[t_cV, t_PTs], W=[pbank_t[PO]])
                    kb.op("pe", lambda e: e.matmul(pbank[PSM][x * 64:(x + 1) * 64, 0:NS], lhsT=ones_b[:, 0:64],
                                                   rhs=PTs[:, x * NS:(x + 1) * NS], start=True, stop=True),
                          R=[t_c2, t_PTs], W=[pbank_t[PSM]])
                kb.op("dve", lambda e: e.tensor_tensor(out=a1[:], in0=pn[:, j, :], in1=vTs_f[:, hk, :], op=ALU.mult),
                      R=[t_pn, t_vTs], W=[t_a1])
                kb.op("dve", lambda e: e.tensor_tensor(out=a1[:], in0=a1[:], in1=pbank[PO][:, 0:NS], op=ALU.add),
                      R=[pbank_t[PO]], W=[t_a1])
                kb.op("dve", lambda e: e.scalar_tensor_tensor(out=a2[:], in0=pbank[PSM][:, 0:NS],
                                                              scalar=expsink[:, j:j + 1], in1=pn[:, j, :],
                                                              op0=ALU.add, op1=ALU.add),
                      R=[pbank_t[PSM], t_pn, t_small], W=[t_a2])
                kb.op("dve", lambda e: e.reciprocal(out=a2[:], in_=a2[:]), R=[t_a2], W=[t_a2])
                kb.op("dve", lambda e: e.tensor_tensor(out=a1[:], in0=a1[:], in1=a2[:], op=ALU.mult),
                      R=[t_a2], W=[t_a1])
                kb.op("dve", lambda e: e.tensor_tensor(out=brT[:, j, 1024:1040], in0=a1[:], in1=sgs[:, j, :],
                                                       op=ALU.mult), R=[t_a1, t_sgs], W=[brt[j][2]])
            kb.end_scope()


    def l1_phase1(hf):
        with ExitStack() as st:
            wbs = [sb(st, "wb%d" % i, [128, 16, 256], BF16) for i in range(3)]
            qh = sb(st, "qh", [128, BW], F32)
            fg = sb(st, "fg", [128, BW], F32)
            kh = sb(st, "kh", [128, BW], F32)
            gl = sb(st, "gl", [128, BW], F32)
            gcs = sb(st, "gcs", [128, 1024], F32)
            eg = sb(st, "eg", [128, 1024], F32)
            qt = sb(st, "qt", [128, 1024], BF16)
            kt = sb(st, "kt", [128, 1024], BF16)
            vtm = sb(st, "vtm1", [128, 9, 128], BF16)
            vsf = sb(st, "vsf", [NS, 128], F32)
            sg = sb(st, "sg1", [128, BW], F32)
            ktT = sb(st, "ktT", [128, 2, 8, 128], BF16)
            ATall = sb(st, "ATall", [128, 8, 2, 64], BF16)
            Sall_f = sb(st, "Sall_f", [128, 17, 128], F32)
            Sall_b = sb(st, "Sall_b", [128, 16, 128], BF16)
            t_Sf = [Tok("Sf%d" % i) for i in range(17)]
            t_Sb = [Tok("Sb0"), Tok("Sb1")]
            t_ATt = [Tok("ATt%d" % i) for i in range(8)]
            qTm = Sall_b[:, :, :].rearrange("p a b -> p (a b)")
            oT = sb(st, "oT", [128, 512], F32)
            osq = sb(st, "osq", [128, 512], F32)
            rstd = sb(st, "rstd", [128, 512], F32)
            PTb = [sb(st, "PTm%d" % i, [128, 512], BF16) for i in range(2)]
            rsb = osq
            tmpb = rstd
            t_wb = [Tok("wb%d" % i) for i in range(3)]
            t_qh, t_fg, t_kh, t_gl, t_gcs, t_eg, t_qt, t_kt = (Tok("qh"), Tok("fg"), Tok("kh"), Tok("gl"),
                                                                 Tok("gcs"), Tok("eg"), Tok("qt"), Tok("kt"))
            t_v = [Tok("v1_%d" % i) for i in range(9)]
            t_vsf, t_sg, t_qTm, t_ktT, t_oT, t_osq, t_rstd = (Tok("vsf"), Tok("sg1"), Tok("qTm"), Tok("ktT"), Tok("oT"),
                                                             Tok("osq"), Tok("rstd"))
            t_AT = [Tok("AT0"), Tok("AT1")]
            t_PT = [Tok("PTm0"), Tok("PTm1")]
            t_rs, t_tmp = t_osq, t_rstd
            t_vscr = Tok("vscr")
            PT_ring, AT_ring, tr_ring = Ring([0, 1]), Ring([0, 1]), Ring([0, 1])
            if hf == 0:
                Sin = sb(st, "Sin", [128, NS, 128], F32)
                vbc = sb(st, "vbc", [128, NS, 128], F32)
                t_Sin, t_vbc = Tok("Sin"), Tok("vbc")
            kb.op("dve", lambda e: e.memset(ktT[:, :, :, :].rearrange("p a b c -> p (a b c)"), 0.0), W=[t_ktT])
            if hf == 0:
                for h in range(12):
                    kb.op("dve", lambda e: e.memset(Sst[:, h, :], 0.0), W=[t_S[h]])
            nslab = 28
            wring = Ring([0, 1, 2])
            slab_buf = {}
            nxt_slab = [0]

            def ensure_slab(upto):
                while nxt_slab[0] <= min(upto, nslab - 1):
                    s_ = nxt_slab[0]
                    i = wring.next()
                    kb.dma("pool", wbs[i][:], w1_d.ap()[s_], W=[t_wb[i]])
                    slab_buf[s_] = i
                    nxt_slab[0] += 1

            ensure_slab(1)
            groups = [(0, 512), (512, 1024)] + ([(1024, 1040)] if hf == 0 else [])
            ncols = BW if hf == 0 else 1024
            SC = float(128 ** -0.5)

            qh_b = sb(st, "qh_b", [128, BW], F32)
            fg_b = sb(st, "fg_b", [128, BW], F32)
            sg_b = sb(st, "sg_b", [128, BW], F32)
            vtm_b = sb(st, "vtm_b", [128, 9, 128], BF16)
            QH, FG, SG, VT = [qh, qh_b], [fg, fg_b], [sg, sg_b], [vtm, vtm_b]
            T_QH, T_FG, T_SG = [t_qh, Tok("qh_b")], [t_fg, Tok("fg_b")], [t_sg, Tok("sg_b")]
            T_V = [t_v, [Tok("v1b_%d" % i) for i in range(9)]]

            def proj_q(h, sx):
                ensure_slab(2 * h + 1)
                wi = slab_buf[2 * h]
                for (c0, c1) in groups:
                    bk = proj_group(wbs[wi], t_wb[wi], (0, 128), c0, c1)
                    kb.op("act", lambda e: e.activation(out=QH[sx][:, c0:c1], in_=pbank[bk][:, 0:c1 - c0], func=AF.Silu),
                          R=[pbank_t[bk]], W=[T_QH[sx]])

            def proj_f(h, sx):
                wi = slab_buf[2 * h]
                for (c0, c1) in groups:
                    bk = proj_group(wbs[wi], t_wb[wi], (128, 256), c0, c1)
                    kb.op("act", lambda e: e.activation(out=FG[sx][:, c0:c1], in_=pbank[bk][:, 0:c1 - c0], func=AF.Exp,
                                                        scale=-1.0), R=[pbank_t[bk]], W=[T_FG[sx]])
                    kb.op("dve", lambda e: e.tensor_scalar(out=FG[sx][:, c0:c1], in0=FG[sx][:, c0:c1], scalar1=1.0,
                                                           scalar2=None, op0=ALU.add), R=[], W=[T_FG[sx]])
                    kb.op("dve", lambda e: e.reciprocal(out=FG[sx][:, c0:c1], in_=FG[sx][:, c0:c1]), R=[], W=[T_FG[sx]])

            def proj_g(h, sx):
                ensure_slab(2 * h + 2)
                wi = slab_buf[2 * h + 1]
                for (c0, c1) in groups:
                    bk = proj_group(wbs[wi], t_wb[wi], (128, 256), c0, c1)
                    kb.op("act", lambda e: e.activation(out=SG[sx][:, c0:c1], in_=pbank[bk][:, 0:c1 - c0], func=AF.Silu),
                          R=[pbank_t[bk]], W=[T_SG[sx]])

            def proj_v(h, sx):
                w1i = slab_buf[2 * h + 1]
                ntile = 9 if hf == 0 else 8
                for t in range(ntile):
                    M = 128 if t < 8 else NS
                    bk = proj_ring.next()
                    htok = [hTt[t]] if t < 8 else [hTx]
                    for kc in range(16):
                        kb.op("pe", lambda e: e.matmul(pbank[bk][0:M, 0:128], lhsT=hT[:, kc, t * 128:t * 128 + M],
                                                       rhs=wbs[w1i][:, kc, 0:128], start=(kc == 0), stop=(kc == 15)),
                              R=[t_wb[w1i]] + htok, W=[pbank_t[bk]])
                    kb.op("act", lambda e: e.activation(out=VT[sx][0:M, t, :], in_=pbank[bk][0:M, 0:128], func=AF.Copy),
                          R=[pbank_t[bk]], W=[T_V[sx][t]])
                    if t == 8:
                        kb.op("dve", lambda e: e.tensor_copy(out=vsf[:, :], in_=pbank[bk][0:NS, 0:128]),
                              W=[pbank_t[bk], t_vsf])
                        kb.dma("sp", vscr_d.ap()[:, h * 128:(h + 1) * 128], vsf[:, :], R=[t_vsf], W=[t_vscr], store=True)
                ensure_slab(2 * h + 3)

            def hgrn_head(h, sx, nxt):
                qh, fg, sg, vtm = QH[sx], FG[sx], SG[sx], VT[sx]
                t_qh, t_fg, t_sg, t_v = T_QH[sx], T_FG[sx], T_SG[sx], T_V[sx]
                kb.op("dve", lambda e: e.tensor_scalar(out=fg[:, 0:ncols], in0=fg[:, 0:ncols], scalar1=omlbT[:, h:h + 1],
                                                       scalar2=lbT[:, h:h + 1], op0=ALU.mult, op1=ALU.add),
                      R=[t_small], W=[t_fg])
                kb.op("dve", lambda e: e.tensor_scalar(out=kh[:, 0:ncols], in0=fg[:, 0:ncols], scalar1=-1.0, scalar2=1.0,
                                                       op0=ALU.mult, op1=ALU.add), R=[t_fg], W=[t_kh])
                kb.op("act", lambda e: e.activation(out=gl[:, 0:1024], in_=fg[:, 0:1024], func=AF.Ln), R=[t_fg], W=[t_gl])
                kb.op("dve", lambda e: e.tensor_tensor_scan(out=gcs[:, :], data0=rmask[:, :], data1=gl[:, 0:1024],
                                                            initial=0.0, op0=ALU.mult, op1=ALU.add),
                      R=[t_gl, t_c5], W=[t_gcs])
                kb.op("act", lambda e: e.activation(out=eg[:, :], in_=gcs[:, :], func=AF.Exp), R=[t_gcs], W=[t_eg])
                ek = gl[:, 0:1024]
                kb.op("act", lambda e: e.activation(out=ek, in_=gcs[:, :], func=AF.Exp, scale=-1.0),
                      R=[t_gcs], W=[t_gl])
                kb.op("dve", lambda e: e.tensor_tensor(out=qt[:, :], in0=qh[:, 0:1024], in1=eg[:, :], op=ALU.mult),
                      R=[t_qh, t_eg], W=[t_qt])
                kb.op("dve", lambda e: e.tensor_tensor(out=kt[:, :], in0=kh[:, 0:1024], in1=ek, op=ALU.mult),
                      R=[t_kh, t_gl], W=[t_kt])
                if nxt is not None:
                    proj_q(nxt, 1 - sx)
                for t in range(8):
                    pi = tr_ring.next()
                    kb.op("pe", lambda e: e.transpose(out=ptr[pi][:, 0:128], in_=kt[:, t * 128:(t + 1) * 128],
                                                      identity=ident[:, :]), R=[t_kt, t_c1], W=[ptr_t[pi]])
                    kb.op("act", lambda e: e.activation(out=ktT[0:64, 0, t, :], in_=ptr[pi][0:64, 0:128], func=AF.Copy),
                          R=[ptr_t[pi]], W=[t_ktT])
                    kb.op("act", lambda e: e.activation(out=ktT[64:128, 1, t, :], in_=ptr[pi][64:128, 0:128], func=AF.Copy),
                          R=[ptr_t[pi]], W=[t_ktT])
                eg3 = eg[:, :].rearrange("p (c s) -> p c s", s=64)
                SCB, UB = 2, (3, PSM)
                t_U = [pbank_t[3], pbank_t[PSM]]
                kb.op("dve", lambda e: e.tensor_copy(out=Sall_f[:, 0, :], in_=Sst[:, h, :]), R=[t_S[h]], W=[t_Sf[0]])

                def u_mm(c):
                    t, u = c // 2, c % 2
                    ub = UB[(c // 4) % 2]
                    kb.op("pe", lambda e: e.matmul(pbank[ub][:, (c % 4) * 128:(c % 4 + 1) * 128], lhsT=ktT[:, u, t, :],
                                                   rhs=vtm[:, t, :], start=True, stop=True),
                          R=[t_ktT, t_v[t]], W=[t_U[(c // 4) % 2]])

                def scan_step(c):
                    ub = UB[(c // 4) % 2]
                    ecol = eg3[:, c, 63:64]
                    kb.op("dve", lambda e: e.tensor_tensor(out=Sall_f[:, c + 1, :], in0=Sall_f[:, c, :],
                                                           in1=pbank[ub][:, (c % 4) * 128:(c % 4 + 1) * 128], op=ALU.add),
                          R=[t_Sf[c], t_U[(c // 4) % 2]], W=[t_Sf[c + 1]])
                    kb.op("dve", lambda e: e.tensor_scalar(out=Sall_f[:, c + 1, :], in0=Sall_f[:, c + 1, :], scalar1=ecol,
                                                           scalar2=None, op0=ALU.mult), R=[t_eg], W=[t_Sf[c + 1]])

                def cast_half(hh):
                    kb.op("act", lambda e: e.activation(
                        out=Sall_b[:, hh * 8:(hh + 1) * 8, :].rearrange("p a b -> p (a b)"),
                        in_=Sall_f[:, hh * 8:(hh + 1) * 8, :].rearrange("p a b -> p (a b)"), func=AF.Copy),
                        R=t_Sf[hh * 8:(hh + 1) * 8], W=[t_Sb[hh]])

                def o_mm(c):
                    t, u = c // 2, c % 2
                    oc = (c % 8) * 64
                    kb.op("pe", lambda e: e.matmul(pbank[PO][:, oc:oc + 64], lhsT=Sall_b[:, c, :],
                                                   rhs=qt[:, c * 64:(c + 1) * 64], start=True, stop=False),
                          R=[t_Sb[c // 8], t_qt], W=[pbank_t[PO]])
                    kb.op("pe", lambda e: e.matmul(pbank[PO][:, oc:oc + 64], lhsT=vtm[:, t, :],
                                                   rhs=ATall[:, t, u, :], start=False, stop=True),
                          R=[t_v[t], t_ATt[t]], W=[pbank_t[PO]])

                for c in range(8):
                    u_mm(c)
                for t in range(8):
                    for u in range(2):
                        c = 2 * t + u
                        kb.op("pe", lambda e: e.matmul(pbank[SCB][u * 64:(u + 1) * 64, t * 64:(t + 1) * 64],
                                                       lhsT=kt[:, c * 64:(c + 1) * 64], rhs=qt[:, c * 64:(c + 1) * 64],
                                                       start=True, stop=True), R=[t_kt, t_qt], W=[pbank_t[SCB]])
                for c in range(8):
                    scan_step(c)
                cast_half(0)
                if nxt is not None:
                    proj_f(nxt, 1 - sx)
                for c in range(8, 16):
                    u_mm(c)
                for t in range(8):
                    for u in range(2):
                        kb.op("dve", lambda e: e.tensor_tensor(out=ATall[:, t, u, :], in0=pbank[SCB][:, t * 64:(t + 1) * 64],
                                                               in1=cmask[:, u, :], op=ALU.mult),
                              R=[pbank_t[SCB], t_c5], W=[t_ATt[t]])
                for c in range(8, 16):
                    scan_step(c)
                cast_half(1)
                kb.op("dve", lambda e: e.tensor_copy(out=Sst[:, h, :], in_=Sall_f[:, 16, :]), R=[t_Sf[16]], W=[t_S[h]])
                if nxt is not None:
                    proj_g(nxt, 1 - sx)
                for c in range(8):
                    o_mm(c)
                rms_finish(h, 0, 512, 0, sx)
                if nxt is not None:
                    proj_v(nxt, 1 - sx)
                for c in range(8, 16):
                    o_mm(c)
                rms_finish(h, 512, 512, 1, sx)

            def rms_finish(h, c0, n, gi, sx=0):
                sg, t_sg = SG[sx], T_SG[sx]
                kb.op("act", lambda e: e.activation(out=osq[:, 0:n], in_=pbank[PO][:, 0:n], func=AF.Square),
                      R=[pbank_t[PO]], W=[t_osq])
                kb.op("dve", lambda e: e.tensor_copy(out=oT[:, 0:n], in_=pbank[PO][:, 0:n]), W=[pbank_t[PO], t_oT])
                bn = proj_ring.next()
                kb.op("pe", lambda e: e.matmul(pbank[bn][:, 0:n], lhsT=ones_f[:, :], rhs=osq[:, 0:n], start=True, stop=True),
                      R=[t_osq, t_c3], W=[pbank_t[bn]])
                kb.op("act", lambda e: e.activation(out=rstd[:, 0:n], in_=pbank[bn][:, 0:n], func=AF.Ln,
                                                    bias=epsrms[:, :], scale=1.0 / 128.0),
                      R=[pbank_t[bn], t_const], W=[t_rstd])
                kb.op("act", lambda e: e.activation(out=rstd[:, 0:n], in_=rstd[:, 0:n], func=AF.Exp, scale=-0.5),
                      R=[], W=[t_rstd])
                kb.op("dve", lambda e: e.tensor_tensor(out=oT[:, 0:n], in0=oT[:, 0:n], in1=rstd[:, 0:n], op=ALU.mult),
                      R=[t_rstd], W=[t_oT])
                kb.op("dve", lambda e: e.scalar_tensor_tensor(out=brT[:, h, c0:c0 + n], in0=oT[:, 0:n],
                                                              scalar=nwT[:, h:h + 1], in1=sg[:, c0:c0 + n],
                                                              op0=ALU.mult, op1=ALU.mult),
                      R=[t_oT, t_sg, t_small], W=[brt[h][gi]])

            def hgrn_sample(h, sx):
                qh, fg, t_qh, t_fg = QH[sx], FG[sx], T_QH[sx], T_FG[sx]
                kb.dma("sp", Sin[:], hst_d.ap()[h], W=[t_Sin])
                kb.dma("sp", vbc[:], bass.AP(vscr_d, h * 128, [[0, 128], [1536, NS], [1, 128]]), R=[t_vscr], W=[t_vbc])
                for bb in range(NS):
                    kb.op("dve", lambda e: e.tensor_scalar(out=Sin[:, bb, :], in0=Sin[:, bb, :],
                                                           scalar1=fg[:, 1024 + bb:1025 + bb], scalar2=None, op0=ALU.mult),
                          R=[t_fg], W=[t_Sin])
                    kb.op("dve", lambda e: e.scalar_tensor_tensor(out=Sin[:, bb, :], in0=vbc[:, bb, :],
                                                                  scalar=kh[:, 1024 + bb:1025 + bb], in1=Sin[:, bb, :],
                                                                  op0=ALU.mult, op1=ALU.add),
                          R=[t_vbc, t_kh], W=[t_Sin])
                for bb in range(NS):
                    kb.op("pe", lambda e: e.matmul(pbank[PO][:, bb:bb + 1], lhsT=Sin[:, bb, :],
                                                   rhs=qh[:, 1024 + bb:1025 + bb], start=True, stop=True),
                          R=[t_Sin, t_qh], W=[pbank_t[PO]])
                kb.dma("sp", hsts_o.ap()[h], Sin[:], R=[t_Sin], store=True)
                rms_finish(h, 1024, NS, 2, sx)

            def mem_head(i, L):
                j = 12 + i
                s_ = 24 + i
                ensure_slab(s_ + 2)
                wi = slab_buf[s_]
                for (c0, c1) in groups:
                    bk = proj_group(wbs[wi], t_wb[wi], (0, 128), c0, c1)
                    kb.op("act", lambda e: e.activation(out=qTm[:, c0:c1], in_=pbank[bk][:, 0:c1 - c0], func=AF.Copy),
                          R=[pbank_t[bk]], W=[t_qTm, t_Sb[0], t_Sb[1]])
                    bk = proj_group(wbs[wi], t_wb[wi], (128, 256), c0, c1)
                    kb.op("act", lambda e: e.activation(out=sg[:, c0:c1], in_=pbank[bk][:, 0:c1 - c0], func=AF.Silu),
                          R=[pbank_t[bk]], W=[t_sg])
                if hf == 0:
                    kb.op("dve", lambda e: e.tensor_copy(out=qTs[:, j, :], in_=qTm[:, 1024:1040]), R=[t_qTm], W=[t_qTs])
                    kb.op("dve", lambda e: e.tensor_copy(out=sgs[:, j, :], in_=sg[:, 1024:1040]), R=[t_sg], W=[t_sgs])
                for gi in range(2):
                    pis = []
                    for mc in range(2):
                        bk = sc_ring.next()
                        kb.op("pe", lambda e: e.matmul(pbank[bk][:, :], lhsT=mkT[:, L, i, mc * 128:(mc + 1) * 128],
                                                       rhs=qTm[:, gi * 512:(gi + 1) * 512], start=True, stop=True),
                              R=[t_mk, t_qTm], W=[pbank_t[bk]])
                        pi = PT_ring.next()
                        kb.op("act", lambda e: e.activation(out=PTb[pi][:, :], in_=pbank[bk][:, :], func=AF.Exp, scale=SC),
                              R=[pbank_t[bk]], W=[t_PT[pi]])
                        pis.append(pi)
                    for mc in range(2):
                        pi = pis[mc]
                        kb.op("pe", lambda e: e.matmul(pbank[PO][:, :], lhsT=mvb[:, L, mc, i * 128:(i + 1) * 128],
                                                       rhs=PTb[pi][:, :], start=(mc == 0), stop=(mc == 1)),
                              R=[t_mk, t_PT[pi]], W=[pbank_t[PO]])
                        kb.op("pe", lambda e: e.matmul(pbank[PSM][:, :], lhsT=ones_b[:, :], rhs=PTb[pi][:, :],
                                                       start=(mc == 0), stop=(mc == 1)),
                              R=[t_c2, t_PT[pi]], W=[pbank_t[PSM]])
                    c0 = gi * 512
                    kb.op("dve", lambda e: e.reciprocal(out=rsb[:], in_=pbank[PSM][:, :]), R=[pbank_t[PSM]], W=[t_rs])
                    kb.op("dve", lambda e: e.tensor_tensor(out=tmpb[:], in0=pbank[PO][:, :], in1=rsb[:], op=ALU.mult),
                          R=[pbank_t[PO], t_rs], W=[t_tmp])
                    kb.op("dve", lambda e: e.tensor_tensor(out=brT[:, j, c0:c0 + 512], in0=tmpb[:],
                                                           in1=sg[:, c0:c0 + 512], op=ALU.mult),
                          R=[t_tmp, t_sg], W=[brt[j][gi]])

            ensure_slab(2)
            proj_q(0, 0)
            proj_f(0, 0)
            proj_g(0, 0)
            proj_v(0, 0)
            for h in range(12):
                hgrn_head(h, h % 2, h + 1 if h + 1 < 12 else None)
                if hf == 0:
                    hgrn_sample(h, h % 2)
            for i in range(4):
                mem_head(i, 1)
            if hf == 1:
                kb.dma("sp", hstp_o.ap().rearrange("h p v -> p h v"), Sst[:, :, :], R=t_S, store=True, sem_tok=t_S[0])
            kb.end_scope()

    for hf in range(2):
        if STOP == "setup":
            break
        l0_phase1(hf)
        if STOP == "p1a":
            break
        if hf == 0:
            l0_sample()
            sample_mem(0, None)
        if STOP == "sample":
            break
        phase2(0, hf)
        if STOP == "p2a":
            break
        if DEBUG_L0_ONLY:
            continue
        l1_phase1(hf)
        if hf == 0:
            sample_mem(1, None)
        phase2(1, hf)
    kb.barrier()
    es.close()
    return nc, kb


def arr_kp(w):
    w = np.asarray(w, dtype=np.float32)
    return np.ascontiguousarray(w.reshape(16, 128, -1).transpose(1, 0, 2))


def host_consts():
    c = {}
    c["c_ident"] = np.eye(128, dtype=np.float32)
    c["c_ones"] = np.ones((128, 128), np.float32)
    blk = np.zeros((128, 128), np.float32)
    blk[:64, :64] = 1
    blk[64:, 64:] = 1
    c["c_blk"] = blk
    oh = np.zeros((32, 384), np.float32)
    vm = np.zeros((24, 384), np.float32)
    bk = t5_bucket_np(np.arange(128))
    for d in range(128):
        oh[bk[d], 127 + d] = 1.0
        vm[:, 127 + d] = 1.0
    c["c_oh"] = oh
    c["c_vm"] = vm
    cm = np.zeros((128, 2, 64), np.float32)
    for r in range(128):
        cm[r, r // 64, (r % 64):] = 1.0
    c["c_cm"] = cm
    rm = np.ones((128, 1024), np.float32)
    rm[:, ::64] = 0.0
    c["c_rm"] = rm
    return c


_CACHE = {}


def kernel(x_prompt, x_sample, cache_mem_k, cache_mem_v, cache_swa_k, cache_swa_v, state_hgrn, mem_prompt,
           rel_bias, swa_w_in, swa_sinks, hg_w_in, hg_lb_logits, hg_norm_w, w_mem_k, w_mem_v, w_out, ln_w, ln_b):
    f = lambda a: np.asarray(a, dtype=np.float32)
    x_prompt, x_sample = f(x_prompt), f(x_sample)
    if "nc" not in _CACHE:
        _CACHE["nc"] = build_program()
    nc, kb = _CACHE["nc"]
    consts = host_consts()
    W = f(swa_w_in)[0]
    q, k, v, mq, g = W[:, :1536], W[:, 1536:1792], W[:, 1792:2048], W[:, 2048:2560], W[:, 2560:]
    slabs = []
    for a in (0, 2):
        slabs.append(np.concatenate([k[:, a * 64:(a + 1) * 64]] * 2 + [k[:, (a + 1) * 64:(a + 2) * 64]] * 2, axis=1))
    slabs.append(v)
    for j in range(16):
        A = q[:, j * 128:(j + 1) * 128] if j < 12 else mq[:, (j - 12) * 128:(j - 11) * 128]
        slabs.append(np.concatenate([A, g[:, j * 128:(j + 1) * 128]], axis=1))
    w0 = np.stack([arr_kp(s) for s in slabs])
    W = f(hg_w_in)[0]
    q, fq, iv, mq, g = W[:, :1536], W[:, 1536:3072], W[:, 3072:4608], W[:, 4608:5120], W[:, 5120:]
    slabs = []
    for h in range(12):
        sl = slice(h * 128, (h + 1) * 128)
        slabs.append(np.concatenate([q[:, sl], fq[:, sl]], axis=1))
        slabs.append(np.concatenate([iv[:, sl], g[:, sl]], axis=1))
    for i in range(4):
        slabs.append(np.concatenate([mq[:, i * 128:(i + 1) * 128], g[:, (12 + i) * 128:(13 + i) * 128]], axis=1))
    w1 = np.stack([arr_kp(s) for s in slabs])
    wk = np.stack([arr_kp(f(w_mem_k)[i]) for i in range(2)])
    wv = np.stack([arr_kp(f(w_mem_v)[i]) for i in range(2)])
    wo = np.stack([arr_kp(f(w_out)[i]) for i in range(2)])
    relb = f(rel_bias)
    hd_half = np.arange(128) // 64
    relb0 = np.stack([relb[0, 2 * j + hd_half] for j in range(12)], axis=1)
    sinksc = np.stack([f(swa_sinks)[0, 2 * j + hd_half] for j in range(12)], axis=1)
    lbl = np.ascontiguousarray(f(hg_lb_logits).reshape(2, 12, 128).transpose(2, 0, 1))
    hgnw = np.ascontiguousarray(f(hg_norm_w)[0].reshape(12, 128).T)
    shared = dict(wk=wk, wv=wv, w0=w0, w1=w1, wo=wo, lnw=f(ln_w), lnb=f(ln_b), relb=relb,
                  relb0=np.ascontiguousarray(relb0), sinksc=np.ascontiguousarray(sinksc), lbl=lbl, hgnw=hgnw)
    shared.update(consts)
    cmk, cmv_ = f(cache_mem_k), f(cache_mem_v)
    csk, csv = f(cache_swa_k)[0], f(cache_swa_v)[0]
    sth = f(state_hgrn)[0]
    mem_prompt = f(mem_prompt)
    in_maps = []
    for c in range(NCORES):
        sl = slice(c * NS, (c + 1) * NS)
        m = dict(shared)
        m["xT"] = arr_kp(x_prompt[c].T)
        m["xsT"] = arr_kp(x_sample[sl, 0, :].T)
        m["xres"] = np.ascontiguousarray(x_prompt[c])
        m["xsres"] = np.ascontiguousarray(x_sample[sl, 0, :])
        m["memT"] = arr_kp(mem_prompt[c].T)
        kk = csk[sl]
        kT = kk.transpose(3, 0, 2, 1)
        kTp = np.zeros((128, NS, 4, 2, 128), np.float32)
        kTp[0:64, :, :, 0, :] = kT
        kTp[64:128, :, :, 1, :] = kT
        m["cswaKT"] = kTp
        m["cswaV"] = np.ascontiguousarray(csv[sl].reshape(NS, 128, 256).transpose(1, 0, 2))
        m["cswaKraw"] = np.ascontiguousarray(kk.reshape(NS, 128, 256))
        m["cswaVraw"] = np.ascontiguousarray(csv[sl].reshape(NS, 128, 256))
        m["cmkT"] = np.ascontiguousarray(cmk[:, sl].transpose(0, 3, 4, 1, 2))
        m["cmv"] = np.ascontiguousarray(cmv_[:, sl].reshape(2, NS, 2, 128, 4, 128).transpose(0, 4, 3, 1, 2, 5))
        m["hst"] = np.ascontiguousarray(sth[sl].transpose(1, 2, 0, 3))
        in_maps.append(m)
    if STOP == 'setup':
        for m in in_maps:
            for kname in BIG:
                m[kname] = np.zeros((1, 1), np.float32)
    if DEBUG_NCORES < NCORES:
        res = run_bass_kernel_spmd(nc, in_maps[:DEBUG_NCORES], core_ids=list(range(DEBUG_NCORES)))
        R = list(res.results) + [res.results[0]] * (NCORES - DEBUG_NCORES)
    else:
        res = run_bass_kernel_spmd(nc, in_maps, core_ids=list(range(NCORES)))
        R = res.results
    y_prompt = np.stack([R[c]["y"] for c in range(NCORES)])
    y_sample = np.concatenate([R[c]["ys"] for c in range(NCORES)])[:, None, :]
    mem_k = np.stack([R[c]["mkT_o"].transpose(0, 3, 1, 2) for c in range(NCORES)], axis=1)
    mem_v = np.stack([R[c]["mv_o"].reshape(2, 256, 4, 128) for c in range(NCORES)], axis=1)
    swa_kp = np.stack([R[c]["swakT_o"].transpose(2, 0, 1) for c in range(NCORES)])[None]
    swa_vp = np.stack([R[c]["swav_o"].reshape(128, 4, 64) for c in range(NCORES)])[None]
    hstp = np.stack([R[c]["hstp_o"] for c in range(NCORES)])[None]
    ks = []
    vs = []
    hs = []
    for c in range(NCORES):
        knew = R[c]["swaksT_o"].transpose(2, 0, 1).reshape(NS, 1, 256)
        ks.append(np.concatenate([R[c]["swaks_sh_o"], knew], axis=1).reshape(NS, 128, 4, 64))
        vnew = R[c]["swavs_new_o"].reshape(NS, 1, 256)
        vs.append(np.concatenate([R[c]["swavs_sh_o"], vnew], axis=1).reshape(NS, 128, 4, 64))
        hs.append(R[c]["hsts_o"].transpose(2, 0, 1, 3))
    swa_ks = np.concatenate(ks)[None]
    swa_vs = np.concatenate(vs)[None]
    hsts = np.concatenate(hs)[None]
    outs = (y_prompt, y_sample, mem_k, mem_v, swa_kp, swa_vp, hstp, swa_ks, swa_vs, hsts)
    return tuple(np.ascontiguousarray(o, dtype=np.float32) for o in outs)
```

```python
import numpy as np
from contextlib import ExitStack
import concourse.bass as bass
import concourse.mybir as mybir
from concourse.bass_utils import run_bass_kernel_spmd

F32 = mybir.dt.float32
BF16 = mybir.dt.bfloat16
AF = mybir.ActivationFunctionType
ALU = mybir.AluOpType

NCORES = 8
D = 2048
SEQ = 2048
TT = 1024
NS = 16
XW = 1152
BW = 1040
ALPHA = float((2.0 * 2) ** 0.25)
LN_EPS = 1e-5
RMS_EPS = 1e-6
DEBUG_L0_ONLY = False
STOP = None
DEBUG_NCORES = 8
SETUP_STOP = 99
P1_STOP = 99
BIG = ('xT', 'xres', 'w0', 'w1', 'wo', 'cmkT', 'cmv', 'hst', 'cswaKT', 'cswaV', 'cswaKraw', 'cswaVraw')


class _Stop(Exception):
    pass


class Tok:
    __slots__ = ("name", "w", "r", "dsem", "ssem", "p")

    def __init__(self, name, p=False):
        self.name = name
        self.w = None
        self.r = {}
        self.dsem = None
        self.ssem = None
        self.p = p


class KB:
    ENG = ("pe", "act", "dve", "pool", "sp")

    def __init__(self, nc, es, n_dma_sems=95):
        self.nc = nc
        self.eng = {"pe": nc.tensor, "act": nc.scalar, "dve": nc.vector, "pool": nc.gpsimd, "sp": nc.sync}
        self.sems = []
        self.esem = {}
        for e in self.ENG:
            self.esem[e] = len(self.sems)
            self.sems.append(es.enter_context(nc.semaphore("e_" + e)))
        self.free_dma = []
        for i in range(n_dma_sems):
            self.free_dma.append(len(self.sems))
            self.sems.append(es.enter_context(nc.semaphore("d%d" % i)))
        self.cnt = {e: 0 for e in self.ENG}
        self.known = {e: {} for e in self.ENG}
        self.dcnt = {}
        self.nwait = 0
        self.scoped = []

    def _wait(self, e, deps):
        k = self.known[e]
        pe_own = self.esem["pe"]
        for (s, v) in deps:
            if e == "pe" and s == pe_own:
                continue
            if k.get(s, 0) >= v:
                continue
            self.eng[e].wait_ge(self.sems[s], v)
            self.nwait += 1
            k[s] = v

    @staticmethod
    def _deps(R, W):
        d = []
        for t in R:
            if t.w is not None:
                d.append(t.w)
        for t in W:
            if t.w is not None:
                d.append(t.w)
            d.extend(t.r.items())
        return d

    def op(self, e, fn, R=(), W=()):
        self._wait(e, self._deps(R, W))
        ins = fn(self.eng[e])
        self.cnt[e] += 1
        s = self.esem[e]
        v = self.cnt[e]
        ins.then_inc(self.sems[s], 1)
        for t in R:
            t.r[s] = v
        for t in W:
            t.w = (s, v)
            t.r = {}

    def _dsem(self, tok, store):
        if store:
            if tok.ssem is None:
                tok.ssem = self.free_dma.pop()
                self.dcnt.setdefault(tok.ssem, 0)
                if not tok.p:
                    self.scoped.append(tok)
            return tok.ssem
        if tok.dsem is None:
            tok.dsem = self.free_dma.pop()
            self.dcnt.setdefault(tok.dsem, 0)
            if not tok.p:
                self.scoped.append(tok)
        return tok.dsem

    def end_scope(self):
        self.barrier()
        for t in self.scoped:
            for a in ("dsem", "ssem"):
                v = getattr(t, a)
                if v is not None:
                    self.free_dma.append(v)
                    setattr(t, a, None)
        self.scoped = []

    def dma(self, e, out, in_, R=(), W=(), sem_tok=None, store=False):
        if sem_tok is None:
            sem_tok = R[0] if store else W[0]
        s = self._dsem(sem_tok, store)
        deps = [(a, b) for (a, b) in self._deps(R, W) if a != s]
        self._wait(e, deps)
        ins = self.eng[e].dma_start(out=out, in_=in_)
        self.dcnt[s] += 16
        v = self.dcnt[s]
        ins.then_inc(self.sems[s], 16)
        for t in R:
            t.r[s] = v
        for t in W:
            t.w = (s, v)
            t.r = {}

    def barrier(self, engines=None):
        engines = engines or self.ENG
        for e in engines:
            deps = [(self.esem[x], self.cnt[x]) for x in self.ENG if x != e and self.cnt[x] > 0]
            deps += [(s, v) for s, v in self.dcnt.items() if v > 0]
            self._wait(e, deps)


class Ring:
    def __init__(self, items):
        self.items = items
        self.i = 0

    def next(self):
        it = self.items[self.i % len(self.items)]
        self.i += 1
        return it


def t5_bucket_np(d):
    n = np.maximum(d, 0)
    max_exact = 16
    nf = np.maximum(n, 1).astype(np.float32)
    large = max_exact + (np.log(nf / np.float32(max_exact)) / np.float32(np.log(128 / max_exact))
                         * np.float32(32 - max_exact)).astype(np.int32)
    large = np.minimum(large, 31)
    return np.where(n < max_exact, n, large)


def build_program():
    nc = bass.Bass("TRN2", target_bir_lowering=False)

    def din(name, shape):
        if STOP == "setup" and name in BIG:
            shape = [1, 1]
        return nc.dram_tensor(name, list(shape), F32, kind="ExternalInput")

    def dout(name, shape):
        return nc.dram_tensor(name, list(shape), F32, kind="ExternalOutput")

    xT_d = din("xT", [128, 16, SEQ])
    xsT_d = din("xsT", [128, 16, NS])
    xres_d = din("xres", [SEQ, D])
    xsres_d = din("xsres", [NS, D])
    memT_d = din("memT", [128, 16, 256])
    wk_d = din("wk", [2, 128, 16, 512])
    wv_d = din("wv", [2, 128, 16, 512])
    w0_d = din("w0", [19, 128, 16, 256])
    w1_d = din("w1", [28, 128, 16, 256])
    wo_d = din("wo", [2, 128, 16, D])
    lnw_d = din("lnw", [2, D])
    lnb_d = din("lnb", [2, D])
    relb_d = din("relb", [32, 24])
    relb0_d = din("relb0", [128, 12])
    sinks_d = din("sinksc", [128, 12])
    lbl_d = din("lbl", [128, 2, 12])
    hgnw_d = din("hgnw", [128, 12])
    cswaKT_d = din("cswaKT", [128, NS, 4, 2, 128])
    cswaV_d = din("cswaV", [128, NS, 256])
    cswaKraw_d = din("cswaKraw", [NS, 128, 256])
    cswaVraw_d = din("cswaVraw", [NS, 128, 256])
    cmkT_d = din("cmkT", [2, 4, 128, NS, 256])
    cmv_d = din("cmv", [2, 4, 128, NS, 2, 128])
    hst_d = din("hst", [12, 128, NS, 128])
    c_ident_d = din("c_ident", [128, 128])
    c_ones_d = din("c_ones", [128, 128])
    c_blk_d = din("c_blk", [128, 128])
    c_oh_d = din("c_oh", [32, 384])
    c_vm_d = din("c_vm", [24, 384])
    c_cm_d = din("c_cm", [128, 2, 64])
    c_rm_d = din("c_rm", [128, 1024])

    y_d = dout("y", [SEQ, D])
    ys_d = dout("ys", [NS, D])
    mkT_o = dout("mkT_o", [2, 4, 128, 256])
    mv_o = dout("mv_o", [2, 256, 512])
    swakT_o = dout("swakT_o", [4, 64, 128])
    swav_o = dout("swav_o", [128, 256])
    hstp_o = dout("hstp_o", [12, 128, 128])
    swaks_sh_o = dout("swaks_sh_o", [NS, 127, 256])
    swaksT_o = dout("swaksT_o", [4, 64, NS])
    swavs_sh_o = dout("swavs_sh_o", [NS, 127, 256])
    swavs_new_o = dout("swavs_new_o", [NS, 256])
    hsts_o = dout("hsts_o", [12, 128, NS, 128])

    h1res_d = nc.dram_tensor("h1res", [SEQ + NS, D], F32, kind="Internal")
    gscr_d = nc.dram_tensor("gscr", [24, 384], F32, kind="Internal")
    vscr_d = nc.dram_tensor("vscr", [NS, 1536], F32, kind="Internal")

    es = ExitStack()
    kb = KB(nc, es)

    uid = [0]

    def sb(st, name, shape, dt):
        uid[0] += 1
        return st.enter_context(nc.sbuf_tensor("s%d_%s" % (uid[0], name), list(shape), dt))

    def ps(st, name, shape, dt):
        uid[0] += 1
        return st.enter_context(nc.psum_tensor("p%d_%s" % (uid[0], name), list(shape), dt))

    hT = sb(es, "hT", [128, 16, XW], BF16)
    brT = sb(es, "brT", [128, 16, BW], BF16)
    mkT = sb(es, "mkT", [128, 2, 4, 256], BF16)
    mvb = sb(es, "mvb", [128, 2, 2, 512], BF16)
    ident = sb(es, "ident", [128, 128], BF16)
    ones_b = sb(es, "ones_b", [128, 128], BF16)
    ones_f = sb(es, "ones_f", [128, 128], F32)
    blk_f = sb(es, "blk_f", [128, 128], F32)
    cmask = sb(es, "cmask", [128, 2, 64], F32)
    rmask = sb(es, "rmask", [128, 1024], F32)
    expsink = sb(es, "expsink", [128, 12], F32)
    expb0 = sb(es, "expb0", [128, 12], F32)
    expbcol = sb(es, "expbcol", [128, 24], F32)
    lbT = sb(es, "lbT", [128, 12], F32)
    omlbT = sb(es, "omlbT", [128, 12], F32)
    nwT = sb(es, "nwT", [128, 12], F32)
    epsln = sb(es, "epsln", [128, 1], F32)
    epsrms = sb(es, "epsrms", [128, 1], F32)
    qTs = sb(es, "qTs", [128, 16, NS], BF16)
    sgs = sb(es, "sgs", [128, 16, NS], F32)
    kTs_f = sb(es, "kTs_f", [128, 4, NS], F32)
    vTs_f = sb(es, "vTs_f", [128, 4, NS], F32)
    Sst = sb(es, "Sst", [128, 12, 128], F32)

    pbank = [ps(es, "pb%d" % i, [128, 512], F32) for i in range(6)]
    pbank_t = [Tok("pb%d" % i, True) for i in range(6)]
    ptr_full = [ps(es, "ptrf%d" % i, [128, 1024], BF16) for i in range(2)]
    ptr = [ptr_full[0][:, 0:512], ptr_full[1][:, 0:512]]
    ptr_t = [Tok("ptr%d" % i, True) for i in range(2)]

    hTt = [Tok("hT%d" % i, True) for i in range(8)]
    hTx = Tok("hTx", True)
    brt = [[Tok("br%d_%d" % (j, g), True) for g in range(3)] for j in range(16)]
    t_const = Tok("const", True)
    t_mk = Tok("mk", True)
    t_small = Tok("small", True)
    t_qTs = Tok("qTs", True)
    t_sgs = Tok("sgs", True)
    t_kTs = Tok("kTs", True)
    t_vTs = Tok("vTs", True)
    t_S = [Tok("S%d" % h, True) for h in range(12)]
    t_gscr = Tok("gscr", True)
    t_h1res = [Tok("h1res%d" % i, True) for i in range(17)]
    t_out = Tok("out", True)

    def hT_toks(c0, c1):
        ts = []
        for t in range(8):
            if c0 < (t + 1) * 128 and c1 > t * 128:
                ts.append(hTt[t])
        if c1 > 1024:
            ts.append(hTx)
        return ts

    with ExitStack() as st:
        cst = sb(st, "cst", [128, 128], F32)
        relb = sb(st, "relb", [32, 24], F32)
        oh = sb(st, "oh", [32, 384], F32)
        vm = sb(st, "vm", [24, 384], F32)
        G = sb(st, "G", [24, 384], F32)
        lbl = sb(st, "lbl", [128, 2, 12], F32)
        memT = sb(st, "memT", [128, 16, 256], BF16)
        wkb = sb(st, "wkb", [128, 16, 512], BF16)
        wvb = sb(st, "wvb", [128, 16, 512], BF16)
        mko = sb(st, "mko", [128, 4, 256], F32)
        mvo = sb(st, "mvo", [128, 2, 512], F32)
        t_relb, t_oh, t_vm, t_G, t_lbl, t_memT = Tok("relb"), Tok("oh"), Tok("vm"), Tok("G"), Tok("lbl"), Tok("memT")
        t_wk, t_wv, t_mko, t_mvo = Tok("wk"), Tok("wv"), Tok("mko"), Tok("mvo")
        t_c1, t_c2, t_c3, t_c4, t_c5 = Tok("c1", True), Tok("c2", True), Tok("c3", True), Tok("c4", True), Tok("c5", True)

        kb.dma("pool", ident[:], c_ident_d.ap()[:, :], W=[t_c1])
        kb.dma("pool", ones_b[:], c_ones_d.ap()[:, :], W=[t_c2])
        kb.dma("sp", ones_f[:], c_ones_d.ap()[:, :], W=[t_c3])
        kb.dma("sp", blk_f[:], c_blk_d.ap()[:, :], W=[t_c4])
        kb.dma("sp", cmask[:], c_cm_d.ap()[:, :, :], W=[t_c5])
        kb.dma("sp", rmask[:], c_rm_d.ap()[:, :], W=[t_c5], sem_tok=t_c5)
        kb.dma("sp", relb[:], relb_d.ap()[:, :], W=[t_relb])
        kb.dma("sp", oh[:], c_oh_d.ap()[:, :], W=[t_oh])
        kb.dma("sp", vm[:], c_vm_d.ap()[:, :], W=[t_vm])
        kb.dma("sp", expsink[:], sinks_d.ap()[:, :], W=[t_small])
        kb.dma("sp", expb0[:], relb0_d.ap()[:, :], W=[t_small], sem_tok=t_small)
        kb.dma("sp", lbl[:], lbl_d.ap()[:, :, :], W=[t_lbl])
        kb.dma("sp", nwT[:], hgnw_d.ap()[:, :], W=[t_small], sem_tok=t_small)
        kb.dma("pool", memT[:], memT_d.ap()[:, :, :], W=[t_memT])

        def _ck(n):
            if SETUP_STOP == n:
                raise _Stop()
        try:
          _ck(1)
          kb.op("dve", lambda e: e.memset(epsln[:], LN_EPS), W=[t_const])
          kb.op("dve", lambda e: e.memset(epsrms[:], RMS_EPS), W=[t_const])
          kb.op("act", lambda e: e.activation(out=expsink[:], in_=expsink[:], func=AF.Exp), R=[t_small], W=[t_small])
          kb.op("act", lambda e: e.activation(out=expb0[:], in_=expb0[:], func=AF.Exp), R=[t_small], W=[t_small])
          kb.op("dve", lambda e: e.tensor_tensor(out=lbT[:], in0=lbl[:, 1, :], in1=lbl[:, 0, :], op=ALU.subtract),
                R=[t_lbl], W=[t_small])
          kb.op("act", lambda e: e.activation(out=lbT[:], in_=lbT[:], func=AF.Sigmoid), R=[t_small], W=[t_small])
          kb.op("dve", lambda e: e.tensor_scalar(out=omlbT[:], in0=lbT[:], scalar1=-1.0, scalar2=1.0,
                                                 op0=ALU.mult, op1=ALU.add), R=[t_small], W=[t_small])
          _ck(2)
          b0 = pbank[0]
          kb.op("pe", lambda e: e.matmul(b0[0:24, 0:384], lhsT=relb[0:32, 0:24], rhs=oh[0:32, 0:384],
                                         start=True, stop=True), R=[t_relb, t_oh], W=[pbank_t[0]])
          kb.op("act", lambda e: e.activation(out=G[:], in_=b0[0:24, 0:384], func=AF.Exp), R=[pbank_t[0]], W=[t_G])
          kb.op("dve", lambda e: e.tensor_tensor(out=G[:], in0=G[:], in1=vm[:], op=ALU.mult), R=[t_G, t_vm], W=[t_G])
          kb.dma("sp", gscr_d.ap()[:, :], G[:], R=[t_G], W=[t_gscr], store=True)

          _ck(3)
          for L in range(2):
              kb.dma("pool", wkb[:], wk_d.ap()[L], W=[t_wk])
              kb.dma("pool", wvb[:], wv_d.ap()[L], W=[t_wv])
              _ck(4)
              for i in range(4):
                  bk = 1 + (i % 2)
                  for kc in range(16):
                      kb.op("pe", lambda e: e.matmul(pbank[bk][:, 0:256], lhsT=wkb[:, kc, i * 128:(i + 1) * 128],
                                                     rhs=memT[:, kc, :], start=(kc == 0), stop=(kc == 15)),
                            R=[t_wk, t_memT, t_c1], W=[pbank_t[bk]])
                  kb.op("act", lambda e: e.activation(out=mkT[:, L, i, :], in_=pbank[bk][:, 0:256], func=AF.Copy),
                        R=[pbank_t[bk]], W=[t_mk])
                  kb.op("dve", lambda e: e.tensor_copy(out=mko[:, i, :], in_=pbank[bk][:, 0:256]),
                        W=[pbank_t[bk], t_mko])
              _ck(5)
              kb.dma("sp", mkT_o.ap()[L].rearrange("i p m -> p i m"), mko[:], R=[t_mko], store=True)
              _ck(6)
              for mc in range(2):
                  bk = 3 + mc
                  for kc in range(16):
                      kb.op("pe", lambda e: e.matmul(pbank[bk][:, :], lhsT=memT[:, kc, mc * 128:(mc + 1) * 128],
                                                     rhs=wvb[:, kc, :], start=(kc == 0), stop=(kc == 15)),
                            R=[t_wv, t_memT], W=[pbank_t[bk]])
                  kb.op("act", lambda e: e.activation(out=mvb[:, L, mc, :], in_=pbank[bk][:, :], func=AF.Copy),
                        R=[pbank_t[bk]], W=[t_mk])
                  kb.op("dve", lambda e: e.tensor_copy(out=mvo[:, mc, :], in_=pbank[bk][:, :]),
                        W=[pbank_t[bk], t_mvo])
              kb.dma("sp", mv_o.ap()[L].rearrange("(mc p) n -> p mc n", p=128), mvo[:], R=[t_mvo],
                     store=True)
        except _Stop:
            pass
        kb.end_scope()

    proj_ring = Ring([0, 1])
    sc_ring = Ring([2, 3])
    PO, PSM = 4, 5

    def load_x(hf):
        kb.dma("pool", hT[:, :, 0:1024], xT_d.ap()[:, :, hf * 1024:(hf + 1) * 1024], W=hTt, sem_tok=hTt[0])
        if hf == 0:
            kb.dma("pool", hT[:, :, 1024:1040], xsT_d.ap()[:, :, :], W=[hTx])
        else:
            kb.dma("pool", hT[:, :, 1024:1152], xT_d.ap()[:, :, 896:1024], W=[hTx])

    def proj_group(wb, t_wb, wcols, c0, c1):
        bk = proj_ring.next()
        toks = hT_toks(c0, c1)
        for kc in range(16):
            kb.op("pe", lambda e: e.matmul(pbank[bk][:, 0:c1 - c0], lhsT=wb[:, kc, wcols[0]:wcols[1]],
                                           rhs=hT[:, kc, c0:c1], start=(kc == 0), stop=(kc == 15)),
                  R=[t_wb] + toks, W=[pbank_t[bk]])
        return bk

    def phase2(L, hf):
        with ExitStack() as st:
            wo = sb(st, "wo", [128, 16, D], BF16)
            lnwb = sb(st, "lnwb", [128, 2, D], F32)
            xr = [sb(st, "xr%d" % i, [128, D], F32) for i in range(2)]
            yb = sb(st, "yb", [128, D], BF16)
            stats = sb(st, "stats", [128, 4, 6], F32)
            mv2 = sb(st, "mv2", [128, 2], F32)
            sd = sb(st, "sd", [128, 1], F32)
            t_wo = [Tok("wo%d" % i) for i in range(4)]
            t_ln = Tok("ln")
            t_xr = [Tok("xr0"), Tok("xr1")]
            t_yb, t_stats, t_mv2, t_sd = Tok("yb"), Tok("stats"), Tok("mv2"), Tok("sd")
            for dg in range(4):
                kb.dma("pool", wo[:, :, dg * 512:(dg + 1) * 512], wo_d.ap()[L][:, :, dg * 512:(dg + 1) * 512],
                       W=[t_wo[dg]])
            kb.dma("sp", lnwb[:, 0, :], bass.AP(lnw_d, L * D, [[0, 128], [1, D]]), W=[t_ln])
            kb.dma("sp", lnwb[:, 1, :], bass.AP(lnb_d, L * D, [[0, 128], [1, D]]), W=[t_ln])
            ring6 = Ring([0, 1, 2, 3, 4, 5])
            tr_ring = Ring([0, 1])
            tiles = list(range(8)) + ([8] if hf == 0 else [])

            def load_res(t):
                M = 128 if t < 8 else NS
                b = xr[t % 2]
                if L == 0:
                    src = xres_d.ap()[hf * 1024 + t * 128: hf * 1024 + t * 128 + 128, :] if t < 8 else xsres_d.ap()[:, :]
                    kb.dma("sp", b[0:M, :], src, W=[t_xr[t % 2]])
                else:
                    r0 = hf * 1024 + t * 128 if t < 8 else SEQ
                    ti = hf * 8 + t if t < 8 else 16
                    kb.dma("sp", b[0:M, :], h1res_d.ap()[r0:r0 + M, :], R=[t_h1res[ti]], W=[t_xr[t % 2]])

            load_res(tiles[0])
            for idx, t in enumerate(tiles):
                M = 128 if t < 8 else NS
                c0 = t * 128
                g = t // 4
                if idx + 1 < len(tiles):
                    load_res(tiles[idx + 1])
                y = xr[t % 2]
                ty = t_xr[t % 2]
                for dg in range(4):
                    bk = ring6.next()
                    for ec in range(16):
                        kb.op("pe", lambda e: e.matmul(pbank[bk][0:M, :], lhsT=brT[:, ec, c0:c0 + M],
                                                       rhs=wo[:, ec, dg * 512:(dg + 1) * 512],
                                                       start=(ec == 0), stop=(ec == 15)),
                              R=[brt[ec][g], t_wo[dg]], W=[pbank_t[bk]])
                    kb.op("dve", lambda e: e.scalar_tensor_tensor(out=y[0:M, dg * 512:(dg + 1) * 512],
                                                                  in0=y[0:M, dg * 512:(dg + 1) * 512], scalar=ALPHA,
                                                                  in1=pbank[bk][0:M, :], op0=ALU.mult, op1=ALU.add),
                          R=[pbank_t[bk]], W=[ty])
                    kb.op("dve", lambda e: e.bn_stats(out=stats[0:M, dg, :], in_=y[0:M, dg * 512:(dg + 1) * 512]),
                          R=[ty], W=[t_stats])
                kb.op("dve", lambda e: e.bn_aggr(out=mv2[0:M, :], in_=stats[0:M, :, :].rearrange("p a b -> p (a b)")),
                      R=[t_stats], W=[t_mv2])
                kb.op("act", lambda e: e.activation(out=sd[0:M, :], in_=mv2[0:M, 1:2], func=AF.Sqrt,
                                                    bias=epsln[0:M, :], scale=1.0), R=[t_mv2, t_const], W=[t_sd])
                kb.op("dve", lambda e: e.reciprocal(out=sd[0:M, :], in_=sd[0:M, :]), R=[t_sd], W=[t_sd])
                kb.op("dve", lambda e: e.scalar_tensor_tensor(out=y[0:M, :], in0=y[0:M, :], scalar=mv2[0:M, 0:1],
                                                              in1=lnwb[0:M, 0, :], op0=ALU.subtract, op1=ALU.mult),
                      R=[t_mv2, t_ln], W=[ty])
                kb.op("dve", lambda e: e.scalar_tensor_tensor(out=y[0:M, :], in0=y[0:M, :], scalar=sd[0:M, 0:1],
                                                              in1=lnwb[0:M, 1, :], op0=ALU.mult, op1=ALU.add),
                      R=[t_sd, t_ln], W=[ty])
                if L == 1 or DEBUG_L0_ONLY:
                    dst = y_d.ap()[hf * 1024 + c0: hf * 1024 + c0 + 128, :] if t < 8 else ys_d.ap()[:, :]
                    kb.dma("sp", dst, y[0:M, :], R=[ty], store=True)
                if L == 0:
                    r0 = hf * 1024 + t * 128 if t < 8 else SEQ
                    ti = hf * 8 + t if t < 8 else 16
                    kb.dma("sp", h1res_d.ap()[r0:r0 + M, :], y[0:M, :], R=[ty], W=[t_h1res[ti]], store=True)
                    kb.op("act", lambda e: e.activation(out=yb[0:M, :], in_=y[0:M, :], func=AF.Copy), R=[ty], W=[t_yb])
                    htok = hTt[t] if t < 8 else hTx
                    for q4 in range(4):
                        pi = tr_ring.next()
                        for c in range(4):
                            dc = q4 * 4 + c
                            kb.op("pe", lambda e: e.transpose(out=ptr[pi][:, c * 128:c * 128 + M],
                                                              in_=yb[0:M, dc * 128:(dc + 1) * 128],
                                                              identity=ident[0:M, 0:M]),
                                  R=[t_yb, t_c1], W=[ptr_t[pi]])
                        kb.op("act", lambda e: e.activation(
                            out=hT[:, q4 * 4:(q4 + 1) * 4, c0:c0 + M],
                            in_=ptr[pi][:, :].rearrange("p (c m) -> p c m", c=4)[:, :, 0:M], func=AF.Copy),
                            R=[ptr_t[pi]], W=[htok])
            kb.end_scope()

    def l0_phase1(hf):
        with ExitStack() as st:
            expbT = sb(st, "expbT", [128, 24, 2, 128], F32)
            wbs = [sb(st, "wb%d" % i, [128, 16, 256], BF16) for i in range(3)]
            kT2 = sb(st, "kT2", [128, 4, 2, XW], BF16)
            vtm = sb(st, "vtm", [128, 9, 256], BF16)
            qTb = [sb(st, "qT%d" % i, [128, BW], BF16) for i in range(2)]
            sgb = [sb(st, "sg%d" % i, [128, BW], F32) for i in range(2)]
            Eb = [sb(st, "E%d" % i, [128, 512], F32) for i in range(2)]
            PTb = [sb(st, "PT%d" % i, [128, 512], BF16) for i in range(2)]
            rsb = sb(st, "rsb", [128, 512], F32)
            tmpb = sb(st, "tmpb", [128, 512], F32)
            kout = sb(st, "kout", [128, 4, 128], F32)
            vout = sb(st, "vout", [128, 256], F32)
            t_expbT = Tok("expbT")
            t_wb = [Tok("wb%d" % i) for i in range(3)]
            t_kT2 = [[Tok("kT2_%d_%d" % (h, g)) for g in range(3)] for h in range(4)]
            t_v = [Tok("v%d" % i) for i in range(9)]
            t_qT = [[Tok("qT%d_%d" % (i, g)) for g in range(3)] for i in range(2)]
            t_sg = [[Tok("sg%d_%d" % (i, g)) for g in range(3)] for i in range(2)]
            t_E = [Tok("E0"), Tok("E1")]
            t_PT = [Tok("PT0"), Tok("PT1")]
            t_rs, t_tmp, t_kout, t_vout = Tok("rs"), Tok("tmp"), Tok("kout"), Tok("vout")
            E_ring, PT_ring = Ring([0, 1]), Ring([0, 1])

            t_kz = Tok("kz")
            kb.op("dve", lambda e: e.memset(kT2[:, :, :, :].rearrange("p a b c -> p (a b c)"), 0.0), W=[t_kz])
            for k in range(128):
                kb.dma("sp", expbT[k:k + 1, :, :, :],
                       bass.AP(gscr_d, 127 - k, [[0, 1], [384, 24], [128, 2], [1, 128]]),
                       R=[t_gscr], W=[t_expbT])
            load_x(hf)
            nslab = 19
            wring = Ring([0, 1, 2])
            slab_buf = {}

            nxt_slab = [0]

            def ensure_slab(upto):
                while nxt_slab[0] <= min(upto, nslab - 1):
                    s = nxt_slab[0]
                    i = wring.next()
                    kb.dma("pool", wbs[i][:], w0_d.ap()[s], W=[t_wb[i]])
                    slab_buf[s] = i
                    nxt_slab[0] += 1

            ensure_slab(1)
            ext = NS if hf == 0 else 128
            kv_groups = [(0, 512), (512, 1024), (1024, 1024 + ext)]
            qg_groups = [(0, 512), (512, 1024)] + ([(1024, 1040)] if hf == 0 else [])

            if hf == 0:
                kb.op("dve", lambda e: e.tensor_copy(out=expbcol[:], in_=expbT[:, :, 1, 0]), R=[t_expbT], W=[t_small])

            for s in range(2):
                ensure_slab(s + 2)
                wi = slab_buf[s]
                for u in range(2):
                    hk = 2 * s + u
                    for g, (c0, c1) in enumerate(kv_groups):
                        bk = proj_group(wbs[wi], t_wb[wi], (u * 128, (u + 1) * 128), c0, c1)
                        kb.op("act", lambda e: e.activation(out=kT2[0:64, hk, 0, c0:c1], in_=pbank[bk][0:64, 0:c1 - c0],
                                                            func=AF.Copy), R=[pbank_t[bk], t_kz], W=[t_kT2[hk][g]])
                        kb.op("act", lambda e: e.activation(out=kT2[64:128, hk, 1, c0:c1], in_=pbank[bk][64:128, 0:c1 - c0],
                                                            func=AF.Copy), R=[pbank_t[bk], t_kz], W=[t_kT2[hk][g]])
                        if hf == 1 and g == 1:
                            kb.op("dve", lambda e: e.tensor_copy(out=kout[:, hk, :], in_=pbank[bk][:, 384:512]),
                                  W=[pbank_t[bk], t_kout])
                        if hf == 0 and g == 2:
                            kb.op("dve", lambda e: e.tensor_copy(out=kTs_f[:, hk, :], in_=pbank[bk][:, 0:NS]),
                                  W=[pbank_t[bk], t_kTs])
            if hf == 1:
                kb.dma("sp", swakT_o.ap().rearrange("h p q -> p h q"), kout[0:64, :, :], R=[t_kout],
                       store=True)
            else:
                kb.dma("sp", swaksT_o.ap().rearrange("h p q -> p h q"), kTs_f[0:64, :, :], R=[t_kTs],
                       store=True)
            if P1_STOP == 1:
                kb.end_scope()
                return
            wi = slab_buf[2]
            for t in range(9):
                M = 128 if (t < 8 or hf == 1) else NS
                bk = proj_ring.next()
                htok = [hTt[t]] if t < 8 else [hTx]
                for kc in range(16):
                    kb.op("pe", lambda e: e.matmul(pbank[bk][0:M, 0:256], lhsT=hT[:, kc, t * 128:t * 128 + M],
                                                   rhs=wbs[wi][:, kc, 0:256], start=(kc == 0), stop=(kc == 15)),
                          R=[t_wb[wi]] + htok, W=[pbank_t[bk]])
                kb.op("act", lambda e: e.activation(out=vtm[0:M, t, :], in_=pbank[bk][0:M, 0:256], func=AF.Copy),
                      R=[pbank_t[bk]], W=[t_v[t]])
                if hf == 1 and t == 7:
                    kb.op("dve", lambda e: e.tensor_copy(out=vout[:, :], in_=pbank[bk][:, 0:256]),
                          W=[pbank_t[bk], t_vout])
                    kb.dma("sp", swav_o.ap()[:, :], vout[:, :], R=[t_vout], store=True)
                if hf == 0 and t == 8:
                    kb.op("dve", lambda e: e.tensor_copy(out=vout[0:NS, :], in_=pbank[bk][0:NS, 0:256]),
                          W=[pbank_t[bk], t_vout])
                    kb.dma("sp", swavs_new_o.ap()[:, :], vout[0:NS, :], R=[t_vout], store=True)
            if hf == 0:
                bk = proj_ring.next()
                for hk in range(4):
                    for x in range(2):
                        for kc in range(16):
                            kb.op("pe", lambda e: e.matmul(pbank[bk][x * 64:(x + 1) * 64, hk * NS:(hk + 1) * NS],
                                                           lhsT=wbs[wi][:, kc, hk * 64:(hk + 1) * 64],
                                                           rhs=hT[:, kc, 1024:1040], start=(kc == 0), stop=(kc == 15)),
                                  R=[t_wb[wi], hTx], W=[pbank_t[bk]])
                kb.op("dve", lambda e: e.tensor_copy(out=vTs_f[:, :, :].rearrange("p a b -> p (a b)"),
                                                     in_=pbank[bk][:, 0:4 * NS]), R=[pbank_t[bk]], W=[t_vTs])

            if P1_STOP == 2:
                kb.end_scope()
                return
            def inproj_j(j):
                s = 3 + j
                ensure_slab(s + 2)
                wi = slab_buf[s]
                b = j % 2
                for g, (c0, c1) in enumerate(qg_groups):
                    bk = proj_group(wbs[wi], t_wb[wi], (0, 128), c0, c1)
                    kb.op("act", lambda e: e.activation(out=qTb[b][:, c0:c1], in_=pbank[bk][:, 0:c1 - c0],
                                                        func=AF.Copy), R=[pbank_t[bk]], W=[t_qT[b][g]])
                    bk2 = proj_group(wbs[wi], t_wb[wi], (128, 256), c0, c1)
                    kb.op("act", lambda e: e.activation(out=sgb[b][:, c0:c1], in_=pbank[bk2][:, 0:c1 - c0],
                                                        func=AF.Silu), R=[pbank_t[bk2]], W=[t_sg[b][g]])
                if hf == 0:
                    kb.op("dve", lambda e: e.tensor_copy(out=qTs[:, j, :], in_=qTb[b][:, 1024:1040]),
                          R=[t_qT[b][2]], W=[t_qTs])
                    kb.op("dve", lambda e: e.tensor_copy(out=sgs[:, j, :], in_=sgb[b][:, 1024:1040]),
                          R=[t_sg[b][2]], W=[t_sgs])

            def finalize(j, b, gi, add_sink):
                c0 = gi * 512
                if add_sink:
                    kb.op("dve", lambda e: e.tensor_scalar(out=rsb[:], in0=pbank[PSM][:, :], scalar1=expsink[:, j:j + 1],
                                                           scalar2=None, op0=ALU.add),
                          R=[pbank_t[PSM], t_small], W=[t_rs])
                    kb.op("dve", lambda e: e.reciprocal(out=rsb[:], in_=rsb[:]), R=[t_rs], W=[t_rs])
                else:
                    kb.op("dve", lambda e: e.reciprocal(out=rsb[:], in_=pbank[PSM][:, :]), R=[pbank_t[PSM]], W=[t_rs])
                kb.op("dve", lambda e: e.tensor_tensor(out=tmpb[:], in0=pbank[PO][:, :], in1=rsb[:], op=ALU.mult),
                      R=[pbank_t[PO], t_rs], W=[t_tmp])
                kb.op("dve", lambda e: e.tensor_tensor(out=brT[:, j, c0:c0 + 512], in0=tmpb[:],
                                                       in1=sgb[b][:, c0:c0 + 512], op=ALU.mult),
                      R=[t_tmp, t_sg[b][gi]], W=[brt[j][gi]])

            def swa_pair(j):
                hk = j // 3
                b = j % 2
                pend = []

                def scores(n):
                    gb = hf * 8 + n
                    cs = [0] if gb == 0 else [0, 1]
                    bk = sc_ring.next()
                    for x in range(2):
                        for c in cs:
                            if c == 0:
                                kc0, ktok = n * 128, t_kT2[hk][n // 4]
                            elif n > 0:
                                kc0, ktok = (n - 1) * 128, t_kT2[hk][(n - 1) // 4]
                            else:
                                kc0, ktok = 1024, t_kT2[hk][2]
                            kb.op("pe", lambda e: e.matmul(
                                pbank[bk][:, (x * 2 + c) * 128:(x * 2 + c + 1) * 128],
                                lhsT=kT2[:, hk, x, kc0:kc0 + 128],
                                rhs=qTb[b][:, n * 128:(n + 1) * 128], start=True, stop=True),
                                R=[ktok, t_qT[b][n // 4]], W=[pbank_t[bk]])
                    ei, pi = E_ring.next(), PT_ring.next()
                    if len(cs) == 2:
                        src = pbank[bk][:, :]
                        eo, po_, tb = Eb[ei][:, :], PTb[pi][:, :], expbT[:, 2 * j:2 * j + 2, :, :].rearrange("p a c q -> p (a c q)")
                    else:
                        v4 = lambda ap: ap.rearrange("p (a c q) -> p a c q", a=2, c=2)[:, :, 0, :]
                        src, eo, po_ = v4(pbank[bk][:, :]), v4(Eb[ei][:, :]), v4(PTb[pi][:, :])
                        tb = expbT[:, 2 * j:2 * j + 2, 0, :]
                    kb.op("act", lambda e: e.activation(out=eo, in_=src, func=AF.Exp, scale=0.125),
                          R=[pbank_t[bk]], W=[t_E[ei]])
                    kb.op("dve", lambda e: e.tensor_tensor(out=po_, in0=eo, in1=tb, op=ALU.mult),
                          R=[t_E[ei], t_expbT], W=[t_PT[pi]])
                    return (n, cs, pi)

                def pv(item):
                    n, cs, pi = item
                    for x in range(2):
                        for ci, c in enumerate(cs):
                            if c == 0:
                                vt = n
                            elif n > 0:
                                vt = n - 1
                            else:
                                vt = 8
                            rhs = PTb[pi][:, (x * 2 + c) * 128:(x * 2 + c + 1) * 128]
                            oc = (n % 4) * 128
                            kb.op("pe", lambda e: e.matmul(pbank[PO][x * 64:(x + 1) * 64, oc:oc + 128],
                                                           lhsT=vtm[:, vt, hk * 64:(hk + 1) * 64], rhs=rhs,
                                                           start=(ci == 0), stop=(ci == len(cs) - 1)),
                                  R=[t_v[vt], t_PT[pi]], W=[pbank_t[PO]])
                            kb.op("pe", lambda e: e.matmul(pbank[PSM][x * 64:(x + 1) * 64, oc:oc + 128],
                                                           lhsT=ones_b[:, 0:64], rhs=rhs,
                                                           start=(ci == 0), stop=(ci == len(cs) - 1)),
                                  R=[t_c2, t_PT[pi]], W=[pbank_t[PSM]])
                    if n % 4 == 3:
                        finalize(j, b, n // 4, True)

                prev = scores(0)
                for n in range(8):
                    nxt = scores(n + 1) if n + 1 < 8 else None
                    pv(prev)
                    prev = nxt

            def mem_head(i, L):
                j = 12 + i
                b = j % 2
                for gi in range(2):
                    pis = []
                    for mc in range(2):
                        bk = sc_ring.next()
                        kb.op("pe", lambda e: e.matmul(pbank[bk][:, :], lhsT=mkT[:, L, i, mc * 128:(mc + 1) * 128],
                                                       rhs=qTb[b][:, gi * 512:(gi + 1) * 512], start=True, stop=True),
                              R=[t_mk, t_qT[b][gi]], W=[pbank_t[bk]])
                        pi = PT_ring.next()
                        kb.op("act", lambda e: e.activation(out=PTb[pi][:, :], in_=pbank[bk][:, :], func=AF.Exp,
                                                            scale=float(128 ** -0.5)), R=[pbank_t[bk]], W=[t_PT[pi]])
                        pis.append(pi)
                    for mc in range(2):
                        pi = pis[mc]
                        kb.op("pe", lambda e: e.matmul(pbank[PO][:, :], lhsT=mvb[:, L, mc, i * 128:(i + 1) * 128],
                                                       rhs=PTb[pi][:, :], start=(mc == 0), stop=(mc == 1)),
                              R=[t_mk, t_PT[pi]], W=[pbank_t[PO]])
                        kb.op("pe", lambda e: e.matmul(pbank[PSM][:, :], lhsT=ones_b[:, :], rhs=PTb[pi][:, :],
                                                       start=(mc == 0), stop=(mc == 1)),
                              R=[t_c2, t_PT[pi]], W=[pbank_t[PSM]])
                    finalize(j, b, gi, False)

            inproj_j(0)
            if P1_STOP == 3:
                kb.end_scope()
                return
            for j in range(16):
                if j + 1 < 16:
                    inproj_j(j + 1)
                if j < 12:
                    swa_pair(j)
                else:
                    mem_head(j - 12, 0)
                if P1_STOP == 4 and j == 0:
                    kb.end_scope()
                    return
                if P1_STOP == 5 and j == 11:
                    kb.end_scope()
                    return
            kb.end_scope()

    def sample_mem(L, st_outer):
        with ExitStack() as st:
            ck = sb(st, "ck", [128, NS, 256], BF16)
            cv = sb(st, "cv", [128, NS, 2, 128], BF16)
            PTs = sb(st, "PTsm", [128, 32], BF16)
            r1 = sb(st, "r1m", [128, NS], F32)
            r2 = sb(st, "r2m", [128, NS], F32)
            t_ck, t_cv, t_PTs, t_r1, t_r2 = Tok("ck"), Tok("cv"), Tok("PTs"), Tok("r1"), Tok("r2")
            for i in range(4):
                kb.dma("pool", ck[:], cmkT_d.ap()[L, i], W=[t_ck])
                kb.dma("pool", cv[:], cmv_d.ap()[L, i], W=[t_cv])
                bk = sc_ring.next()
                for mc in range(2):
                    for bb in range(NS):
                        kb.op("pe", lambda e: e.matmul(pbank[bk][:, mc * NS + bb: mc * NS + bb + 1],
                                                       lhsT=ck[:, bb, mc * 128:(mc + 1) * 128],
                                                       rhs=qTs[:, 12 + i, bb:bb + 1], start=True, stop=True),
                              R=[t_ck, t_qTs], W=[pbank_t[bk]])
                kb.op("act", lambda e: e.activation(out=PTs[:, :], in_=pbank[bk][:, 0:32], func=AF.Exp,
                                                    scale=float(128 ** -0.5)), R=[pbank_t[bk]], W=[t_PTs])
                for bb in range(NS):
                    for mc in range(2):
                        kb.op("pe", lambda e: e.matmul(pbank[PO][:, bb:bb + 1], lhsT=cv[:, bb, mc, :],
                                                       rhs=PTs[:, mc * NS + bb: mc * NS + bb + 1],
                                                       start=(mc == 0), stop=(mc == 1)),
                              R=[t_cv, t_PTs], W=[pbank_t[PO]])
                for mc in range(2):
                    kb.op("pe", lambda e: e.matmul(pbank[PSM][:, 0:NS], lhsT=ones_b[:, :],
                                                   rhs=PTs[:, mc * NS:(mc + 1) * NS], start=(mc == 0), stop=(mc == 1)),
                          R=[t_c2, t_PTs], W=[pbank_t[PSM]])
                kb.op("dve", lambda e: e.reciprocal(out=r1[:], in_=pbank[PSM][:, 0:NS]), R=[pbank_t[PSM]], W=[t_r1])
                kb.op("dve", lambda e: e.tensor_tensor(out=r2[:], in0=pbank[PO][:, 0:NS], in1=r1[:], op=ALU.mult),
                      R=[pbank_t[PO], t_r1], W=[t_r2])
                kb.op("dve", lambda e: e.tensor_tensor(out=brT[:, 12 + i, 1024:1040], in0=r2[:], in1=sgs[:, 12 + i, :],
                                                       op=ALU.mult), R=[t_r2, t_sgs], W=[brt[12 + i][2]])
            kb.end_scope()

    def l0_sample():
        with ExitStack() as st:
            cK = sb(st, "cK", [128, NS, 4, 2, 128], BF16)
            cV = sb(st, "cV", [128, NS, 256], BF16)
            prod = sb(st, "prod", [128, 12, NS], F32)
            pn = sb(st, "pn", [128, 12, NS], F32)
            Es = sb(st, "Es", [128, 32], F32)
            PTs = sb(st, "PTs", [128, 32], BF16)
            a1 = sb(st, "a1", [128, NS], F32)
            a2 = sb(st, "a2", [128, NS], F32)
            t_cK, t_cV, t_prod, t_pn, t_Es, t_PTs, t_a1, t_a2 = (Tok("cK"), Tok("cV"), Tok("prod"), Tok("pn"),
                                                                   Tok("Es"), Tok("PTs"), Tok("a1"), Tok("a2"))
            kb.dma("pool", cK[:], cswaKT_d.ap()[:, :, :, :, :], W=[t_cK])
            kb.dma("pool", cV[:], cswaV_d.ap()[:, :, :], W=[t_cV])
            kb.dma("sp", swaks_sh_o.ap()[:, :, :], cswaKraw_d.ap()[:, 1:128, :], sem_tok=t_out)
            kb.dma("sp", swavs_sh_o.ap()[:, :, :], cswaVraw_d.ap()[:, 1:128, :], sem_tok=t_out)
            for j in range(12):
                kb.op("dve", lambda e: e.tensor_tensor(out=prod[:, j, :], in0=qTs[:, j, :], in1=kTs_f[:, j // 3, :],
                                                       op=ALU.mult), R=[t_qTs, t_kTs], W=[t_prod])
            bk = sc_ring.next()
            kb.op("pe", lambda e: e.matmul(pbank[bk][:, 0:12 * NS], lhsT=blk_f[:, :],
                                           rhs=prod[:, :, :].rearrange("p a b -> p (a b)"), start=True, stop=True),
                  R=[t_prod, t_c4], W=[pbank_t[bk]])
            kb.op("act", lambda e: e.activation(out=pn[:, :, :].rearrange("p a b -> p (a b)"), in_=pbank[bk][:, 0:12 * NS],
                                                func=AF.Exp, scale=0.125), R=[pbank_t[bk]], W=[t_pn])
            for j in range(12):
                kb.op("dve", lambda e: e.tensor_scalar(out=pn[:, j, :], in0=pn[:, j, :], scalar1=expb0[:, j:j + 1],
                                                       scalar2=None, op0=ALU.mult), R=[t_small], W=[t_pn])
            for j in range(12):
                hk = j // 3
                bk = sc_ring.next()
                for x in range(2):
                    for bb in range(NS):
                        kb.op("pe", lambda e: e.matmul(pbank[bk][:, x * NS + bb: x * NS + bb + 1],
                                                       lhsT=cK[:, bb, hk, x, :],
                                                       rhs=qTs[:, j, bb:bb + 1], start=True, stop=True),
                              R=[t_cK, t_qTs], W=[pbank_t[bk]])
                kb.op("act", lambda e: e.activation(out=Es[:, :], in_=pbank[bk][:, 0:32], func=AF.Exp, scale=0.125),
                      R=[pbank_t[bk]], W=[t_Es])
                for x in range(2):
                    kb.op("dve", lambda e: e.tensor_scalar(out=PTs[:, x * NS:(x + 1) * NS], in0=Es[:, x * NS:(x + 1) * NS],
                                                           scalar1=expbcol[:, 2 * j + x: 2 * j + x + 1], scalar2=None,
                                                           op0=ALU.mult), R=[t_Es, t_small], W=[t_PTs])
                for x in range(2):
                    for bb in range(NS):
                        kb.op("pe", lambda e: e.matmul(pbank[PO][x * 64:(x + 1) * 64, bb:bb + 1],
                                                       lhsT=cV[:, bb, hk * 64:(hk + 1) * 64],
                                                       rhs=PTs[:, x * NS + bb: x * NS + bb + 1], start=True, stop=True),
                              R=[t_cV, t_PTs], W=[pbank_t[PO]])
                    kb.op("pe", lambda e: e.matmul(pbank[PSM][x * 64:(x + 1) * 64, 0:NS], lhsT=ones_b[:, 0:64],
                                                   rhs=PTs[:, x * NS:(x + 1) * NS], start=True, stop=True),
                          R=[t_c2, t_PTs], W=[pbank_t[PSM]])
                kb.op("dve", lambda e: e.tensor_tensor(out=a1[:], in0=pn[:, j, :], in1=vTs_f[:, hk, :], op=ALU.mult),
                      R=[t_pn, t_vTs], W=[t_a1])
                kb.op("dve", lambda e: e.tensor_tensor(out=a1[:], in0=a1[:], in1=pbank[PO][:, 0:NS], op=ALU.add),
                      R=[pbank_t[PO]], W=[t_a1])
                kb.op("dve", lambda e: e.scalar_tensor_tensor(out=a2[:], in0=pbank[PSM][:, 0:NS],
                                                              scalar=expsink[:, j:j + 1], in1=pn[:, j, :],
                                                              op0=ALU.add, op1=ALU.add),
                      R=[pbank_t[PSM], t_pn, t_small], W=[t_a2])
                kb.op("dve", lambda e: e.reciprocal(out=a2[:], in_=a2[:]), R=[t_a2], W=[t_a2])
                kb.op("dve", lambda e: e.tensor_tensor(out=a1[:], in0=a1[:], in1=a2[:], op=ALU.mult),
                      R=[t_a2], W=[t_a1])
                kb.op("dve", lambda e: e.tensor_tensor(out=brT[:, j, 1024:1040], in0=a1[:], in1=sgs[:, j, :],
                                                       op=ALU.mult), R=[t_a1, t_sgs], W=[brt[j][2]])
            kb.end_scope()


    def l1_phase1(hf):
        with ExitStack() as st:
            wbs = [sb(st, "wb%d" % i, [128, 16, 256], BF16) for i in range(3)]
            qh = sb(st, "qh", [128, BW], F32)
            fg = sb(st, "fg", [128, BW], F32)
            kh = sb(st, "kh", [128, BW], F32)
            gl = sb(st, "gl", [128, BW], F32)
            gcs = sb(st, "gcs", [128, 1024], F32)
            eg = sb(st, "eg", [128, 1024], F32)
            qt = sb(st, "qt", [128, 1024], BF16)
            kt = sb(st, "kt", [128, 1024], BF16)
            vtm = sb(st, "vtm1", [128, 9, 128], BF16)
            vsf = sb(st, "vsf", [NS, 128], F32)
            sg = sb(st, "sg1", [128, BW], F32)
            ktT = sb(st, "ktT", [128, 2, 8, 128], BF16)
            ATall = sb(st, "ATall", [128, 8, 2, 64], BF16)
            Sall_f = sb(st, "Sall_f", [128, 17, 128], F32)
            Sall_b = sb(st, "Sall_b", [128, 16, 128], BF16)
            t_Sf = [Tok("Sf%d" % i) for i in range(17)]
            t_Sb = [Tok("Sb0"), Tok("Sb1")]
            t_ATt = [Tok("ATt%d" % i) for i in range(8)]
            qTm = Sall_b[:, :, :].rearrange("p a b -> p (a b)")
            oT = sb(st, "oT", [128, 512], F32)
            osq = sb(st, "osq", [128, 512], F32)
            rstd = sb(st, "rstd", [128, 512], F32)
            PTb = [sb(st, "PTm%d" % i, [128, 512], BF16) for i in range(2)]
            rsb = osq
            tmpb = rstd
            t_wb = [Tok("wb%d" % i) for i in range(3)]
            t_qh, t_fg, t_kh, t_gl, t_gcs, t_eg, t_qt, t_kt = (Tok("qh"), Tok("fg"), Tok("kh"), Tok("gl"),
                                                                 Tok("gcs"), Tok("eg"), Tok("qt"), Tok("kt"))
            t_v = [Tok("v1_%d" % i) for i in range(9)]
            t_vsf, t_sg, t_qTm, t_ktT, t_oT, t_osq, t_rstd = (Tok("vsf"), Tok("sg1"), Tok("qTm"), Tok("ktT"), Tok("oT"),
                                                             Tok("osq"), Tok("rstd"))
            t_AT = [Tok("AT0"), Tok("AT1")]
            t_PT = [Tok("PTm0"), Tok("PTm1")]
            t_rs, t_tmp = t_osq, t_rstd
            t_vscr = Tok("vscr")
            PT_ring, AT_ring, tr_ring = Ring([0, 1]), Ring([0, 1]), Ring([0, 1])
            if hf == 0:
                Sin = sb(st, "Sin", [128, NS, 128], F32)
                vbc = sb(st, "vbc", [128, NS, 128], F32)
                t_Sin, t_vbc = Tok("Sin"), Tok("vbc")
            kb.op("dve", lambda e: e.memset(ktT[:, :, :, :].rearrange("p a b c -> p (a b c)"), 0.0), W=[t_ktT])
            if hf == 0:
                for h in range(12):
                    kb.op("dve", lambda e: e.memset(Sst[:, h, :], 0.0), W=[t_S[h]])
            nslab = 28
            wring = Ring([0, 1, 2])
            slab_buf = {}
            nxt_slab = [0]

            def ensure_slab(upto):
                while nxt_slab[0] <= min(upto, nslab - 1):
                    s_ = nxt_slab[0]
                    i = wring.next()
                    kb.dma("pool", wbs[i][:], w1_d.ap()[s_], W=[t_wb[i]])
                    slab_buf[s_] = i
                    nxt_slab[0] += 1

            ensure_slab(1)
            groups = [(0, 512), (512, 1024)] + ([(1024, 1040)] if hf == 0 else [])
            ncols = BW if hf == 0 else 1024
            SC = float(128 ** -0.5)

            qh_b = sb(st, "qh_b", [128, BW], F32)
            fg_b = sb(st, "fg_b", [128, BW], F32)
            sg_b = sb(st, "sg_b", [128, BW], F32)
            vtm_b = sb(st, "vtm_b", [128, 9, 128], BF16)
            QH, FG, SG, VT = [qh, qh_b], [fg, fg_b], [sg, sg_b], [vtm, vtm_b]
            T_QH, T_FG, T_SG = [t_qh, Tok("qh_b")], [t_fg, Tok("fg_b")], [t_sg, Tok("sg_b")]
            T_V = [t_v, [Tok("v1b_%d" % i) for i in range(9)]]

            def proj_q(h, sx):
                ensure_slab(2 * h + 1)
                wi = slab_buf[2 * h]
                for (c0, c1) in groups:
                    bk = proj_group(wbs[wi], t_wb[wi], (0, 128), c0, c1)
                    kb.op("act", lambda e: e.activation(out=QH[sx][:, c0:c1], in_=pbank[bk][:, 0:c1 - c0], func=AF.Silu),
                          R=[pbank_t[bk]], W=[T_QH[sx]])

            def proj_f(h, sx):
                wi = slab_buf[2 * h]
                for (c0, c1) in groups:
                    bk = proj_group(wbs[wi], t_wb[wi], (128, 256), c0, c1)
                    kb.op("act", lambda e: e.activation(out=FG[sx][:, c0:c1], in_=pbank[bk][:, 0:c1 - c0], func=AF.Exp,
                                                        scale=-1.0), R=[pbank_t[bk]], W=[T_FG[sx]])
                    kb.op("dve", lambda e: e.tensor_scalar(out=FG[sx][:, c0:c1], in0=FG[sx][:, c0:c1], scalar1=1.0,
                                                           scalar2=None, op0=ALU.add), R=[], W=[T_FG[sx]])
                    kb.op("dve", lambda e: e.reciprocal(out=FG[sx][:, c0:c1], in_=FG[sx][:, c0:c1]), R=[], W=[T_FG[sx]])

            def proj_g(h, sx):
                ensure_slab(2 * h + 2)
                wi = slab_buf[2 * h + 1]
                for (c0, c1) in groups:
                    bk = proj_group(wbs[wi], t_wb[wi], (128, 256), c0, c1)
                    kb.op("act", lambda e: e.activation(out=SG[sx][:, c0:c1], in_=pbank[bk][:, 0:c1 - c0], func=AF.Silu),
                          R=[pbank_t[bk]], W=[T_SG[sx]])

            def proj_v(h, sx):
                w1i = slab_buf[2 * h + 1]
                ntile = 9 if hf == 0 else 8
                for t in range(ntile):
                    M = 128 if t < 8 else NS
                    bk = proj_ring.next()
                    htok = [hTt[t]] if t < 8 else [hTx]
                    for kc in range(16):
                        kb.op("pe", lambda e: e.matmul(pbank[bk][0:M, 0:128], lhsT=hT[:, kc, t * 128:t * 128 + M],
                                                       rhs=wbs[w1i][:, kc, 0:128], start=(kc == 0), stop=(kc == 15)),
                              R=[t_wb[w1i]] + htok, W=[pbank_t[bk]])
                    kb.op("act", lambda e: e.activation(out=VT[sx][0:M, t, :], in_=pbank[bk][0:M, 0:128], func=AF.Copy),
                          R=[pbank_t[bk]], W=[T_V[sx][t]])
                    if t == 8:
                        kb.op("dve", lambda e: e.tensor_copy(out=vsf[:, :], in_=pbank[bk][0:NS, 0:128]),
                              W=[pbank_t[bk], t_vsf])
                        kb.dma("sp", vscr_d.ap()[:, h * 128:(h + 1) * 128], vsf[:, :], R=[t_vsf], W=[t_vscr], store=True)
                ensure_slab(2 * h + 3)

            def hgrn_head(h, sx, nxt):
                qh, fg, sg, vtm = QH[sx], FG[sx], SG[sx], VT[sx]
                t_qh, t_fg, t_sg, t_v = T_QH[sx], T_FG[sx], T_SG[sx], T_V[sx]
                kb.op("dve", lambda e: e.tensor_scalar(out=fg[:, 0:ncols], in0=fg[:, 0:ncols], scalar1=omlbT[:, h:h + 1],
                                                       scalar2=lbT[:, h:h + 1], op0=ALU.mult, op1=ALU.add),
                      R=[t_small], W=[t_fg])
                kb.op("dve", lambda e: e.tensor_scalar(out=kh[:, 0:ncols], in0=fg[:, 0:ncols], scalar1=-1.0, scalar2=1.0,
                                                       op0=ALU.mult, op1=ALU.add), R=[t_fg], W=[t_kh])
                kb.op("act", lambda e: e.activation(out=gl[:, 0:1024], in_=fg[:, 0:1024], func=AF.Ln), R=[t_fg], W=[t_gl])
                kb.op("dve", lambda e: e.tensor_tensor_scan(out=gcs[:, :], data0=rmask[:, :], data1=gl[:, 0:1024],
                                                            initial=0.0, op0=ALU.mult, op1=ALU.add),
                      R=[t_gl, t_c5], W=[t_gcs])
                kb.op("act", lambda e: e.activation(out=eg[:, :], in_=gcs[:, :], func=AF.Exp), R=[t_gcs], W=[t_eg])
                ek = gl[:, 0:1024]
                kb.op("act", lambda e: e.activation(out=ek, in_=gcs[:, :], func=AF.Exp, scale=-1.0),
                      R=[t_gcs], W=[t_gl])
                kb.op("dve", lambda e: e.tensor_tensor(out=qt[:, :], in0=qh[:, 0:1024], in1=eg[:, :], op=ALU.mult),
                      R=[t_qh, t_eg], W=[t_qt])
                kb.op("dve", lambda e: e.tensor_tensor(out=kt[:, :], in0=kh[:, 0:1024], in1=ek, op=ALU.mult),
                      R=[t_kh, t_gl], W=[t_kt])
                ktd = osq[:, :].bitcast(BF16)
                eg3b = eg[:, :].rearrange("p (c s) -> p c s", s=64)
                kb.op("dve", lambda e: e.tensor_tensor(
                    out=ktd.rearrange("p (c s) -> p c s", s=64), in0=kt[:, :].rearrange("p (c s) -> p c s", s=64),
                    in1=eg3b[:, :, 63:64].to_broadcast([128, 16, 64]), op=ALU.mult), R=[t_kt, t_eg], W=[t_osq])
                if nxt is not None:
                    proj_q(nxt, 1 - sx)
                for t in range(8):
                    pi = tr_ring.next()
                    kb.op("pe", lambda e: e.transpose(out=ptr[pi][:, 0:128], in_=ktd[:, t * 128:(t + 1) * 128],
                                                      identity=ident[:, :]), R=[t_osq, t_c1], W=[ptr_t[pi]])
                    kb.op("act", lambda e: e.activation(out=ktT[0:64, 0, t, :], in_=ptr[pi][0:64, 0:128], func=AF.Copy),
                          R=[ptr_t[pi]], W=[t_ktT])
                    kb.op("act", lambda e: e.activation(out=ktT[64:128, 1, t, :], in_=ptr[pi][64:128, 0:128], func=AF.Copy),
                          R=[ptr_t[pi]], W=[t_ktT])
                eg3 = eg[:, :].rearrange("p (c s) -> p c s", s=64)
                SCB, UB = 2, (3, PSM)
                t_U = [pbank_t[3], pbank_t[PSM]]
                kb.op("dve", lambda e: e.tensor_copy(out=Sall_f[:, 0, :], in_=Sst[:, h, :]), R=[t_S[h]], W=[t_Sf[0]])

                def u_mm(c):
                    t, u = c // 2, c % 2
                    ub = UB[(c // 4) % 2]
                    kb.op("pe", lambda e: e.matmul(pbank[ub][:, (c % 4) * 128:(c % 4 + 1) * 128], lhsT=ktT[:, u, t, :],
                                                   rhs=vtm[:, t, :], start=True, stop=True),
                          R=[t_ktT, t_v[t]], W=[t_U[(c // 4) % 2]])

                def scan_step(c):
                    ub = UB[(c // 4) % 2]
                    ecol = eg3[:, c, 63:64]
                    kb.op("dve", lambda e: e.scalar_tensor_tensor(
                        out=Sall_f[:, c + 1, :], in0=Sall_f[:, c, :], scalar=ecol,
                        in1=pbank[ub][:, (c % 4) * 128:(c % 4 + 1) * 128], op0=ALU.mult, op1=ALU.add),
                        R=[t_Sf[c], t_U[(c // 4) % 2], t_eg], W=[t_Sf[c + 1]])

                def cast_half(hh):
                    kb.op("act", lambda e: e.activation(
                        out=Sall_b[:, hh * 8:(hh + 1) * 8, :].rearrange("p a b -> p (a b)"),
                        in_=Sall_f[:, hh * 8:(hh + 1) * 8, :].rearrange("p a b -> p (a b)"), func=AF.Copy),
                        R=t_Sf[hh * 8:(hh + 1) * 8], W=[t_Sb[hh]])

                def o_mm(c):
                    t, u = c // 2, c % 2
                    oc = (c % 8) * 64
                    kb.op("pe", lambda e: e.matmul(pbank[PO][:, oc:oc + 64], lhsT=Sall_b[:, c, :],
                                                   rhs=qt[:, c * 64:(c + 1) * 64], start=True, stop=False),
                          R=[t_Sb[c // 8], t_qt], W=[pbank_t[PO]])
                    kb.op("pe", lambda e: e.matmul(pbank[PO][:, oc:oc + 64], lhsT=vtm[:, t, :],
                                                   rhs=ATall[:, t, u, :], start=False, stop=True),
                          R=[t_v[t], t_ATt[t]], W=[pbank_t[PO]])

                for c in range(8):
                    u_mm(c)
                for t in range(8):
                    for u in range(2):
                        c = 2 * t + u
                        kb.op("pe", lambda e: e.matmul(pbank[SCB][u * 64:(u + 1) * 64, t * 64:(t + 1) * 64],
                                                       lhsT=kt[:, c * 64:(c + 1) * 64], rhs=qt[:, c * 64:(c + 1) * 64],
                                                       start=True, stop=True), R=[t_kt, t_qt], W=[pbank_t[SCB]])
                for c in range(8):
                    scan_step(c)
                cast_half(0)
                if nxt is not None:
                    proj_f(nxt, 1 - sx)
                for c in range(8, 16):
                    u_mm(c)
                for u in range(2):
                    kb.op("dve", lambda e: e.tensor_tensor(
                        out=ATall[:, :, u, :], in0=pbank[SCB][:, 0:512].rearrange("p (t s) -> p t s", s=64),
                        in1=cmask[:, u, :].unsqueeze(1).to_broadcast([128, 8, 64]), op=ALU.mult),
                        R=[pbank_t[SCB], t_c5], W=t_ATt)
                for c in range(8, 16):
                    scan_step(c)
                cast_half(1)
                kb.op("dve", lambda e: e.tensor_copy(out=Sst[:, h, :], in_=Sall_f[:, 16, :]), R=[t_Sf[16]], W=[t_S[h]])
                if nxt is not None:
                    proj_g(nxt, 1 - sx)
                for c in range(8):
                    o_mm(c)
                rms_finish(h, 0, 512, 0, sx)
                if nxt is not None:
                    proj_v(nxt, 1 - sx)
                for c in range(8, 16):
                    o_mm(c)
                rms_finish(h, 512, 512, 1, sx)

            def rms_finish(h, c0, n, gi, sx=0):
                sg, t_sg = SG[sx], T_SG[sx]
                kb.op("act", lambda e: e.activation(out=osq[:, 0:n], in_=pbank[PO][:, 0:n], func=AF.Square),
                      R=[pbank_t[PO]], W=[t_osq])
                kb.op("dve", lambda e: e.tensor_copy(out=oT[:, 0:n], in_=pbank[PO][:, 0:n]), W=[pbank_t[PO], t_oT])
                bn = proj_ring.next()
                kb.op("pe", lambda e: e.matmul(pbank[bn][:, 0:n], lhsT=ones_f[:, :], rhs=osq[:, 0:n], start=True, stop=True),
                      R=[t_osq, t_c3], W=[pbank_t[bn]])
                kb.op("act", lambda e: e.activation(out=rstd[:, 0:n], in_=pbank[bn][:, 0:n], func=AF.Ln,
                                                    bias=epsrms[:, :], scale=1.0 / 128.0),
                      R=[pbank_t[bn], t_const], W=[t_rstd])
                kb.op("act", lambda e: e.activation(out=rstd[:, 0:n], in_=rstd[:, 0:n], func=AF.Exp, scale=-0.5),
                      R=[], W=[t_rstd])
                kb.op("dve", lambda e: e.tensor_tensor(out=oT[:, 0:n], in0=oT[:, 0:n], in1=rstd[:, 0:n], op=ALU.mult),
                      R=[t_rstd], W=[t_oT])
                kb.op("dve", lambda e: e.scalar_tensor_tensor(out=brT[:, h, c0:c0 + n], in0=oT[:, 0:n],
                                                              scalar=nwT[:, h:h + 1], in1=sg[:, c0:c0 + n],
                                                              op0=ALU.mult, op1=ALU.mult),
                      R=[t_oT, t_sg, t_small], W=[brt[h][gi]])

            def hgrn_sample(h, sx):
                qh, fg, t_qh, t_fg = QH[sx], FG[sx], T_QH[sx], T_FG[sx]
                kb.dma("sp", Sin[:], hst_d.ap()[h], W=[t_Sin])
                kb.dma("sp", vbc[:], bass.AP(vscr_d, h * 128, [[0, 128], [1536, NS], [1, 128]]), R=[t_vscr], W=[t_vbc])
                for bb in range(NS):
                    kb.op("dve", lambda e: e.tensor_scalar(out=Sin[:, bb, :], in0=Sin[:, bb, :],
                                                           scalar1=fg[:, 1024 + bb:1025 + bb], scalar2=None, op0=ALU.mult),
                          R=[t_fg], W=[t_Sin])
                    kb.op("dve", lambda e: e.scalar_tensor_tensor(out=Sin[:, bb, :], in0=vbc[:, bb, :],
                                                                  scalar=kh[:, 1024 + bb:1025 + bb], in1=Sin[:, bb, :],
                                                                  op0=ALU.mult, op1=ALU.add),
                          R=[t_vbc, t_kh], W=[t_Sin])
                for bb in range(NS):
                    kb.op("pe", lambda e: e.matmul(pbank[PO][:, bb:bb + 1], lhsT=Sin[:, bb, :],
                                                   rhs=qh[:, 1024 + bb:1025 + bb], start=True, stop=True),
                          R=[t_Sin, t_qh], W=[pbank_t[PO]])
                kb.dma("sp", hsts_o.ap()[h], Sin[:], R=[t_Sin], store=True)
                rms_finish(h, 1024, NS, 2, sx)

            def mem_head(i, L):
                j = 12 + i
                s_ = 24 + i
                ensure_slab(s_ + 2)
                wi = slab_buf[s_]
                for (c0, c1) in groups:
                    bk = proj_group(wbs[wi], t_wb[wi], (0, 128), c0, c1)
                    kb.op("act", lambda e: e.activation(out=qTm[:, c0:c1], in_=pbank[bk][:, 0:c1 - c0], func=AF.Copy),
                          R=[pbank_t[bk]], W=[t_qTm, t_Sb[0], t_Sb[1]])
                    bk = proj_group(wbs[wi], t_wb[wi], (128, 256), c0, c1)
                    kb.op("act", lambda e: e.activation(out=sg[:, c0:c1], in_=pbank[bk][:, 0:c1 - c0], func=AF.Silu),
                          R=[pbank_t[bk]], W=[t_sg])
                if hf == 0:
                    kb.op("dve", lambda e: e.tensor_copy(out=qTs[:, j, :], in_=qTm[:, 1024:1040]), R=[t_qTm], W=[t_qTs])
                    kb.op("dve", lambda e: e.tensor_copy(out=sgs[:, j, :], in_=sg[:, 1024:1040]), R=[t_sg], W=[t_sgs])
                for gi in range(2):
                    pis = []
                    for mc in range(2):
                        bk = sc_ring.next()
                        kb.op("pe", lambda e: e.matmul(pbank[bk][:, :], lhsT=mkT[:, L, i, mc * 128:(mc + 1) * 128],
                                                       rhs=qTm[:, gi * 512:(gi + 1) * 512], start=True, stop=True),
                              R=[t_mk, t_qTm], W=[pbank_t[bk]])
                        pi = PT_ring.next()
                        kb.op("act", lambda e: e.activation(out=PTb[pi][:, :], in_=pbank[bk][:, :], func=AF.Exp, scale=SC),
                              R=[pbank_t[bk]], W=[t_PT[pi]])
                        pis.append(pi)
                    for mc in range(2):
                        pi = pis[mc]
                        kb.op("pe", lambda e: e.matmul(pbank[PO][:, :], lhsT=mvb[:, L, mc, i * 128:(i + 1) * 128],
                                                       rhs=PTb[pi][:, :], start=(mc == 0), stop=(mc == 1)),
                              R=[t_mk, t_PT[pi]], W=[pbank_t[PO]])
                        kb.op("pe", lambda e: e.matmul(pbank[PSM][:, :], lhsT=ones_b[:, :], rhs=PTb[pi][:, :],
                                                       start=(mc == 0), stop=(mc == 1)),
                              R=[t_c2, t_PT[pi]], W=[pbank_t[PSM]])
                    c0 = gi * 512
                    kb.op("dve", lambda e: e.reciprocal(out=rsb[:], in_=pbank[PSM][:, :]), R=[pbank_t[PSM]], W=[t_rs])
                    kb.op("dve", lambda e: e.tensor_tensor(out=tmpb[:], in0=pbank[PO][:, :], in1=rsb[:], op=ALU.mult),
                          R=[pbank_t[PO], t_rs], W=[t_tmp])
                    kb.op("dve", lambda e: e.tensor_tensor(out=brT[:, j, c0:c0 + 512], in0=tmpb[:],
                                                           in1=sg[:, c0:c0 + 512], op=ALU.mult),
                          R=[t_tmp, t_sg], W=[brt[j][gi]])

            ensure_slab(2)
            proj_q(0, 0)
            proj_f(0, 0)
            proj_g(0, 0)
            proj_v(0, 0)
            for h in range(12):
                hgrn_head(h, h % 2, h + 1 if h + 1 < 12 else None)
                if hf == 0:
                    hgrn_sample(h, h % 2)
            for i in range(4):
                mem_head(i, 1)
            if hf == 1:
                kb.dma("sp", hstp_o.ap().rearrange("h p v -> p h v"), Sst[:, :, :], R=t_S, store=True, sem_tok=t_S[0])
            kb.end_scope()

    for hf in range(2):
        if STOP == "setup":
            break
        l0_phase1(hf)
        if STOP == "p1a":
            break
        if hf == 0:
            l0_sample()
            sample_mem(0, None)
        if STOP == "sample":
            break
        phase2(0, hf)
        if STOP == "p2a":
            break
        if DEBUG_L0_ONLY:
            continue
        l1_phase1(hf)
        if hf == 0:
            sample_mem(1, None)
        phase2(1, hf)
    kb.barrier()
    es.close()
    return nc, kb


def arr_kp(w):
    w = np.asarray(w, dtype=np.float32)
    return np.ascontiguousarray(w.reshape(16, 128, -1).transpose(1, 0, 2))


def host_consts():
    c = {}
    c["c_ident"] = np.eye(128, dtype=np.float32)
    c["c_ones"] = np.ones((128, 128), np.float32)
    blk = np.zeros((128, 128), np.float32)
    blk[:64, :64] = 1
    blk[64:, 64:] = 1
    c["c_blk"] = blk
    oh = np.zeros((32, 384), np.float32)
    vm = np.zeros((24, 384), np.float32)
    bk = t5_bucket_np(np.arange(128))
    for d in range(128):
        oh[bk[d], 127 + d] = 1.0
        vm[:, 127 + d] = 1.0
    c["c_oh"] = oh
    c["c_vm"] = vm
    cm = np.zeros((128, 2, 64), np.float32)
    for r in range(128):
        cm[r, r // 64, (r % 64):] = 1.0
    c["c_cm"] = cm
    rm = np.ones((128, 1024), np.float32)
    rm[:, ::64] = 0.0
    c["c_rm"] = rm
    return c


_CACHE = {}


def kernel(x_prompt, x_sample, cache_mem_k, cache_mem_v, cache_swa_k, cache_swa_v, state_hgrn, mem_prompt,
           rel_bias, swa_w_in, swa_sinks, hg_w_in, hg_lb_logits, hg_norm_w, w_mem_k, w_mem_v, w_out, ln_w, ln_b):
    f = lambda a: np.asarray(a, dtype=np.float32)
    x_prompt, x_sample = f(x_prompt), f(x_sample)
    if "nc" not in _CACHE:
        _CACHE["nc"] = build_program()
    nc, kb = _CACHE["nc"]
    consts = host_consts()
    W = f(swa_w_in)[0]
    q, k, v, mq, g = W[:, :1536], W[:, 1536:1792], W[:, 1792:2048], W[:, 2048:2560], W[:, 2560:]
    slabs = []
    for a in (0, 2):
        slabs.append(np.concatenate([k[:, a * 64:(a + 1) * 64]] * 2 + [k[:, (a + 1) * 64:(a + 2) * 64]] * 2, axis=1))
    slabs.append(v)
    for j in range(16):
        A = q[:, j * 128:(j + 1) * 128] if j < 12 else mq[:, (j - 12) * 128:(j - 11) * 128]
        slabs.append(np.concatenate([A, g[:, j * 128:(j + 1) * 128]], axis=1))
    w0 = np.stack([arr_kp(s) for s in slabs])
    W = f(hg_w_in)[0]
    q, fq, iv, mq, g = W[:, :1536], W[:, 1536:3072], W[:, 3072:4608], W[:, 4608:5120], W[:, 5120:]
    slabs = []
    for h in range(12):
        sl = slice(h * 128, (h + 1) * 128)
        slabs.append(np.concatenate([q[:, sl], fq[:, sl]], axis=1))
        slabs.append(np.concatenate([iv[:, sl], g[:, sl]], axis=1))
    for i in range(4):
        slabs.append(np.concatenate([mq[:, i * 128:(i + 1) * 128], g[:, (12 + i) * 128:(13 + i) * 128]], axis=1))
    w1 = np.stack([arr_kp(s) for s in slabs])
    wk = np.stack([arr_kp(f(w_mem_k)[i]) for i in range(2)])
    wv = np.stack([arr_kp(f(w_mem_v)[i]) for i in range(2)])
    wo = np.stack([arr_kp(f(w_out)[i]) for i in range(2)])
    relb = f(rel_bias)
    hd_half = np.arange(128) // 64
    relb0 = np.stack([relb[0, 2 * j + hd_half] for j in range(12)], axis=1)
    sinksc = np.stack([f(swa_sinks)[0, 2 * j + hd_half] for j in range(12)], axis=1)
    lbl = np.ascontiguousarray(f(hg_lb_logits).reshape(2, 12, 128).transpose(2, 0, 1))
    hgnw = np.ascontiguousarray(f(hg_norm_w)[0].reshape(12, 128).T)
    shared = dict(wk=wk, wv=wv, w0=w0, w1=w1, wo=wo, lnw=f(ln_w), lnb=f(ln_b), relb=relb,
                  relb0=np.ascontiguousarray(relb0), sinksc=np.ascontiguousarray(sinksc), lbl=lbl, hgnw=hgnw)
    shared.update(consts)
    cmk, cmv_ = f(cache_mem_k), f(cache_mem_v)
    csk, csv = f(cache_swa_k)[0], f(cache_swa_v)[0]
    sth = f(state_hgrn)[0]
    mem_prompt = f(mem_prompt)
    in_maps = []
    for c in range(NCORES):
        sl = slice(c * NS, (c + 1) * NS)
        m = dict(shared)
        m["xT"] = arr_kp(x_prompt[c].T)
        m["xsT"] = arr_kp(x_sample[sl, 0, :].T)
        m["xres"] = np.ascontiguousarray(x_prompt[c])
        m["xsres"] = np.ascontiguousarray(x_sample[sl, 0, :])
        m["memT"] = arr_kp(mem_prompt[c].T)
        kk = csk[sl]
        kT = kk.transpose(3, 0, 2, 1)
        kTp = np.zeros((128, NS, 4, 2, 128), np.float32)
        kTp[0:64, :, :, 0, :] = kT
        kTp[64:128, :, :, 1, :] = kT
        m["cswaKT"] = kTp
        m["cswaV"] = np.ascontiguousarray(csv[sl].reshape(NS, 128, 256).transpose(1, 0, 2))
        m["cswaKraw"] = np.ascontiguousarray(kk.reshape(NS, 128, 256))
        m["cswaVraw"] = np.ascontiguousarray(csv[sl].reshape(NS, 128, 256))
        m["cmkT"] = np.ascontiguousarray(cmk[:, sl].transpose(0, 3, 4, 1, 2))
        m["cmv"] = np.ascontiguousarray(cmv_[:, sl].reshape(2, NS, 2, 128, 4, 128).transpose(0, 4, 3, 1, 2, 5))
        m["hst"] = np.ascontiguousarray(sth[sl].transpose(1, 2, 0, 3))
        in_maps.append(m)
    if STOP == 'setup':
        for m in in_maps:
            for kname in BIG:
                m[kname] = np.zeros((1, 1), np.float32)
    if DEBUG_NCORES < NCORES:
        res = run_bass_kernel_spmd(nc, in_maps[:DEBUG_NCORES], core_ids=list(range(DEBUG_NCORES)))
        R = list(res.results) + [res.results[0]] * (NCORES - DEBUG_NCORES)
    else:
        res = run_bass_kernel_spmd(nc, in_maps, core_ids=list(range(NCORES)))
        R = res.results
    y_prompt = np.stack([R[c]["y"] for c in range(NCORES)])
    y_sample = np.concatenate([R[c]["ys"] for c in range(NCORES)])[:, None, :]
    mem_k = np.stack([R[c]["mkT_o"].transpose(0, 3, 1, 2) for c in range(NCORES)], axis=1)
    mem_v = np.stack([R[c]["mv_o"].reshape(2, 256, 4, 128) for c in range(NCORES)], axis=1)
    swa_kp = np.stack([R[c]["swakT_o"].transpose(2, 0, 1) for c in range(NCORES)])[None]
    swa_vp = np.stack([R[c]["swav_o"].reshape(128, 4, 64) for c in range(NCORES)])[None]
    hstp = np.stack([R[c]["hstp_o"] for c in range(NCORES)])[None]
    ks = []
    vs = []
    hs = []
    for c in range(NCORES):
        knew = R[c]["swaksT_o"].transpose(2, 0, 1).reshape(NS, 1, 256)
        ks.append(np.concatenate([R[c]["swaks_sh_o"], knew], axis=1).reshape(NS, 128, 4, 64))
        vnew = R[c]["swavs_new_o"].reshape(NS, 1, 256)
        vs.append(np.concatenate([R[c]["swavs_sh_o"], vnew], axis=1).reshape(NS, 128, 4, 64))
        hs.append(R[c]["hsts_o"].transpose(2, 0, 1, 3))
    swa_ks = np.concatenate(ks)[None]
    swa_vs = np.concatenate(vs)[None]
    hsts = np.concatenate(hs)[None]
    outs = (y_prompt, y_sample, mem_k, mem_v, swa_kp, swa_vp, hstp, swa_ks, swa_vs, hsts)
    return tuple(np.ascontiguousarray(o, dtype=np.float32) for o in outs)
```
